# Optimizing a Trainium2 kernel written in Bass

```python
import jax, jax.numpy as jnp
from jax import lax
import numpy as np

D_MODEL = 1024
BATCH = 16
SEQ = 2048
DEPTH = 1
DEC_BATCH = 32
DEC_SEQ = 64
PAST_LEN = 1024

CHUNK = 64
N_META = 16
HEAD_DIM = 64
D_ATT = D_MODEL // 2
D_CONV = D_MODEL - D_ATT
N_HEADS = D_ATT // HEAD_DIM
N_KV = 2
GROUP = N_HEADS // N_KV
KV_DIM = N_KV * HEAD_DIM
WINDOW = 128
BAND_CHUNKS = -(-WINDOW // CHUNK) + 1
CONV_W = 31
RMS_EPS = 1e-6
LN_EPS = 1e-5
ATTN_SCALE = HEAD_DIM ** -0.5
SPLIT_IDX = [D_ATT, D_ATT + KV_DIM, D_ATT + 2 * KV_DIM, 2 * D_ATT + 2 * KV_DIM,
             2 * D_ATT + 2 * KV_DIM + 2 * D_CONV]
D_IN = 2 * D_ATT + 2 * KV_DIM + 3 * D_CONV

kernel_name = "hymba_swa_sink_conformer_conv_stream_step"


def _rms(x, g):
    xf = x.astype(jnp.float32)
    y = xf * lax.rsqrt(jnp.mean(xf * xf, axis=-1, keepdims=True) + RMS_EPS)
    return (y * g.astype(jnp.float32)).astype(x.dtype)


def _layernorm(x, g, b):
    xf = x.astype(jnp.float32)
    mu = jnp.mean(xf, axis=-1, keepdims=True)
    xc = xf - mu
    y = xc * lax.rsqrt(jnp.mean(xc * xc, axis=-1, keepdims=True) + LN_EPS)
    return (y * g.astype(jnp.float32) + b.astype(jnp.float32)).astype(x.dtype)


def _alibi_slopes():
    h = jnp.arange(1, N_HEADS + 1, dtype=jnp.float32)
    return (2.0 ** (-8.0 * h / N_HEADS)).reshape(N_KV, GROUP)


def _split(z):
    lead = z.shape[:-1]
    q, k, v, ga, glu, gb = jnp.split(z, SPLIT_IDX, axis=-1)
    q = q.reshape(lead + (N_KV, GROUP, HEAD_DIM))
    k = k.reshape(lead + (N_KV, HEAD_DIM))
    v = v.reshape(lead + (N_KV, HEAD_DIM))
    a, b = jnp.split(glu, 2, axis=-1)
    u = a * jax.nn.sigmoid(b)
    return q, k, v, ga, u, gb


def _sink_attention(q, k, v, bias, sinks):
    s = jnp.einsum('bnqkgd,bnjkd->bnkgqj', q, k).astype(jnp.float32) * ATTN_SCALE + bias
    sink = sinks.astype(jnp.float32).reshape(N_KV, GROUP, 1, 1)
    m = jnp.maximum(jnp.max(s, axis=-1, keepdims=True), sink)
    e = jnp.exp(s - m)
    p = e / (jnp.sum(e, axis=-1, keepdims=True) + jnp.exp(sink - m))
    return jnp.einsum('bnkgqj,bnjkd->bnqkgd', p.astype(v.dtype), v)


def _prompt_attention(q, k, v, mk, mv, sinks):
    b, s = q.shape[0], q.shape[1]
    nc = s // CHUNK
    qc = q.reshape(b, nc, CHUNK, N_KV, GROUP, HEAD_DIM)

    def band(t, mt):
        tc = t.reshape(b, nc, CHUNK, N_KV, HEAD_DIM)
        tp = jnp.concatenate(
            [jnp.zeros((b, BAND_CHUNKS - 1, CHUNK, N_KV, HEAD_DIM), t.dtype), tc], axis=1)
        rows = [tp[:, j:j + nc] for j in range(BAND_CHUNKS)]
        meta = jnp.broadcast_to(mt, (b, nc, N_META, N_KV, HEAD_DIM))
        return jnp.concatenate([meta] + rows, axis=2)

    kb = band(k, mk)
    vb = band(v, mv)
    span = BAND_CHUNKS * CHUNK
    qi = jnp.arange(CHUNK)
    kj = jnp.arange(span)
    dist = jnp.abs(qi[:, None] + (BAND_CHUNKS - 1) * CHUNK - kj[None, :]).astype(jnp.float32)
    alibi = -_alibi_slopes()[:, :, None, None] * dist
    key_chunk = jnp.arange(nc)[:, None] - (BAND_CHUNKS - 1) + kj[None, :] // CHUNK
    band_bias = jnp.where((key_chunk >= 0)[:, None, None, None, :], alibi, -jnp.inf)
    bias = jnp.concatenate(
        [jnp.zeros((nc, N_KV, GROUP, CHUNK, N_META), jnp.float32), band_bias], axis=-1)
    out = _sink_attention(qc, kb, vb, bias[None], sinks)
    return out.reshape(b, s, D_ATT)


def _sample_attention(q, k, v, mk, mv, ck, cv, sinks):
    b, t = q.shape[0], q.shape[1]
    rows = ck.shape[1]

    def keys(new, cache, mt):
        meta = jnp.broadcast_to(mt, (b, N_META, N_KV, HEAD_DIM))
        return jnp.concatenate([meta, cache, new], axis=1)[:, None]

    qpos = jnp.arange(t)
    kpos = jnp.arange(rows + t) - rows
    dist = jnp.abs(qpos[:, None] - kpos[None, :]).astype(jnp.float32)
    alibi = -_alibi_slopes()[:, :, None, None] * dist
    bias = jnp.concatenate(
        [jnp.zeros((N_KV, GROUP, t, N_META), jnp.float32), alibi], axis=-1)
    out = _sink_attention(q[:, None], keys(k, ck, mk), keys(v, cv, mv), bias[None, None], sinks)
    return out.reshape(b, t, D_ATT)


def _dwconv(x, w):
    return lax.conv_general_dilated(
        x, w[:, None, :].astype(x.dtype), window_strides=(1,), padding='VALID',
        dimension_numbers=('NWC', 'WIO', 'NWC'), feature_group_count=x.shape[-1])


def _conv_branch(u, left, conv_w, ln_g, ln_b, w_pw):
    h = _dwconv(jnp.concatenate([left, u], axis=1), conv_w)
    h = jax.nn.silu(_layernorm(h, ln_g, ln_b))
    return h @ w_pw


def _merge(att, ga, conv, gb, g_att, g_conv, w_out, g_post):
    y = jnp.concatenate([_rms(att, g_att) * jax.nn.silu(ga),
                         _rms(conv, g_conv) * jax.nn.silu(gb)], axis=-1) @ w_out
    return _rms(y, g_post)


def setup_inputs(seed: int = 0) -> dict:
    key = jax.random.key(seed)
    ks = jax.random.split(key, 18)
    rows = min(WINDOW, PAST_LEN)
    f32 = jnp.float32
    nrm = lambda k, shp: jax.random.normal(k, shp, f32)
    return {
        "x_prompt": nrm(ks[0], (BATCH, SEQ, D_MODEL)),
        "x_sample": nrm(ks[1], (DEC_BATCH, DEC_SEQ, D_MODEL)),
        "cache_k": nrm(ks[2], (DEPTH, DEC_BATCH, rows, N_KV, HEAD_DIM)),
        "cache_v": nrm(ks[3], (DEPTH, DEC_BATCH, rows, N_KV, HEAD_DIM)),
        "state_conv": 0.5 * nrm(ks[4], (DEPTH, DEC_BATCH, CONV_W - 1, D_CONV)),
        "meta_tokens": nrm(ks[5], (N_META, D_MODEL)),
        "g_pre": 1.0 + 0.05 * nrm(ks[6], (DEPTH, D_MODEL)),
        "w_in": nrm(ks[7], (DEPTH, D_MODEL, D_IN)) * D_MODEL ** -0.5,
        "sinks": 0.5 * nrm(ks[8], (DEPTH, N_HEADS)),
        "g_att": 1.0 + 0.05 * nrm(ks[9], (DEPTH, D_ATT)),
        "conv_w": nrm(ks[10], (DEPTH, CONV_W, D_CONV)) * CONV_W ** -0.5,
        "ln_g": 1.0 + 0.05 * nrm(ks[11], (DEPTH, D_CONV)),
        "ln_b": 0.02 * nrm(ks[12], (DEPTH, D_CONV)),
        "w_pw": nrm(ks[13], (DEPTH, D_CONV, D_CONV)) * D_CONV ** -0.5,
        "g_conv": 1.0 + 0.05 * nrm(ks[14], (DEPTH, D_CONV)),
        "w_out": nrm(ks[15], (DEPTH, D_ATT + D_CONV, D_MODEL)) * (D_ATT + D_CONV) ** -0.5,
        "g_post": 1.0 + 0.05 * nrm(ks[16], (DEPTH, D_MODEL)),
    }


def reference(x_prompt, x_sample, cache_k, cache_v, state_conv, meta_tokens, g_pre, w_in,
              sinks, g_att, conv_w, ln_g, ln_b, w_pw, g_conv, w_out, g_post):
    h_p = x_prompt
    h_s = x_sample
    mh = meta_tokens
    b_p = x_prompt.shape[0]
    nk_p, nv_p, nc_p, nk_s, nv_s, nc_s = [], [], [], [], [], []
    for l in range(DEPTH):
        mq, mk, mv, mga, mu, mgb = _split(_rms(mh, g_pre[l]) @ w_in[l])
        meta_left = jnp.concatenate(
            [jnp.zeros((max(CONV_W - 1 - N_META, 0), D_CONV), mu.dtype), mu[-(CONV_W - 1):]], axis=0)

        q, k, v, ga, u, gb = _split(_rms(h_p, g_pre[l]) @ w_in[l])
        att = _prompt_attention(q, k, v, mk, mv, sinks[l])
        conv = _conv_branch(u, jnp.broadcast_to(meta_left, (b_p, CONV_W - 1, D_CONV)),
                            conv_w[l], ln_g[l], ln_b[l], w_pw[l])
        h_p = h_p + _merge(att, ga, conv, gb, g_att[l], g_conv[l], w_out[l], g_post[l])
        nk_p.append(k[:, -WINDOW:])
        nv_p.append(v[:, -WINDOW:])
        nc_p.append(u[:, -(CONV_W - 1):])

        rows = cache_k.shape[2]
        qs, ks_, vs, gas, us, gbs = _split(_rms(h_s, g_pre[l]) @ w_in[l])
        att_s = _sample_attention(qs, ks_, vs, mk, mv, cache_k[l], cache_v[l], sinks[l])
        conv_s = _conv_branch(us, state_conv[l], conv_w[l], ln_g[l], ln_b[l], w_pw[l])
        h_s = h_s + _merge(att_s, gas, conv_s, gbs, g_att[l], g_conv[l], w_out[l], g_post[l])
        nk_s.append(jnp.concatenate([cache_k[l], ks_], axis=1)[:, -rows:])
        nv_s.append(jnp.concatenate([cache_v[l], vs], axis=1)[:, -rows:])
        nc_s.append(jnp.concatenate([state_conv[l], us], axis=1)[:, -(CONV_W - 1):])

        if l + 1 < DEPTH:
            m_att = _sink_attention(mq[None, None], mk[None, None], mv[None, None],
                                    jnp.zeros((1, 1, N_KV, GROUP, N_META, N_META), jnp.float32),
                                    sinks[l])[0, 0].reshape(N_META, D_ATT)
            m_conv = _conv_branch(mu[None], jnp.zeros((1, CONV_W - 1, D_CONV), mu.dtype),
                                  conv_w[l], ln_g[l], ln_b[l], w_pw[l])[0]
            mh = mh + _merge(m_att, mga, m_conv, mgb, g_att[l], g_conv[l], w_out[l], g_post[l])

    new_k_prompt = jnp.stack(nk_p, axis=0)
    new_v_prompt = jnp.stack(nv_p, axis=0)
    new_conv_prompt = jnp.stack(nc_p, axis=0)
    new_k_sample = jnp.stack(nk_s, axis=0)
    new_v_sample = jnp.stack(nv_s, axis=0)
    new_conv_sample = jnp.stack(nc_s, axis=0)
    return (h_p, h_s, new_k_prompt, new_v_prompt, new_conv_prompt,
            new_k_sample, new_v_sample, new_conv_sample)
```

```python
import contextlib
import numpy as np
import concourse.bass as bass
import concourse.mybir as mybir
from concourse.bass_utils import run_bass_kernel_spmd

F32 = mybir.dt.float32
BF16 = mybir.dt.bfloat16
AF = mybir.ActivationFunctionType
ALU = mybir.AluOpType

NCORES = 8
D = 1024
SEQ = 2048
NPB = 2
NSB = 4
DEC = 64
D_IN = 2816
CW = 31
HALO = CW - 1
NMETA = 16
RMS_EPS = 1e-6
LN_EPS = 1e-5
SCALE = 0.125
ST = 512
OQ, OK_, OV, OGA, OA, OB, OGB = 0, 512, 640, 768, 1280, 1792, 2304


class Sched:
    ENGS = ("pe", "act", "dve", "pool", "sp")
    EXCL = frozenset(["pz0", "pz1", "ptr", "pst", "pS0", "pS1", "pS2", "pO"])

    def __init__(self, nc):
        self.nc = nc
        self.streams = {e: [] for e in self.ENGS}
        self.count = {e: 0 for e in self.ENGS}
        self.waited = {e: {} for e in self.ENGS}
        self.dma_count = {}
        self.dma_rr = 0
        self.last_w = {}
        self.readers = {}
        self.sem_names = set(self.ENGS)

    def _deps(self, reads, writes):
        deps = set()
        for r in reads:
            t = self.last_w.get(r)
            if t is not None:
                deps.add(t)
            if r in self.EXCL:
                for t in self.readers.get(r, ()):
                    deps.add(t)
        for w in writes:
            t = self.last_w.get(w)
            if t is not None:
                deps.add(t)
            for t in self.readers.get(w, ()):
                deps.add(t)
        return deps

    def _commit(self, tok, reads, writes):
        for r in reads:
            self.readers.setdefault(r, []).append(tok)
        for w in writes:
            self.last_w[w] = tok
            self.readers[w] = []

    def _emit_waits(self, eng, deps):
        need = {}
        for (s, v) in deps:
            if s == "pe" and eng == "pe":
                continue
            if v > need.get(s, 0):
                need[s] = v
        for s, v in sorted(need.items()):
            if self.waited[eng].get(s, 0) >= v:
                continue
            self.waited[eng][s] = v
            self.streams[eng].append(("wait", s, v))

    def op(self, eng, fn, reads=(), writes=()):
        deps = self._deps(reads, writes)
        self._emit_waits(eng, deps)
        self.count[eng] += 1
        tok = (eng, self.count[eng])
        self.streams[eng].append(("op", fn, eng, 1))
        self._commit(tok, reads, writes)
        return tok

    NDMA = 24

    def dma(self, fn, sem, reads=(), writes=(), n=1, eng="sp"):
        assert n == 1
        sem = "d%d" % (self.dma_rr % self.NDMA)
        self.dma_rr += 1
        self.sem_names.add(sem)
        deps = self._deps(reads, writes)
        prev = self.dma_count.get(sem, 0)
        if prev:
            deps.add((sem, prev))
        self._emit_waits(eng, deps)
        self.dma_count[sem] = prev + 16
        tok = (sem, self.dma_count[sem])
        self.streams[eng].append(("dma", fn, sem, 1))
        self._commit(tok, reads, writes)
        return tok

    def wait_all(self, eng, toks):
        self._emit_waits(eng, toks)

    def emit(self):
        nc = self.nc
        with contextlib.ExitStack() as es:
            sems = {}
            for s in sorted(self.sem_names):
                sems[s] = es.enter_context(nc.semaphore("s_" + s))
            block = es.enter_context(nc.Block())

            def run(engname, e):
                for item in self.streams[engname]:
                    if item[0] == "wait":
                        e.wait_ge(sems[item[1]], item[2])
                    elif item[0] == "op":
                        ins = item[1](e)
                        ins.then_inc(sems[item[2]], 1)
                    else:
                        lst = item[1](e)
                        if not isinstance(lst, (list, tuple)):
                            lst = [lst]
                        assert len(lst) == item[3]
                        for ins in lst:
                            ins.then_inc(sems[item[2]], 16)

            @block.tensor
            def _(e):
                run("pe", e)

            @block.scalar
            def _(e):
                run("act", e)

            @block.vector
            def _(e):
                run("dve", e)

            @block.gpsimd
            def _(e):
                run("pool", e)

            @block.sync
            def _(e):
                run("sp", e)


def build_program():
    nc = bass.Bass("TRN2", target_bir_lowering=False)
    di = lambda name, shape: nc.dram_tensor(name, shape, F32, kind="ExternalInput").ap()
    do = lambda name, shape: nc.dram_tensor(name, shape, F32, kind="ExternalOutput").ap()
    x_p = di("x_prompt", [NPB, SEQ, D])
    x_s = di("x_sample", [NSB * DEC, D])
    cache_k = di("cache_k", [NSB, 128, 128])
    cache_v = di("cache_v", [NSB, 128, 128])
    state_conv = di("state_conv", [NSB, HALO, 512])
    meta = di("meta_tokens", [NMETA, D])
    g_pre = di("g_pre", [8, 128])
    w_in = di("w_in", [D, D_IN])
    sinks = di("sinks", [1, 8])
    g_att = di("g_att", [1, 512])
    vecs = di("vecs", [34, 512])
    w_pw = di("w_pw", [512, 512])
    w_out = di("w_out", [D, D])
    g_post = di("g_post", [1, D])
    c_ident = di("c_ident", [128, 128])
    c_bprev = di("c_bprev", [128, 1024])
    c_bown = di("c_bown", [128, 1024])

    y_p = do("y_prompt", [NPB, SEQ, D])
    y_s = do("y_sample", [NSB * DEC, D])
    nk_p = do("nk_p", [NPB, 128, 128])
    nv_p = do("nv_p", [NPB, 128, 128])
    nc_p = do("nc_p", [NPB, HALO, 512])
    nk_s = do("nk_s", [NSB, 128, 128])
    nv_s = do("nv_s", [NSB, 128, 128])
    nc_s = do("nc_s", [NSB, HALO, 512])

    S = Sched(nc)
    es = contextlib.ExitStack()
    with es:
        sb = lambda name, shape, dt: es.enter_context(nc.sbuf_tensor(name, shape, dt))
        ps = lambda name, shape, dt: es.enter_context(nc.psum_tensor(name, shape, dt))

        NXS = 3
        NPE = 8
        Win = sb("Win", [128, 8, D_IN], BF16)
        WoA = sb("WoA", [128, 4, D], BF16)
        WoC = sb("WoC", [128, 4, D], BF16)
        Wpw = sb("Wpw", [128, 4, 512], BF16)
        XS = sb("XS", [128, NXS, D], F32)
        XR = sb("XR", [128, 2, D], F32)
        Xs = sb("Xs", [128, D], BF16)
        XT = sb("XT", [128, 8, ST], BF16)
        QT = sb("QT", [128, 4, ST], BF16)
        KT2 = [sb("KT%d" % i, [128, 128 + ST], BF16) for i in range(2)]
        VA = sb("VA", [128, 5, 2, 128], BF16)
        SGA = sb("SGA", [128, 4, ST], BF16)
        SGB = [sb("SGB%d" % i, [128, 4, ST], BF16) for i in range(2)]
        U = sb("U", [128, 4, HALO + ST], BF16)
        UH = sb("UH", [128, 4, HALO], BF16)
        UL = sb("UL", [128, 4, NSB * HALO], F32)
        Dg = sb("Dg", [128, 4 * NPE, 128], BF16)
        TBa = sb("TBa", [128, ST], F32)
        TBb = sb("TBb", [128, ST], F32)
        H2 = [sb("H_%d" % i, [128, 4, ST], F32) for i in range(2)]
        HB = sb("HB", [128, 4, ST], BF16)
        R0 = sb("R0", [128, ST], F32)
        R1 = sb("R1", [128, ST], F32)
        CM = sb("CM", [128, 4, ST], BF16)
        SQ = CM
        P01 = [[sb("P%d_%d" % (i, j), [128, 512], BF16) for j in range(2)] for i in range(2)]
        P2 = [sb("P%d_2" % i, [64, 512], BF16) for i in range(2)]
        RD = sb("RD", [128, 512], F32)
        ATT = sb("ATT", [128, 4, ST], F32)
        AM = [sb("AM%d" % i, [128, 4, ST], BF16) for i in range(3)]
        YS = sb("YS0", [128, D], F32)
        Gpost = sb("Gpost", [128, D], F32)
        Bprev = sb("Bprev", [128, 1024], BF16)
        Bown = sb("Bown", [128, 1024], BF16)
        identb = sb("identb", [128, 128], BF16)
        identf = sb("identf", [128, 128], F32)
        onesm = sb("onesm", [128, 128], BF16)
        PV = sb("PVEC", [128, 4, 34], F32)
        GP = sb("GP", [128, 8], F32)
        GPh = sb("GPh", [128, 8], F32)
        GA = sb("GA", [128, 4], F32)
        sst = sb("sst", [128, 16], F32)
        skt = sb("skt", [1, 8], F32)
        ske = sb("ske", [1, 8], F32)
        vmrow = sb("vmrow", [1, 2, 128], BF16)
        MK2 = [sb("MK%d" % i, [128, NMETA], BF16) for i in range(2)]
        VM = sb("VM", [49, 128], BF16)
        UM = sb("UM", [128, 4, HALO], BF16)
        KC = sb("KC", [128, NSB, 128], BF16)
        VC = sb("VC", [128, NSB, 2, 128], BF16)
        CST = sb("CST", [128, 128], F32)
        VS = VA[0:64, 0:NSB, :, :]
        US = U[:].rearrange("p j t -> p (j t)")[:, 0:4 * NSB * (HALO + DEC)].rearrange("p (j i t) -> p j i t", j=4, i=NSB)
        OST = RD
        VST = RD[0:34, :]
        VST2 = TBb[0:8, 0:128]
        CSB = Xs[:, 0:128]

        pz = [ps("pz%d" % i, [128, 512], F32) for i in range(2)]
        ptr = ps("ptr", [128, 8, 128], BF16)
        pst = ps("pst", [128, 512], F32)
        pS = [ps("pS%d" % i, [128, 512], F32) for i in range(3)]
        pO = ps("pO", [128, 512], F32)

        cnt = {"n": 0}

        def uniq(p):
            cnt["n"] += 1
            return "%s%d" % (p, cnt["n"])

        def dma(out, in_, reads, writes, **kw):
            return S.dma(lambda e: [e.dma_start(out=out, in_=in_, **kw)], None, reads=reads, writes=writes)

        def rsqrt_act(out_ap, in_ap, n, eps, reads, writes, tmp_ap, tmpname):
            S.op("act", lambda e: e.activation(out=tmp_ap, in_=in_ap, func=AF.Ln, bias=eps, scale=1.0 / n),
                 reads=reads, writes=[tmpname])
            S.op("act", lambda e: e.activation(out=out_ap, in_=tmp_ap, func=AF.Exp, scale=-0.5),
                 reads=[tmpname], writes=writes)

        ALLXS = ["XS0", "XS1", "XS1a", "XS1b", "XS2"]

        dma(XS[:, 0, 0:128], c_ident[:, :], [], ["XS0"])
        S.op("dve", lambda e: e.tensor_copy(out=identb[:], in_=XS[:, 0, 0:128]), reads=["XS0"], writes=["identb"])
        S.op("act", lambda e: e.copy(out=identf[:], in_=XS[:, 0, 0:128]), reads=["XS0"], writes=["identf"])
        dma(XS[:, 1, :], c_bprev[:, :], [], ["XS1"])
        S.op("dve", lambda e: e.tensor_copy(out=Bprev[:], in_=XS[:, 1, :]), reads=["XS1"], writes=["Bprev"])
        dma(XS[:, 2, :], c_bown[:, :], [], ["XS2"])
        S.op("dve", lambda e: e.tensor_copy(out=Bown[:], in_=XS[:, 2, :]), reads=["XS2"], writes=["Bown"])
        S.op("pool", lambda e: e.memset(onesm[:], 1.0 / 512.0), writes=["onesm"])
        S.op("pool", lambda e: e.memset(VA[:], 1.0), writes=["VA"])
        S.op("pool", lambda e: e.memset(VC[:], 1.0), writes=["VC"])
        S.op("pool", lambda e: e.memset(VM[:], 1.0), writes=["VM"])
        S.op("pool", lambda e: e.memset(UM[:], 0.0), writes=["UM"])
        for kv in range(2):
            S.op("pool", lambda e, kv=kv: e.memset(KT2[kv][:], 0.0), writes=["KT"])
            S.op("pool", lambda e, kv=kv: e.memset(MK2[kv][:], 0.0), writes=["MK"])
        dma(Gpost[:], g_post[0:1, :].partition_broadcast(128), [], ["Gpost"])
        dma(VST, vecs[:, :], [], ["RD"])
        dma(VST2, g_pre[:, :], [], ["TBb"])
        for kv in range(2):
            dma(GA[kv * 64:(kv + 1) * 64, :],
                g_att[0, kv * 256:(kv + 1) * 256].rearrange("(g d) -> d g", d=64), [], ["GA"],
                allow_slow_non_contiguous=True)
        dma(skt[:], sinks[:, :], [], ["skt"])

        def tr_vec(e):
            for j in range(4):
                e.transpose(pst[:, j * 34:(j + 1) * 34], VST[:, j * 128:(j + 1) * 128], identf[0:34, 0:34])
            return e.transpose(pst[:, 136:144], VST2, identf[0:8, 0:8])
        S.op("pe", tr_vec, reads=["RD", "TBb", "identf"], writes=["pst"])
        S.op("dve", lambda e: e.tensor_copy(out=PV[:].rearrange("p j t -> p (j t)"), in_=pst[:, 0:136]),
             reads=["pst"], writes=["PV"])
        S.op("dve", lambda e: e.tensor_copy(out=GP[:], in_=pst[:, 136:144]), reads=["pst"], writes=["GP"])

        S.op("act", lambda e: e.activation(out=ske[:], in_=skt[:], func=AF.Exp), reads=["skt"], writes=["ske"])
        S.op("pool", lambda e: e.memset(vmrow[:], 0.0), writes=["vmrow"])
        S.op("pool", lambda e: e.memset(vmrow[0:1, 0, 64:128], 1.0), reads=["vmrow"], writes=["vmrow"])
        S.op("pool", lambda e: e.memset(vmrow[0:1, 1, 0:64], 1.0), reads=["vmrow"], writes=["vmrow"])

        def write_sink_rows(nq):
            row = TBa[0:1, :].bitcast(BF16)
            for kv in range(2):
                S.op("dve", lambda e, kv=kv: e.tensor_copy(
                    out=row[0:1, kv * 512:kv * 512 + 4 * nq].rearrange("p (g q) -> p g q", g=4),
                    in_=ske[0:1, kv * 4:(kv + 1) * 4].unsqueeze(2).to_broadcast([1, 4, nq])),
                    reads=["ske", "TBa"], writes=["TBa"])
            for pb in range(2):
                for kv in range(2):
                    dma(P2[pb][kv * 32 + NMETA:kv * 32 + NMETA + 1, 0:4 * nq], row[0:1, kv * 512:kv * 512 + 4 * nq],
                        ["TBa", "P%d_2" % pb], ["P%d_2" % pb])

        write_sink_rows(128)

        def perm_out(k, base):
            return Win[:, k, base:base + 512].rearrange("p (g kv d) -> p kv g d", g=4, kv=2)

        def perm_in(stg, base):
            return stg[:, base:base + 512].rearrange("p (kv g d) -> p kv g d", kv=2, g=4)

        HC = D_IN // 2
        S.op("dve", lambda e: e.tensor_scalar(out=GPh[:], in0=GP[:], scalar1=0.5, scalar2=None, op0=ALU.mult), reads=["GP"], writes=["GP"])
        stgbufs = [
            (H2[0][:].rearrange("p a b -> p (a b)"), ["H0_%d" % j for j in range(4)]),
            (H2[1][:].rearrange("p a b -> p (a b)"), ["H1_%d" % j for j in range(4)]),
            (XR[:].rearrange("p a b -> p (a b)"), ["XR0", "XR1"]),
        ]
        rot = {"n": 0}

        def next_stage():
            b = stgbufs[rot["n"] % len(stgbufs)]
            rot["n"] += 1
            return b

        segs = [(OQ, 512, "perm"), (OK_, 256, "plain"), (OGA, 512, "perm"), (OGB, 512, "plain"), (OB, 512, "plain"), (OA, 512, "half")]

        def thread_W():
            cvt = {"n": 0}
            for (c0, wd, mode) in segs:
                for kh in range(2):
                    flat, names = next_stage()
                    stg = flat[:, 0:4 * wd].rearrange("p (kk c) -> p kk c", kk=4)
                    dma(stg, w_in[kh * 512:(kh + 1) * 512, c0:c0 + wd].rearrange("(kk p) c -> p kk c", p=128), [], names)
                    for kk in range(4):
                        k = kh * 4 + kk
                        sc = GPh[:, k:k + 1] if mode == "half" else GP[:, k:k + 1]
                        if mode == "perm":
                            o_ap = Win[:, k, c0:c0 + 512].rearrange("p (g kv d) -> p kv g d", g=4, kv=2)
                            i_ap = stg[:, kk, :].rearrange("p (kv g d) -> p kv g d", kv=2, g=4)
                        else:
                            o_ap = Win[:, k, c0:c0 + wd]
                            i_ap = stg[:, kk, :]
                        cvt["n"] += 1
                        if cvt["n"] % 2 == 0:
                            S.op("act", lambda e, o_ap=o_ap, i_ap=i_ap, sc=sc: e.activation(out=o_ap, in_=i_ap, func=AF.Copy, scale=sc),
                                 reads=names + ["GP"], writes=["W%d_%d" % (c0, k)])
                        else:
                            S.op("dve", lambda e, o_ap=o_ap, i_ap=i_ap, sc=sc: e.tensor_scalar(out=o_ap, in0=i_ap, scalar1=sc, scalar2=None, op0=ALU.mult),
                                 reads=names + ["GP"], writes=["W%d_%d" % (c0, k)])
                    yield None
                yield ("flag", "W%d" % c0)
            for gp in range(2):
                flat, names = next_stage()
                stgA = flat[:, 0:2 * D].rearrange("p (a b) -> p a b", a=2)
                for kv in range(2):
                    dma(stgA[kv * 64:(kv + 1) * 64, :, :],
                        w_out[kv * 256 + gp * 128:kv * 256 + (gp + 1) * 128, :].rearrange("(g d) n -> d g n", d=64),
                        [], names if kv == 0 else ["XSw"])
                for gl in range(2):
                    g = gp * 2 + gl
                    S.op("act", lambda e, g=g, gl=gl, stgA=stgA: e.activation(out=WoA[:, g, :], in_=stgA[:, gl, :], func=AF.Copy, scale=GA[:, g:g + 1]),
                         reads=names + ["XSw", "GA"], writes=["WoA", "XSw"])
                yield None
            for hh in range(2):
                flat, names = next_stage()
                stgC = flat[:, 0:2 * D].rearrange("p (a b) -> p a b", a=2)
                dma(stgC, w_out[512 + hh * 256:512 + (hh + 1) * 256, :].rearrange("(j p) n -> p j n", p=128), [], names)
                for jl in range(2):
                    j = hh * 2 + jl
                    S.op("dve", lambda e, j=j, jl=jl, stgC=stgC: e.tensor_scalar(out=WoC[:, j, :], in0=stgC[:, jl, :], scalar1=PV[:, j, 33:34], scalar2=None, op0=ALU.mult),
                         reads=names + ["PV"], writes=["WoC"])
                yield None
            flat, names = next_stage()
            stgP = flat[:, 0:4 * 512].rearrange("p (j n) -> p j n", n=512)
            dma(stgP, w_pw.rearrange("(j p) n -> p j n", p=128), [], names)
            S.op("dve", lambda e: e.tensor_copy(out=Wpw[:], in_=stgP), reads=names, writes=["Wpw"])
            yield None
            for j in range(4):
                for tau in range(NPE):
                    eng = "act" if (j * NPE + tau) % 2 == 0 else "dve"
                    if eng == "act":
                        S.op("act", lambda e, j=j, tau=tau: e.activation(out=Dg[:, j * NPE + tau, :], in_=identf[:, :], func=AF.Copy, scale=PV[:, j, tau:tau + 1]),
                             reads=["identf", "PV"], writes=["Dg"])
                    else:
                        S.op("dve", lambda e, j=j, tau=tau: e.tensor_scalar(out=Dg[:, j * NPE + tau, :], in0=identf[:, :], scalar1=PV[:, j, tau:tau + 1], scalar2=None, op0=ALU.mult),
                             reads=["identf", "PV"], writes=["Dg"])
                yield None
            yield ("flag", "W_done")

        def rms_and_transpose(xsrc, xres, T, xt_col0, dst=None, dstname="XT"):
            S.op("act", lambda e: e.activation(out=TBa[0:T, :].bitcast(BF16)[:, 0:D], in_=xsrc, func=AF.Square, accum_out=sst[0:T, 0:1]),
                 reads=xres, writes=["TBa", "sst0"])
            rsqrt_act(sst[0:T, 1:2], sst[0:T, 0:1], float(D), RMS_EPS, ["sst0"], ["sst1"], sst[0:T, 2:3], "sst2")
            S.op("act", lambda e: e.activation(out=Xs[0:T, :], in_=xsrc, func=AF.Copy, scale=sst[0:T, 1:2]),
                 reads=xres + ["sst1"], writes=["Xs"])

            def tr(e):
                for k in range(8):
                    r = e.transpose(ptr[:, k, 0:T], Xs[0:T, k * 128:(k + 1) * 128], identb[0:T, 0:T])
                return r
            S.op("pe", tr, reads=["Xs", "identb"], writes=["ptr"])
            d_ap = XT[:, :, xt_col0:xt_col0 + T] if dst is None else dst
            S.op("dve", lambda e: e.tensor_copy(out=d_ap, in_=ptr[:, :, 0:T]), reads=["ptr"], writes=[dstname])

        pa = [pS[0], pS[1], pS[2], pO]
        pan = ["pS0", "pS1", "pS2", "pO"]

        def wnames(col0):
            for (c0, wd, _m) in segs:
                if c0 <= col0 < c0 + wd:
                    return ["W%d_%d" % (c0, k) for k in range(8)]
            raise AssertionError(col0)

        def fm_chunk(col0, T, bank, xt=None, xtname="XT"):
            def mm(e):
                for k in range(8):
                    r = e.matmul(pa[bank][:, 0:T], lhsT=Win[:, k, col0:col0 + 128], rhs=(XT if xt is None else xt)[:, k, 0:T], start=(k == 0), stop=(k == 7))
                return r
            S.op("pe", mm, reads=wnames(col0) + [xtname], writes=[pan[bank]])

        def tm_matmul(col0, tok0, ntok, xt=None, xtname="XT"):
            def mm(e):
                for k in range(8):
                    r = e.matmul(pst[0:ntok, 0:128], lhsT=(XT if xt is None else xt)[:, k, tok0:tok0 + ntok], rhs=Win[:, k, col0:col0 + 128],
                                 start=(k == 0), stop=(k == 7))
                return r
            S.op("pe", mm, reads=wnames(col0) + [xtname], writes=["pst"])

        def v_aug_copy(e, tile_ap, src_ap):
            e.copy(out=tile_ap[:, 0, 0:64], in_=src_ap[:, 0:64])
            return e.copy(out=tile_ap[:, 1, 64:128], in_=src_ap[:, 64:128])

        def attention_unit(nq, qcol0, kprev, vprev, kown, vown, nown, pbuf, full):
            N = 4 * nq
            Pp, Po = P01[pbuf]
            Pm = P2[pbuf]
            for kv in range(2):
                rows = slice(kv * 64, (kv + 1) * 64)
                mrows = slice(kv * 32, kv * 32 + NMETA)
                mrows1 = slice(kv * 32, kv * 32 + NMETA + 1)
                qrhs = QT[:, :, qcol0:qcol0 + nq] if full else QT[rows, :, qcol0:qcol0 + nq]
                mk = MK2[kv][:, :] if full else MK2[kv][rows, :]
                bp = Bprev[:, kv * 512:(kv + 1) * 512].rearrange("p (g q) -> p g q", g=4)[:, :, 0:nq]
                bo = Bown[:, kv * 512:(kv + 1) * 512].rearrange("p (g q) -> p g q", g=4)[:, :, 0:nq]

                def qk(e, kv=kv, qrhs=qrhs, bp=bp, bo=bo, mrows=mrows, mk=mk):
                    if kprev is not None:
                        e.matmul(pS[0][:, 0:N], lhsT=kprev(kv), rhs=qrhs, start=True, stop=False)
                        e.matmul(pS[0][:, 0:N], lhsT=identb[:, :], rhs=bp, start=False, stop=True)
                    e.matmul(pS[1][0:nown, 0:N], lhsT=kown(kv), rhs=qrhs, start=True, stop=False)
                    e.matmul(pS[1][0:nown, 0:N], lhsT=identb[:, 0:nown], rhs=bo, start=False, stop=True)
                    return e.matmul(pS[2][mrows, 0:N], lhsT=mk, rhs=qrhs, start=True, stop=True)
                S.op("pe", qk, reads=["QT", "KT", "KC", "MK", "Bprev", "Bown", "identb"], writes=["pS0", "pS1", "pS2"])
                if kprev is not None:
                    S.op("act", lambda e: e.activation(out=Pp[:, 0:N], in_=pS[0][:, 0:N], func=AF.Exp, scale=SCALE),
                         reads=["pS0"], writes=["P%d_0" % pbuf])
                S.op("act", lambda e: e.activation(out=Po[0:nown, 0:N], in_=pS[1][0:nown, 0:N], func=AF.Exp, scale=SCALE),
                     reads=["pS1"], writes=["P%d_1" % pbuf])
                S.op("act", lambda e, mrows=mrows: e.activation(out=Pm[mrows, 0:N], in_=pS[2][mrows, 0:N], func=AF.Exp, scale=SCALE),
                     reads=["pS2"], writes=["P%d_2" % pbuf])

                def pv(e, kv=kv, mrows1=mrows1):
                    first = True
                    if kprev is not None:
                        e.matmul(pO[:, 0:N], lhsT=vprev(kv), rhs=Pp[:, 0:N], start=True, stop=False)
                        first = False
                    e.matmul(pO[:, 0:N], lhsT=vown(kv), rhs=Po[0:nown, 0:N], start=first, stop=False)
                    return e.matmul(pO[:, 0:N], lhsT=VM[mrows1, :], rhs=Pm[mrows1, 0:N], start=False, stop=True)
                S.op("pe", pv, reads=["VA", "VC", "VM", "P%d_0" % pbuf, "P%d_1" % pbuf, "P%d_2" % pbuf], writes=["pO"])
                num = slice(0, 64) if kv == 0 else slice(64, 128)
                den = slice(64, 128) if kv == 0 else slice(0, 64)
                S.op("act", lambda e, num=num, den=den: e.activation(out=RD[num, 0:N], in_=pO[den, 0:N], func=AF.Ln), reads=["pO"], writes=["RD"])
                S.op("act", lambda e, num=num: e.activation(out=RD[num, 0:N], in_=RD[num, 0:N], func=AF.Exp, scale=-1.0), reads=["RD"], writes=["RD"])
                yield None
                S.op("dve", lambda e, num=num: e.tensor_tensor(
                    out=ATT[num, :, qcol0:qcol0 + nq], in0=pO[num, 0:N].rearrange("p (g q) -> p g q", g=4),
                    in1=RD[num, 0:N].rearrange("p (g q) -> p g q", g=4), op=ALU.mult),
                    reads=["pO", "RD"], writes=["ATT"])
                yield None

        def att_finish(T, par):
            A = AM[par]
            an = "AM%d" % par
            S.op("act", lambda e: e.activation(out=A[:, :, 0:T], in_=ATT[:, :, 0:T], func=AF.Square), reads=["ATT"], writes=[an])

            def stats(e):
                for j in range(4):
                    r = e.matmul(pS[0][:, 0:T], lhsT=onesm[:, :], rhs=A[:, j, 0:T], start=(j == 0), stop=(j == 3))
                return r
            S.op("pe", stats, reads=[an, "onesm"], writes=["pS0"])
            S.op("act", lambda e: e.activation(out=TBa[:, 0:T], in_=pS[0][:, 0:T], func=AF.Ln, bias=RMS_EPS, scale=1.0), reads=["pS0"], writes=["TBa"])
            S.op("act", lambda e: e.activation(out=RD[:, 0:T], in_=TBa[:, 0:T], func=AF.Exp, scale=-0.5), reads=["TBa"], writes=["RD"])
            for g in range(4):
                S.op("dve", lambda e, g=g: e.tensor_tensor(out=ATT[:, g, 0:T], in0=ATT[:, g, 0:T], in1=SGA[:, g, 0:T], op=ALU.mult),
                     reads=["ATT", "SGA"], writes=["ATT"])
                S.op("pool", lambda e, g=g: e.tensor_tensor(out=A[:, g, 0:T], in0=ATT[:, g, 0:T], in1=RD[:, 0:T], op=ALU.mult),
                     reads=["ATT", "RD"], writes=[an])

        xr_state = {"n": 0}

        def out_tile(tok0, ntok, par, xsrc_dram, ydst):
            A = AM[par]
            h = xr_state["n"]
            xr_state["n"] += 1
            xs = h % 2
            xn_ = "XR%d" % xs
            dma(XR[0:ntok, xs, :], xsrc_dram, [], [xn_])

            def mm(e):
                for half in range(2):
                    for g in range(4):
                        e.matmul(pz[half][0:ntok, :], lhsT=A[:, g, tok0:tok0 + ntok], rhs=WoA[:, g, half * 512:(half + 1) * 512],
                                 start=(g == 0), stop=False)
                    for j in range(4):
                        r = e.matmul(pz[half][0:ntok, :], lhsT=CM[:, j, tok0:tok0 + ntok], rhs=WoC[:, j, half * 512:(half + 1) * 512],
                                     start=False, stop=(j == 3))
                return r
            S.op("pe", mm, reads=["AM%d" % par, "SQ0", "SQ1", "SQ2", "SQ3", "WoA", "WoC"], writes=["pz0", "pz1"])
            for half in range(2):
                S.op("act", lambda e, half=half: e.activation(out=TBb[0:ntok, :].bitcast(BF16)[:, 0:512], in_=pz[half][0:ntok, :], func=AF.Square,
                                                              accum_out=sst[0:ntok, 4 + half:5 + half]),
                     reads=["pz%d" % half], writes=["TBb", "sst%d" % (4 + half)])
            S.op("pool", lambda e: e.tensor_tensor(out=sst[0:ntok, 6:7], in0=sst[0:ntok, 4:5], in1=sst[0:ntok, 5:6], op=ALU.add),
                 reads=["sst4", "sst5"], writes=["sst6"])
            rsqrt_act(sst[0:ntok, 7:8], sst[0:ntok, 6:7], float(D), RMS_EPS, ["sst6"], ["sst7"], sst[0:ntok, 8:9], "sst8")
            for half in range(2):
                S.op("act", lambda e, half=half: e.activation(out=YS[0:ntok, half * 512:(half + 1) * 512], in_=pz[half][0:ntok, :],
                                                              func=AF.Copy, scale=sst[0:ntok, 7:8]),
                     reads=["pz%d" % half, "sst7"], writes=["YS0"])
            S.op("pool", lambda e: e.tensor_tensor(out=YS[0:ntok, :], in0=YS[0:ntok, :], in1=Gpost[0:ntok, :], op=ALU.mult),
                 reads=["YS0", "Gpost"], writes=["YS0"])
            yield None
            yield None
            S.op("dve", lambda e: e.tensor_tensor(out=XR[0:ntok, xs, :], in0=YS[0:ntok, :], in1=XR[0:ntok, xs, :], op=ALU.add),
                 reads=["YS0", xn_], writes=[xn_])
            dma(ydst, XR[0:ntok, xs, :], [xn_], [uniq("yout")])
            yield None

        XTm = KT2[0][:, 0:128].rearrange("p (k t) -> p k t", k=8)
        dma(XS[0:NMETA, 0, :], meta[:, :], [], ["XS0"])
        rms_and_transpose(XS[0:NMETA, 0, :], ["XS0"], NMETA, 0, dst=XTm[:, :, 0:NMETA], dstname="KT")
        def meta_kv():
            fm_chunk(OK_, NMETA, 0, xt=XTm, xtname="KT")

            def mk_copy(e):
                e.tensor_copy(out=MK2[0][0:64, :], in_=pa[0][0:64, 0:NMETA])
                return e.tensor_copy(out=MK2[1][64:128, :], in_=pa[0][64:128, 0:NMETA])
            S.op("dve", mk_copy, reads=[pan[0]], writes=["MK"])
            tm_matmul(OV, 0, NMETA, xt=XTm, xtname="KT")

            def vm_copy(e):
                e.copy(out=VM[0:NMETA, 0:64], in_=pst[0:NMETA, 0:64])
                return e.copy(out=VM[32:32 + NMETA, 64:128], in_=pst[0:NMETA, 64:128])
            S.op("act", vm_copy, reads=["pst"], writes=["VM"])
            dma(VM[NMETA:NMETA + 1, :], vmrow[0:1, 0, :], ["vmrow", "VM"], ["VM"])
            dma(VM[32 + NMETA:32 + NMETA + 1, :], vmrow[0:1, 1, :], ["vmrow", "VM"], ["VM"])


        def meta_glu():
            for j in range(4):
                fm_chunk(OB + j * 128, NMETA, 0, xt=XTm, xtname="KT")
                fm_chunk(OA + j * 128, NMETA, 1, xt=XTm, xtname="KT")
                S.op("act", lambda e: e.activation(out=TBa[:, 0:NMETA], in_=pa[0][:, 0:NMETA], func=AF.Tanh, scale=0.5), reads=[pan[0]], writes=["TBa"])
                S.op("dve", lambda e, j=j: e.scalar_tensor_tensor(out=UM[:, j, HALO - NMETA:HALO], in0=TBa[:, 0:NMETA], scalar=1.0, in1=pa[1][:, 0:NMETA],
                                                              op0=ALU.add, op1=ALU.mult),
                     reads=["TBa", pan[1]], writes=["UM"])

        for i in range(NSB):
            dma(CST[:], cache_k[i, :, :], [], ["CST"])
            S.op("dve", lambda e: e.tensor_copy(out=CSB[:], in_=CST[:]), reads=["CST"], writes=["Xs"])
            S.op("pe", lambda e: e.transpose(ptr[:, 0, :], CSB[:, :], identb[:, :]), reads=["Xs", "identb"], writes=["ptr"])
            S.op("act", lambda e, i=i: e.copy(out=KC[:, i, :], in_=ptr[:, 0, :]), reads=["ptr"], writes=["KC"])
            dma(CST[:], cache_v[i, :, :], [], ["CST"])
            S.op("act", lambda e, i=i: v_aug_copy(e, VC[:, i, :, :], CST[:, :]), reads=["CST"], writes=["VC"])
            dma(nk_s[i, 0:64, :], cache_k[i, 64:128, :], [], [uniq("o")])
            dma(nv_s[i, 0:64, :], cache_v[i, 64:128, :], [], [uniq("o")])

        nst = SEQ // ST
        units = []
        for b in range(NPB):
            for s_ in range(nst):
                units.append(dict(kind="p", b=b, s=s_, first=(s_ == 0), last=(s_ == nst - 1), T=ST, ntiles=4))
        units.append(dict(kind="s", T=NSB * DEC, ntiles=2, first=True, last=True))
        for i, u in enumerate(units):
            u["par"] = i % 2
            u["i"] = i

        tiles = []
        for u in units:
            for t in range(u["ntiles"]):
                if u["kind"] == "p":
                    tiles.append(x_p[u["b"], u["s"] * ST + t * 128:u["s"] * ST + (t + 1) * 128, :])
                else:
                    tiles.append(x_s[t * 128:(t + 1) * 128, :])
        ring = {"issued": 0}

        def ensure_loaded(gidx, ahead=2):
            while ring["issued"] < min(len(tiles), gidx + 1 + ahead):
                g = ring["issued"]
                sl = g % NXS
                dma(XS[:, sl, :], tiles[g], [], ["XS%d" % sl] + (["XS1a", "XS1b"] if sl == 1 else []))
                ring["issued"] += 1

        gt = {"n": 0}

        def thread_A(u):
            T = u["T"]
            par = u["par"]
            i = u["i"]
            p = (u["kind"] == "p")
            if p and not u["first"]:
                for kv in range(2):
                    S.op("pool", lambda e, kv=kv: e.tensor_copy(out=KT2[kv][:, 0:128], in_=KT2[kv][:, ST:ST + 128]), reads=["KT"], writes=["KT"])
                S.op("pool", lambda e: e.tensor_copy(out=VA[:, 0, :, :], in_=VA[:, 4, :, :]), reads=["VA"], writes=["VA"])
            for t in range(u["ntiles"]):
                g = gt["n"]
                gt["n"] += 1
                ensure_loaded(g, ahead=1)
                rms_and_transpose(XS[:, g % NXS, :], ["XS%d" % (g % NXS)], 128, t * 128)
                yield None
            if i == 0:
                yield ("need", "W%d" % OQ)
            for g in range(4):
                fm_chunk(OQ + g * 128, T, g)
                S.op("act", lambda e, g=g: e.copy(out=QT[:, g, 0:T], in_=pa[g][:, 0:T]), reads=[pan[g]], writes=["QT"])
                yield None
            if i == 0:
                yield ("need", "W%d" % OK_)
                meta_kv()
            fm_chunk(OK_, T, 0)
            def k_copy(e):
                e.copy(out=KT2[0][0:64, 128:128 + T], in_=pa[0][0:64, 0:T])
                return e.copy(out=KT2[1][64:128, 128:128 + T], in_=pa[0][64:128, 0:T])
            S.op("act", k_copy, reads=[pan[0]], writes=["KT"])
            yield None
            if p:
                for t in range(4):
                    tm_matmul(OV, t * 128, 128)
                    S.op("act", lambda e, t=t: v_aug_copy(e, VA[:, 1 + t, :, :], pst[:, 0:128]), reads=["pst"], writes=["VA"])
                    if u["last"] and t == 3:
                        S.op("act", lambda e: e.copy(out=CST[:, :], in_=pst[:, 0:128]), reads=["pst"], writes=["CST"])
                        dma(nv_p[u["b"], :, :], CST[:, :], ["CST"], [uniq("o")])
                        tm_matmul(OK_, t * 128, 128)
                        S.op("act", lambda e: e.copy(out=CST[:, :], in_=pst[:, 0:128]), reads=["pst"], writes=["CST"])
                        dma(nk_p[u["b"], :, :], CST[:, :], ["CST"], [uniq("o")])
                    yield None
            else:
                for q in range(NSB):
                    tm_matmul(OV, q * DEC, DEC)
                    S.op("act", lambda e, q=q: v_aug_copy(e, VS[:, q, :, :], pst[0:DEC, 0:128]), reads=["pst"], writes=["VA"])
                    S.op("act", lambda e: e.copy(out=CST[0:DEC, :], in_=pst[0:DEC, 0:128]), reads=["pst"], writes=["CST"])
                    dma(nv_s[q, 64:128, :], CST[0:DEC, :], ["CST"], [uniq("o")])
                    tm_matmul(OK_, q * DEC, DEC)
                    S.op("act", lambda e: e.copy(out=CST[0:DEC, :], in_=pst[0:DEC, 0:128]), reads=["pst"], writes=["CST"])
                    dma(nk_s[q, 64:128, :], CST[0:DEC, :], ["CST"], [uniq("o")])
                    yield None
            if i == 0:
                yield ("need", "W%d" % OGA)
            for g in range(4):
                fm_chunk(OGA + g * 128, T, g)
                S.op("act", lambda e, g=g: e.activation(out=SGA[:, g, 0:T], in_=pa[g][:, 0:T], func=AF.Silu), reads=[pan[g]], writes=["SGA"])
                yield None
            if i >= 2:
                yield ("need", "gate_done%d" % (i - 2))
            if i == 0:
                yield ("need", "W%d" % OGB)
            for j in range(4):
                fm_chunk(OGB + j * 128, T, j)
                S.op("act", lambda e, j=j: e.activation(out=SGB[par][:, j, 0:T], in_=pa[j][:, 0:T], func=AF.Silu),
                     reads=[pan[j]], writes=["SGB%d" % par])
                yield None
            if p:
                for t in range(4):
                    has_prev = not (u["first"] and t == 0)
                    yield from attention_unit(
                        128, t * 128,
                        (lambda kv, t=t: KT2[kv][:, t * 128:(t + 1) * 128]) if has_prev else None,
                        (lambda kv, t=t: VA[:, t, kv, :]),
                        (lambda kv, t=t: KT2[kv][:, 128 + t * 128:128 + (t + 1) * 128]),
                        (lambda kv, t=t: VA[:, 1 + t, kv, :]),
                        128, t % 2, True)
            else:
                write_sink_rows(DEC)
                for q in range(NSB):
                    yield from attention_unit(
                        DEC, q * DEC,
                        (lambda kv, q=q: KC[kv * 64:(kv + 1) * 64, q, :]),
                        (lambda kv, q=q: VC[:, q, kv, :]),
                        (lambda kv, q=q: KT2[kv][kv * 64:(kv + 1) * 64, 128 + q * DEC:128 + (q + 1) * DEC]),
                        (lambda kv, q=q: VS[:, q, kv, :]),
                        DEC, q % 2, False)
            if i >= 3:
                yield ("need", "out_done%d" % (i - 3))
            att_finish(T, i % 3)
            yield None
            if i >= 1:
                yield ("need", "conv_done%d" % (i - 1))
            if i == 0:
                yield ("need", "W%d" % OB)
                yield ("need", "W%d" % OA)
                meta_glu()
            if p:
                src = UM if u["first"] else UH
                S.op("pool", lambda e: e.tensor_copy(out=U[:, :, 0:HALO], in_=src[:, :, :]), reads=["UM", "UH", "U"], writes=["U"])
            else:
                for q in range(NSB):
                    dma(OST[0:HALO, :], state_conv[q, :, :], [], ["RD"])

                    def trs(e):
                        for j in range(4):
                            r = e.transpose(pst[:, j * 32:j * 32 + HALO], OST[0:HALO, j * 128:(j + 1) * 128], identf[0:HALO, 0:HALO])
                        return r
                    S.op("pe", trs, reads=["RD", "identf"], writes=["pst"])
                    S.op("act", lambda e, q=q: e.copy(out=US[:, :, q, 0:HALO], in_=pst[:, 0:128].rearrange("p (j t) -> p j t", t=32)[:, :, 0:HALO]),
                         reads=["pst", "U"], writes=["U"])
                yield None
            for j in range(4):
                bb = 2 * (j % 2)
                fm_chunk(OB + j * 128, T, bb)
                fm_chunk(OA + j * 128, T, bb + 1)
                S.op("act", lambda e, bb=bb: e.activation(out=TBa[:, 0:T], in_=pa[bb][:, 0:T], func=AF.Tanh, scale=0.5), reads=[pan[bb]], writes=["TBa"])
                if p:
                    S.op("dve", lambda e, j=j, bb=bb: e.scalar_tensor_tensor(out=U[:, j, HALO:HALO + ST], in0=TBa[:, :], scalar=1.0, in1=pa[bb + 1][:, :],
                                                                  op0=ALU.add, op1=ALU.mult),
                         reads=["TBa", pan[bb + 1], "U"], writes=["U"])
                    if u["last"]:
                        S.op("dve", lambda e, j=j, bb=bb: e.scalar_tensor_tensor(out=UL[:, j, 0:HALO], in0=TBa[:, ST - HALO:ST], scalar=1.0,
                                                                      in1=pa[bb + 1][:, ST - HALO:ST], op0=ALU.add, op1=ALU.mult),
                             reads=["TBa", pan[bb + 1]], writes=["UL"])
                else:
                    S.op("dve", lambda e, j=j, bb=bb: e.scalar_tensor_tensor(
                        out=US[:, j, :, HALO:HALO + DEC], in0=TBa[:, 0:T].rearrange("p (q t) -> p q t", q=NSB), scalar=1.0,
                        in1=pa[bb + 1][:, 0:T].rearrange("p (q t) -> p q t", q=NSB), op0=ALU.add, op1=ALU.mult),
                        reads=["TBa", pan[bb + 1], "U"], writes=["U"])
                    S.op("dve", lambda e, j=j, bb=bb: e.scalar_tensor_tensor(
                        out=UL[:, j, :].rearrange("p (q t) -> p q t", q=NSB),
                        in0=TBa[:, 0:T].rearrange("p (q t) -> p q t", q=NSB)[:, :, DEC - HALO:DEC], scalar=1.0,
                        in1=pa[bb + 1][:, 0:T].rearrange("p (q t) -> p q t", q=NSB)[:, :, DEC - HALO:DEC], op0=ALU.add, op1=ALU.mult),
                        reads=["TBa", pan[bb + 1]], writes=["UL"])
                yield None
            if p and not u["last"]:
                S.op("pool", lambda e: e.tensor_copy(out=UH[:, :, :], in_=U[:, :, ST:ST + HALO]), reads=["U"], writes=["UH"])
            if p and u["last"]:
                def tru(e):
                    for j in range(4):
                        r = e.transpose(pst[0:HALO, j * 128:(j + 1) * 128], UL[:, j, 0:HALO], identf[:, :])
                    return r
                S.op("pe", tru, reads=["UL", "identf"], writes=["pst"])
                S.op("act", lambda e: e.copy(out=OST[0:HALO, :], in_=pst[0:HALO, :]), reads=["pst"], writes=["RD"])
                dma(nc_p[u["b"], :, :], OST[0:HALO, :], ["RD"], [uniq("o")])
            if not p:
                for q in range(NSB):
                    def tru(e, q=q):
                        for j in range(4):
                            r = e.transpose(pst[0:HALO, j * 128:(j + 1) * 128], UL[:, j, q * HALO:(q + 1) * HALO], identf[:, :])
                        return r
                    S.op("pe", tru, reads=["UL", "identf"], writes=["pst"])
                    S.op("act", lambda e: e.copy(out=OST[0:HALO, :], in_=pst[0:HALO, :]), reads=["pst"], writes=["RD"])
                    dma(nc_s[q, :, :], OST[0:HALO, :], ["RD"], [uniq("o")])
            yield None

        def conv_gen(u):
            T = u["T"]
            p = (u["kind"] == "p")
            H = H2[u["par"]]
            hn = lambda j: "H%d_%d" % (u["par"], j)
            if p:
                u_of = lambda j, tau: U[:, j, tau:tau + ST]
                h_of = lambda buf, j: buf[:, j, 0:ST]
                c_of = lambda j: pz[j % 2][:, 0:ST]
            else:
                u_of = lambda j, tau: US[:, j, :, tau:tau + DEC]
                h_of = lambda buf, j: buf[:, j, 0:T].rearrange("p (q t) -> p q t", q=NSB)
                c_of = lambda j: pz[j % 2][:, 0:T].rearrange("p (q t) -> p q t", q=NSB)
            for j in range(4):
                def cmm(e, j=j):
                    for tau in range(NPE):
                        r = e.matmul(pz[j % 2][:, 0:T], lhsT=Dg[:, j * NPE + tau, :], rhs=u_of(j, tau), start=(tau == 0), stop=(tau == NPE - 1))
                    return r
                S.op("pe", cmm, reads=["U", "Dg"], writes=["pz%d" % (j % 2)])
                S.op("dve", lambda e, j=j: e.scalar_tensor_tensor(
                    out=h_of(H, j), in0=u_of(j, NPE), scalar=PV[:, j, NPE:NPE + 1], in1=c_of(j), op0=ALU.mult, op1=ALU.add),
                    reads=["U", "PV", "pz%d" % (j % 2)], writes=[hn(j)])
                for tau in range(NPE + 1, CW):
                    S.op("dve", lambda e, j=j, tau=tau: e.scalar_tensor_tensor(
                        out=h_of(H, j), in0=u_of(j, tau), scalar=PV[:, j, tau:tau + 1], in1=h_of(H, j), op0=ALU.mult, op1=ALU.add),
                        reads=["U", "PV", hn(j)], writes=[hn(j)])
                    if tau % 3 == 0:
                        yield None
                yield None
            yield ("flag", "conv_done%d" % u["i"])

        def back_gen(u):
            T = u["T"]
            par = u["par"]
            i = u["i"]
            hb = ["HB%d" % j for j in range(4)]
            sq = ["SQ%d" % j for j in range(4)]
            H = H2[par]
            hn = lambda j: "H%d_%d" % (par, j)
            for j in range(4):
                S.op("act", lambda e, j=j: e.copy(out=HB[:, j, 0:T], in_=H[:, j, 0:T]), reads=[hn(j)], writes=["HB%d" % j])
                S.op("act", lambda e, j=j: e.activation(out=SQ[:, j, 0:T], in_=H[:, j, 0:T], func=AF.Square), reads=[hn(j)], writes=["SQ%d" % j])
            yield None

            def stats(e):
                for j in range(4):
                    e.matmul(pz[0][:, 0:T], lhsT=onesm[:, :], rhs=HB[:, j, 0:T], start=(j == 0), stop=(j == 3))
                for j in range(4):
                    r = e.matmul(pz[1][:, 0:T], lhsT=onesm[:, :], rhs=SQ[:, j, 0:T], start=(j == 0), stop=(j == 3))
                return r
            S.op("pe", stats, reads=hb + sq + ["onesm"], writes=["pz0", "pz1"])
            S.op("act", lambda e: e.copy(out=R0[:, 0:T], in_=pz[0][:, 0:T]), reads=["pz0"], writes=["R0"])
            S.op("pool", lambda e: e.tensor_tensor(out=TBb[:, 0:T], in0=R0[:, 0:T], in1=R0[:, 0:T], op=ALU.mult), reads=["R0"], writes=["TBb"])
            S.op("act", lambda e: e.copy(out=R1[:, 0:T], in_=pz[1][:, 0:T]), reads=["pz1"], writes=["R1"])
            S.op("pool", lambda e: e.tensor_tensor(out=TBb[:, 0:T], in0=R1[:, 0:T], in1=TBb[:, 0:T], op=ALU.subtract), reads=["R1", "TBb"], writes=["TBb"])
            S.op("act", lambda e: e.activation(out=TBb[:, 0:T], in_=TBb[:, 0:T], func=AF.Ln, bias=LN_EPS, scale=1.0), reads=["TBb"], writes=["TBb"])
            S.op("act", lambda e: e.activation(out=R1[:, 0:T], in_=TBb[:, 0:T], func=AF.Exp, scale=-0.5), reads=["TBb"], writes=["R1"])
            yield None
            for j in range(4):
                S.op("dve", lambda e, j=j: e.tensor_tensor(out=H[:, j, 0:T], in0=H[:, j, 0:T], in1=R0[:, 0:T], op=ALU.subtract),
                     reads=[hn(j), "R0"], writes=[hn(j)])
                S.op("pool", lambda e, j=j: e.tensor_tensor(out=H[:, j, 0:T], in0=H[:, j, 0:T], in1=R1[:, 0:T], op=ALU.mult),
                     reads=[hn(j), "R1"], writes=[hn(j)])
                S.op("act", lambda e, j=j: e.activation(out=HB[:, j, 0:T], in_=H[:, j, 0:T], func=AF.Silu, bias=PV[:, j, 32:33], scale=PV[:, j, 31:32]),
                     reads=[hn(j), "PV"], writes=["HB%d" % j])
                yield None
            for jj in range(4):
                bank = jj % 2

                def mm(e, jj=jj, bank=bank):
                    for j in range(4):
                        r = e.matmul(pz[bank][:, 0:T], lhsT=Wpw[:, j, jj * 128:(jj + 1) * 128], rhs=HB[:, j, 0:T], start=(j == 0), stop=(j == 3))
                    return r
                S.op("pe", mm, reads=hb + ["Wpw"], writes=["pz%d" % bank])
                S.op("act", lambda e, jj=jj, bank=bank: e.activation(out=SQ[:, jj, 0:T], in_=pz[bank][:, 0:T], func=AF.Square),
                     reads=["pz%d" % bank], writes=["SQ%d" % jj])
                S.op("act", lambda e, jj=jj, bank=bank: e.copy(out=H[:, jj, 0:T], in_=pz[bank][:, 0:T]), reads=["pz%d" % bank], writes=[hn(jj)])
                S.op("pool", lambda e, jj=jj: e.tensor_tensor(out=H[:, jj, 0:T], in0=H[:, jj, 0:T], in1=SGB[par][:, jj, 0:T], op=ALU.mult),
                     reads=[hn(jj), "SGB%d" % par], writes=[hn(jj)])
                yield None
            yield ("flag", "gate_done%d" % i)

            def stats2(e):
                for j in range(4):
                    r = e.matmul(pz[0][:, 0:T], lhsT=onesm[:, :], rhs=SQ[:, j, 0:T], start=(j == 0), stop=(j == 3))
                return r
            S.op("pe", stats2, reads=sq + ["onesm"], writes=["pz0"])
            S.op("act", lambda e: e.activation(out=TBb[:, 0:T], in_=pz[0][:, 0:T], func=AF.Ln, bias=RMS_EPS, scale=1.0), reads=["pz0"], writes=["TBb"])
            S.op("act", lambda e: e.activation(out=R1[:, 0:T], in_=TBb[:, 0:T], func=AF.Exp, scale=-0.5), reads=["TBb"], writes=["R1"])
            for jj in range(4):
                S.op("pool", lambda e, jj=jj: e.tensor_tensor(out=CM[:, jj, 0:T], in0=H[:, jj, 0:T], in1=R1[:, 0:T], op=ALU.mult),
                     reads=[hn(jj), "R1"], writes=["SQ%d" % jj])
            yield None

        def out_gen(u):
            par = u["i"] % 3
            p = (u["kind"] == "p")
            for t in range(u["ntiles"]):
                if p:
                    r0 = u["s"] * ST + t * 128
                    yield from out_tile(t * 128, 128, par, x_p[u["b"], r0:r0 + 128, :], y_p[u["b"], r0:r0 + 128, :])
                else:
                    yield from out_tile(t * 128, 128, par, x_s[t * 128:(t + 1) * 128, :], y_s[t * 128:(t + 1) * 128, :])

        def interleave(g1, g2, n1=2):
            d1 = g1 is None
            d2 = g2 is None
            while not (d1 and d2):
                for _ in range(n1):
                    if not d1:
                        try:
                            yield next(g1)
                        except StopIteration:
                            d1 = True
                if not d2:
                    try:
                        yield next(g2)
                    except StopIteration:
                        d2 = True

        def A_all():
            for u in units:
                yield from thread_A(u)
                yield ("flag", "A_done%d" % u["i"])

        def chain(*gs):
            for g in gs:
                yield from g

        def one(x):
            yield x

        def B_all():
            prev = None
            for u in units:
                i = u["i"]
                yield ("need", "A_done%d" % i)
                if i == 0:
                    yield ("need", "W_done")
                yield from interleave(conv_gen(u), prev, n1=2)
                prev = chain(back_gen(u), out_gen(u), one(("flag", "out_done%d" % i)))
            yield from prev

        def run_threads(gens, weights):
            flags = set()
            st = [dict(gen=g, wait=None, done=False) for g in gens]
            order = []
            for t, w in zip(st, weights):
                order += [t] * w
            while not all(t["done"] for t in st):
                progressed = False
                for t in order:
                    if t["done"]:
                        continue
                    if t["wait"] is not None:
                        if t["wait"] not in flags:
                            continue
                        t["wait"] = None
                    try:
                        r = next(t["gen"])
                    except StopIteration:
                        t["done"] = True
                        progressed = True
                        continue
                    progressed = True
                    if isinstance(r, tuple):
                        if r[0] == "flag":
                            flags.add(r[1])
                        elif r[0] == "need" and r[1] not in flags:
                            t["wait"] = r[1]
                assert progressed, "schedule deadlock: " + str([t["wait"] for t in st])

        run_threads([thread_W(), B_all(), A_all()], [1, 1, 1])

        S.wait_all("sp", list(S.dma_count.items()))
        S.emit()
    return nc


_CACHE = {}


def _consts():
    h = np.arange(1, 9, dtype=np.float64)
    slopes = (2.0 ** (-h)).reshape(2, 4)
    j = np.arange(128)[:, None]
    i = np.arange(128)[None, :]
    NEG = -30000.0
    bprev = np.zeros((128, 2, 4, 128), np.float32)
    bown = np.zeros((128, 2, 4, 128), np.float32)
    dprev = (i + 128 - j).astype(np.float64)
    mprev = (i >= 64) & (j < 64)
    down = np.abs(i - j).astype(np.float64)
    mown = (i < 64) & (j >= 64)
    for kv in range(2):
        for g in range(4):
            bp = -slopes[kv, g] * dprev / SCALE
            bo = -slopes[kv, g] * down / SCALE
            bprev[:, kv, g, :] = np.where(mprev, NEG, bp)
            bown[:, kv, g, :] = np.where(mown, NEG, bo)
    return (np.eye(128, dtype=np.float32), bprev.reshape(128, 1024), bown.reshape(128, 1024))


def kernel(x_prompt, x_sample, cache_k, cache_v, state_conv, meta_tokens, g_pre, w_in,
           sinks, g_att, conv_w, ln_g, ln_b, w_pw, g_conv, w_out, g_post):
    f = lambda a: np.ascontiguousarray(np.asarray(a, dtype=np.float32))
    x_prompt, x_sample, cache_k, cache_v, state_conv = map(f, (x_prompt, x_sample, cache_k, cache_v, state_conv))
    if "nc" not in _CACHE:
        _CACHE["nc"] = build_program()
    nc = _CACHE["nc"]
    ident, bprev, bown = _consts()
    vecs = np.concatenate([f(conv_w)[0], f(ln_g), f(ln_b), f(g_conv)], axis=0)
    shared = {
        "meta_tokens": f(meta_tokens), "g_pre": f(g_pre).reshape(8, 128), "w_in": f(w_in)[0],
        "sinks": f(sinks), "g_att": f(g_att), "vecs": np.ascontiguousarray(vecs),
        "w_pw": f(w_pw)[0], "w_out": f(w_out)[0], "g_post": f(g_post),
        "c_ident": ident, "c_bprev": bprev, "c_bown": bown,
    }
    in_maps = []
    for c in range(NCORES):
        m = dict(shared)
        m["x_prompt"] = x_prompt[NPB * c:NPB * (c + 1)]
        m["x_sample"] = x_sample[NSB * c:NSB * (c + 1)].reshape(NSB * DEC, D)
        m["cache_k"] = cache_k[0, NSB * c:NSB * (c + 1)].reshape(NSB, 128, 128)
        m["cache_v"] = cache_v[0, NSB * c:NSB * (c + 1)].reshape(NSB, 128, 128)
        m["state_conv"] = state_conv[0, NSB * c:NSB * (c + 1)]
        in_maps.append(m)
    res = run_bass_kernel_spmd(nc, in_maps, core_ids=list(range(NCORES)))
    R = res.results
    cat = lambda k: np.concatenate([np.asarray(r[k], dtype=np.float32) for r in R], axis=0)
    y_p = cat("y_prompt")
    y_s = cat("y_sample").reshape(32, DEC, D)
    nk_p = cat("nk_p").reshape(1, 16, 128, 2, 64)
    nv_p = cat("nv_p").reshape(1, 16, 128, 2, 64)
    nc_p = cat("nc_p").reshape(1, 16, HALO, 512)
    nk_s = cat("nk_s").reshape(1, 32, 128, 2, 64)
    nv_s = cat("nv_s").reshape(1, 32, 128, 2, 64)
    nc_s = cat("nc_s").reshape(1, 32, HALO, 512)
    return (y_p, y_s, nk_p, nv_p, nc_p, nk_s, nv_s, nc_s)
```

```python
import contextlib
import numpy as np
import concourse.bass as bass
import concourse.mybir as mybir
from concourse.bass_utils import run_bass_kernel_spmd

F32 = mybir.dt.float32
BF16 = mybir.dt.bfloat16
AF = mybir.ActivationFunctionType
ALU = mybir.AluOpType

NCORES = 8
D = 1024
SEQ = 2048
NPB = 2
NSB = 4
DEC = 64
D_IN = 2816
CW = 31
HALO = CW - 1
NMETA = 16
RMS_EPS = 1e-6
LN_EPS = 1e-5
SCALE = 0.125
ST = 512
OQ, OK_, OV, OGA, OA, OB, OGB = 0, 512, 640, 768, 1280, 1792, 2304


class Sched:
    ENGS = ("pe", "act", "dve", "pool", "sp")
    EXCL = frozenset(["pz0", "pz1", "ptr", "pst", "pS0", "pS1", "pS2", "pO"])

    def __init__(self, nc):
        self.nc = nc
        self.streams = {e: [] for e in self.ENGS}
        self.count = {e: 0 for e in self.ENGS}
        self.waited = {e: {} for e in self.ENGS}
        self.dma_count = {}
        self.dma_rr = 0
        self.last_w = {}
        self.readers = {}
        self.sem_names = set(self.ENGS)

    def _deps(self, reads, writes):
        deps = set()
        for r in reads:
            t = self.last_w.get(r)
            if t is not None:
                deps.add(t)
            if r in self.EXCL:
                for t in self.readers.get(r, ()):
                    deps.add(t)
        for w in writes:
            t = self.last_w.get(w)
            if t is not None:
                deps.add(t)
            for t in self.readers.get(w, ()):
                deps.add(t)
        return deps

    def _commit(self, tok, reads, writes):
        for r in reads:
            self.readers.setdefault(r, []).append(tok)
        for w in writes:
            self.last_w[w] = tok
            self.readers[w] = []

    def _emit_waits(self, eng, deps):
        need = {}
        for (s, v) in deps:
            if s == "pe" and eng == "pe":
                continue
            if v > need.get(s, 0):
                need[s] = v
        for s, v in sorted(need.items()):
            if self.waited[eng].get(s, 0) >= v:
                continue
            self.waited[eng][s] = v
            self.streams[eng].append(("wait", s, v))

    def op(self, eng, fn, reads=(), writes=()):
        deps = self._deps(reads, writes)
        self._emit_waits(eng, deps)
        self.count[eng] += 1
        tok = (eng, self.count[eng])
        self.streams[eng].append(("op", fn, eng, 1))
        self._commit(tok, reads, writes)
        return tok

    NDMA = 24

    def dma(self, fn, sem, reads=(), writes=(), n=1, eng="sp"):
        assert n == 1
        sem = "d%d" % (self.dma_rr % self.NDMA)
        self.dma_rr += 1
        self.sem_names.add(sem)
        deps = self._deps(reads, writes)
        prev = self.dma_count.get(sem, 0)
        if prev:
            deps.add((sem, prev))
        self._emit_waits(eng, deps)
        self.dma_count[sem] = prev + 16
        tok = (sem, self.dma_count[sem])
        self.streams[eng].append(("dma", fn, sem, 1))
        self._commit(tok, reads, writes)
        return tok

    def wait_all(self, eng, toks):
        self._emit_waits(eng, toks)

    def emit(self):
        nc = self.nc
        with contextlib.ExitStack() as es:
            sems = {}
            for s in sorted(self.sem_names):
                sems[s] = es.enter_context(nc.semaphore("s_" + s))
            block = es.enter_context(nc.Block())

            def run(engname, e):
                for item in self.streams[engname]:
                    if item[0] == "wait":
                        e.wait_ge(sems[item[1]], item[2])
                    elif item[0] == "op":
                        ins = item[1](e)
                        ins.then_inc(sems[item[2]], 1)
                    else:
                        lst = item[1](e)
                        if not isinstance(lst, (list, tuple)):
                            lst = [lst]
                        assert len(lst) == item[3]
                        for ins in lst:
                            ins.then_inc(sems[item[2]], 16)

            @block.tensor
            def _(e):
                run("pe", e)

            @block.scalar
            def _(e):
                run("act", e)

            @block.vector
            def _(e):
                run("dve", e)

            @block.gpsimd
            def _(e):
                run("pool", e)

            @block.sync
            def _(e):
                run("sp", e)


def build_program():
    nc = bass.Bass("TRN2", target_bir_lowering=False)
    di = lambda name, shape: nc.dram_tensor(name, shape, F32, kind="ExternalInput").ap()
    do = lambda name, shape: nc.dram_tensor(name, shape, F32, kind="ExternalOutput").ap()
    x_p = di("x_prompt", [NPB, SEQ, D])
    x_s = di("x_sample", [NSB * DEC, D])
    cache_k = di("cache_k", [NSB, 128, 128])
    cache_v = di("cache_v", [NSB, 128, 128])
    state_conv = di("state_conv", [NSB, HALO, 512])
    meta = di("meta_tokens", [NMETA, D])
    g_pre = di("g_pre", [8, 128])
    w_in = di("w_in", [D, D_IN])
    sinks = di("sinks", [1, 8])
    g_att = di("g_att", [1, 512])
    vecs = di("vecs", [34, 512])
    w_pw = di("w_pw", [512, 512])
    w_out = di("w_out", [D, D])
    g_post = di("g_post", [1, D])
    c_ident = di("c_ident", [128, 128])
    c_bprev = di("c_bprev", [128, 1024])
    c_bown = di("c_bown", [128, 1024])

    y_p = do("y_prompt", [NPB, SEQ, D])
    y_s = do("y_sample", [NSB * DEC, D])
    nk_p = do("nk_p", [NPB, 128, 128])
    nv_p = do("nv_p", [NPB, 128, 128])
    nc_p = do("nc_p", [NPB, HALO, 512])
    nk_s = do("nk_s", [NSB, 128, 128])
    nv_s = do("nv_s", [NSB, 128, 128])
    nc_s = do("nc_s", [NSB, HALO, 512])

    S = Sched(nc)
    es = contextlib.ExitStack()
    with es:
        sb = lambda name, shape, dt: es.enter_context(nc.sbuf_tensor(name, shape, dt))
        ps = lambda name, shape, dt: es.enter_context(nc.psum_tensor(name, shape, dt))

        NXS = 3
        NPE = 8
        Win = sb("Win", [128, 8, D_IN], BF16)
        WoA = sb("WoA", [128, 4, D], BF16)
        WoC = sb("WoC", [128, 4, D], BF16)
        Wpw = sb("Wpw", [128, 4, 512], BF16)
        XS = sb("XS", [128, NXS, D], F32)
        XR = sb("XR", [128, 2, D], F32)
        Xs = sb("Xs", [128, D], BF16)
        XT = sb("XT", [128, 8, ST], BF16)
        QT = sb("QT", [128, 4, ST], BF16)
        KT2 = [sb("KT%d" % i, [128, 128 + ST], BF16) for i in range(2)]
        VA = sb("VA", [128, 5, 2, 128], BF16)
        SGA = sb("SGA", [128, 4, ST], BF16)
        SGB = [sb("SGB%d" % i, [128, 4, ST], BF16) for i in range(2)]
        U = sb("U", [128, 4, HALO + ST], BF16)
        UH = sb("UH", [128, 4, HALO], BF16)
        UL = sb("UL", [128, 4, NSB * HALO], F32)
        Dg = sb("Dg", [128, 4 * NPE, 128], BF16)
        TBa = sb("TBa", [128, ST], F32)
        TBb = sb("TBb", [128, ST], F32)
        H2 = [sb("H_%d" % i, [128, 4, ST], F32) for i in range(2)]
        HB = sb("HB", [128, 4, ST], BF16)
        R0 = sb("R0", [128, ST], F32)
        R1 = sb("R1", [128, ST], F32)
        CM = sb("CM", [128, 4, ST], BF16)
        SQ = CM
        P01 = [[sb("P%d_%d" % (i, j), [128, 512], BF16) for j in range(2)] for i in range(2)]
        P2 = [sb("P%d_2" % i, [64, 512], BF16) for i in range(2)]
        RD = sb("RD", [128, 512], F32)
        ATT = sb("ATT", [128, 4, ST], F32)
        AM = [sb("AM%d" % i, [128, 4, ST], BF16) for i in range(3)]
        YS = sb("YS0", [128, D], F32)
        Gpost = sb("Gpost", [128, D], F32)
        Bprev = sb("Bprev", [128, 1024], BF16)
        Bown = sb("Bown", [128, 1024], BF16)
        identb = sb("identb", [128, 128], BF16)
        identf = sb("identf", [128, 128], F32)
        onesm = sb("onesm", [128, 128], BF16)
        PV = sb("PVEC", [128, 4, 34], F32)
        GP = sb("GP", [128, 8], F32)
        GPh = sb("GPh", [128, 8], F32)
        GA = sb("GA", [128, 4], F32)
        sst = sb("sst", [128, 16], F32)
        skt = sb("skt", [1, 8], F32)
        ske = sb("ske", [1, 8], F32)
        vmrow = sb("vmrow", [1, 2, 128], BF16)
        MK2 = [sb("MK%d" % i, [128, NMETA], BF16) for i in range(2)]
        VM = sb("VM", [49, 128], BF16)
        UM = sb("UM", [128, 4, HALO], BF16)
        KC = sb("KC", [128, NSB, 128], BF16)
        VC = sb("VC", [128, NSB, 2, 128], BF16)
        CST = sb("CST", [128, 128], F32)
        VS = VA[0:64, 0:NSB, :, :]
        US = U[:].rearrange("p j t -> p (j t)")[:, 0:4 * NSB * (HALO + DEC)].rearrange("p (j i t) -> p j i t", j=4, i=NSB)
        OST = RD
        VST = RD[0:34, :]
        VST2 = TBb[0:8, 0:128]
        CSB = Xs[:, 0:128]

        pz = [ps("pz%d" % i, [128, 512], F32) for i in range(2)]
        ptr = ps("ptr", [128, 8, 128], BF16)
        pst = ps("pst", [128, 512], F32)
        pS = [ps("pS%d" % i, [128, 512], F32) for i in range(3)]
        pO = ps("pO", [128, 512], F32)

        cnt = {"n": 0}

        def uniq(p):
            cnt["n"] += 1
            return "%s%d" % (p, cnt["n"])

        def dma(out, in_, reads, writes, **kw):
            return S.dma(lambda e: [e.dma_start(out=out, in_=in_, **kw)], None, reads=reads, writes=writes)

        def rsqrt_act(out_ap, in_ap, n, eps, reads, writes, tmp_ap, tmpname):
            S.op("act", lambda e: e.activation(out=tmp_ap, in_=in_ap, func=AF.Ln, bias=eps, scale=1.0 / n),
                 reads=reads, writes=[tmpname])
            S.op("act", lambda e: e.activation(out=out_ap, in_=tmp_ap, func=AF.Exp, scale=-0.5),
                 reads=[tmpname], writes=writes)

        ALLXS = ["XS0", "XS1", "XS1a", "XS1b", "XS2"]

        dma(XS[:, 0, 0:128], c_ident[:, :], [], ["XS0"])
        S.op("dve", lambda e: e.tensor_copy(out=identb[:], in_=XS[:, 0, 0:128]), reads=["XS0"], writes=["identb"])
        S.op("act", lambda e: e.copy(out=identf[:], in_=XS[:, 0, 0:128]), reads=["XS0"], writes=["identf"])
        dma(XS[:, 1, :], c_bprev[:, :], [], ["XS1"])
        S.op("dve", lambda e: e.tensor_copy(out=Bprev[:], in_=XS[:, 1, :]), reads=["XS1"], writes=["Bprev"])
        dma(XS[:, 2, :], c_bown[:, :], [], ["XS2"])
        S.op("dve", lambda e: e.tensor_copy(out=Bown[:], in_=XS[:, 2, :]), reads=["XS2"], writes=["Bown"])
        S.op("pool", lambda e: e.memset(onesm[:], 1.0 / 512.0), writes=["onesm"])
        S.op("pool", lambda e: e.memset(VA[:], 1.0), writes=["VA"])
        S.op("pool", lambda e: e.memset(VC[:], 1.0), writes=["VC"])
        S.op("pool", lambda e: e.memset(VM[:], 1.0), writes=["VM"])
        S.op("pool", lambda e: e.memset(UM[:], 0.0), writes=["UM"])
        for kv in range(2):
            S.op("pool", lambda e, kv=kv: e.memset(KT2[kv][:], 0.0), writes=["KT"])
            S.op("pool", lambda e, kv=kv: e.memset(MK2[kv][:], 0.0), writes=["MK"])
        dma(Gpost[:], g_post[0:1, :].partition_broadcast(128), [], ["Gpost"])
        dma(VST, vecs[:, :], [], ["RD"])
        dma(VST2, g_pre[:, :], [], ["TBb"])
        for kv in range(2):
            dma(GA[kv * 64:(kv + 1) * 64, :],
                g_att[0, kv * 256:(kv + 1) * 256].rearrange("(g d) -> d g", d=64), [], ["GA"],
                allow_slow_non_contiguous=True)
        dma(skt[:], sinks[:, :], [], ["skt"])

        def tr_vec(e):
            for j in range(4):
                e.transpose(pst[:, j * 34:(j + 1) * 34], VST[:, j * 128:(j + 1) * 128], identf[0:34, 0:34])
            return e.transpose(pst[:, 136:144], VST2, identf[0:8, 0:8])
        S.op("pe", tr_vec, reads=["RD", "TBb", "identf"], writes=["pst"])
        S.op("dve", lambda e: e.tensor_copy(out=PV[:].rearrange("p j t -> p (j t)"), in_=pst[:, 0:136]),
             reads=["pst"], writes=["PV"])
        S.op("dve", lambda e: e.tensor_copy(out=GP[:], in_=pst[:, 136:144]), reads=["pst"], writes=["GP"])

        S.op("act", lambda e: e.activation(out=ske[:], in_=skt[:], func=AF.Exp), reads=["skt"], writes=["ske"])
        S.op("pool", lambda e: e.memset(vmrow[:], 0.0), writes=["vmrow"])
        S.op("pool", lambda e: e.memset(vmrow[0:1, 0, 64:128], 1.0), reads=["vmrow"], writes=["vmrow"])
        S.op("pool", lambda e: e.memset(vmrow[0:1, 1, 0:64], 1.0), reads=["vmrow"], writes=["vmrow"])

        def write_sink_rows(nq):
            row = TBa[0:1, :].bitcast(BF16)
            for kv in range(2):
                S.op("dve", lambda e, kv=kv: e.tensor_copy(
                    out=row[0:1, kv * 512:kv * 512 + 4 * nq].rearrange("p (g q) -> p g q", g=4),
                    in_=ske[0:1, kv * 4:(kv + 1) * 4].unsqueeze(2).to_broadcast([1, 4, nq])),
                    reads=["ske", "TBa"], writes=["TBa"])
            for pb in range(2):
                for kv in range(2):
                    dma(P2[pb][kv * 32 + NMETA:kv * 32 + NMETA + 1, 0:4 * nq], row[0:1, kv * 512:kv * 512 + 4 * nq],
                        ["TBa", "P%d_2" % pb], ["P%d_2" % pb])

        write_sink_rows(128)

        def perm_out(k, base):
            return Win[:, k, base:base + 512].rearrange("p (g kv d) -> p kv g d", g=4, kv=2)

        def perm_in(stg, base):
            return stg[:, base:base + 512].rearrange("p (kv g d) -> p kv g d", kv=2, g=4)

        HC = D_IN // 2
        S.op("dve", lambda e: e.tensor_scalar(out=GPh[:], in0=GP[:], scalar1=0.5, scalar2=None, op0=ALU.mult), reads=["GP"], writes=["GP"])
        stgbufs = [
            (H2[0][:].rearrange("p a b -> p (a b)"), ["H0_%d" % j for j in range(4)]),
            (H2[1][:].rearrange("p a b -> p (a b)"), ["H1_%d" % j for j in range(4)]),
            (XR[:].rearrange("p a b -> p (a b)"), ["XR0", "XR1"]),
        ]
        rot = {"n": 0}

        def next_stage():
            b = stgbufs[rot["n"] % len(stgbufs)]
            rot["n"] += 1
            return b

        segs = [(OQ, 512, "perm"), (OK_, 256, "plain"), (OGA, 512, "perm"), (OGB, 512, "plain"), (OB, 512, "plain"), (OA, 512, "half")]

        def thread_W():
            cvt = {"n": 0}
            for (c0, wd, mode) in segs:
                for kh in range(2):
                    flat, names = next_stage()
                    stg = flat[:, 0:4 * wd].rearrange("p (kk c) -> p kk c", kk=4)
                    dma(stg, w_in[kh * 512:(kh + 1) * 512, c0:c0 + wd].rearrange("(kk p) c -> p kk c", p=128), [], names)
                    for kk in range(4):
                        k = kh * 4 + kk
                        sc = GPh[:, k:k + 1] if mode == "half" else GP[:, k:k + 1]
                        if mode == "perm":
                            o_ap = Win[:, k, c0:c0 + 512].rearrange("p (g kv d) -> p kv g d", g=4, kv=2)
                            i_ap = stg[:, kk, :].rearrange("p (kv g d) -> p kv g d", kv=2, g=4)
                        else:
                            o_ap = Win[:, k, c0:c0 + wd]
                            i_ap = stg[:, kk, :]
                        cvt["n"] += 1
                        if cvt["n"] % 2 == 0:
                            S.op("act", lambda e, o_ap=o_ap, i_ap=i_ap, sc=sc: e.activation(out=o_ap, in_=i_ap, func=AF.Copy, scale=sc),
                                 reads=names + ["GP"], writes=["W%d_%d" % (c0, k)])
                        else:
                            S.op("dve", lambda e, o_ap=o_ap, i_ap=i_ap, sc=sc: e.tensor_scalar(out=o_ap, in0=i_ap, scalar1=sc, scalar2=None, op0=ALU.mult),
                                 reads=names + ["GP"], writes=["W%d_%d" % (c0, k)])
                    yield None
                yield ("flag", "W%d" % c0)
            for gp in range(2):
                flat, names = next_stage()
                stgA = flat[:, 0:2 * D].rearrange("p (a b) -> p a b", a=2)
                for kv in range(2):
                    dma(stgA[kv * 64:(kv + 1) * 64, :, :],
                        w_out[kv * 256 + gp * 128:kv * 256 + (gp + 1) * 128, :].rearrange("(g d) n -> d g n", d=64),
                        [], names if kv == 0 else ["XSw"])
                for gl in range(2):
                    g = gp * 2 + gl
                    S.op("act", lambda e, g=g, gl=gl, stgA=stgA: e.activation(out=WoA[:, g, :], in_=stgA[:, gl, :], func=AF.Copy, scale=GA[:, g:g + 1]),
                         reads=names + ["XSw", "GA"], writes=["WoA", "XSw"])
                yield None
            for hh in range(2):
                flat, names = next_stage()
                stgC = flat[:, 0:2 * D].rearrange("p (a b) -> p a b", a=2)
                dma(stgC, w_out[512 + hh * 256:512 + (hh + 1) * 256, :].rearrange("(j p) n -> p j n", p=128), [], names)
                for jl in range(2):
                    j = hh * 2 + jl
                    S.op("dve", lambda e, j=j, jl=jl, stgC=stgC: e.tensor_scalar(out=WoC[:, j, :], in0=stgC[:, jl, :], scalar1=PV[:, j, 33:34], scalar2=None, op0=ALU.mult),
                         reads=names + ["PV"], writes=["WoC"])
                yield None
            flat, names = next_stage()
            stgP = flat[:, 0:4 * 512].rearrange("p (j n) -> p j n", n=512)
            dma(stgP, w_pw.rearrange("(j p) n -> p j n", p=128), [], names)
            S.op("dve", lambda e: e.tensor_copy(out=Wpw[:], in_=stgP), reads=names, writes=["Wpw"])
            yield None
            for j in range(4):
                for tau in range(NPE):
                    eng = "act" if (j * NPE + tau) % 2 == 0 else "dve"
                    if eng == "act":
                        S.op("act", lambda e, j=j, tau=tau: e.activation(out=Dg[:, j * NPE + tau, :], in_=identf[:, :], func=AF.Copy, scale=PV[:, j, tau:tau + 1]),
                             reads=["identf", "PV"], writes=["Dg"])
                    else:
                        S.op("dve", lambda e, j=j, tau=tau: e.tensor_scalar(out=Dg[:, j * NPE + tau, :], in0=identf[:, :], scalar1=PV[:, j, tau:tau + 1], scalar2=None, op0=ALU.mult),
                             reads=["identf", "PV"], writes=["Dg"])
                yield None
            yield ("flag", "W_done")

        def rms_and_transpose(xsrc, xres, T, xt_col0, dst=None, dstname="XT"):
            S.op("act", lambda e: e.activation(out=TBa[0:T, :].bitcast(BF16)[:, 0:D], in_=xsrc, func=AF.Square, accum_out=sst[0:T, 0:1]),
                 reads=xres, writes=["TBa", "sst0"])
            rsqrt_act(sst[0:T, 1:2], sst[0:T, 0:1], float(D), RMS_EPS, ["sst0"], ["sst1"], sst[0:T, 2:3], "sst2")
            S.op("act", lambda e: e.activation(out=Xs[0:T, :], in_=xsrc, func=AF.Copy, scale=sst[0:T, 1:2]),
                 reads=xres + ["sst1"], writes=["Xs"])

            def tr(e):
                for k in range(8):
                    r = e.transpose(ptr[:, k, 0:T], Xs[0:T, k * 128:(k + 1) * 128], identb[0:T, 0:T])
                return r
            S.op("pe", tr, reads=["Xs", "identb"], writes=["ptr"])
            d_ap = XT[:, :, xt_col0:xt_col0 + T] if dst is None else dst
            S.op("dve", lambda e: e.tensor_copy(out=d_ap, in_=ptr[:, :, 0:T]), reads=["ptr"], writes=[dstname])

        pa = [pS[0], pS[1], pS[2], pO]
        pan = ["pS0", "pS1", "pS2", "pO"]

        def wnames(col0):
            for (c0, wd, _m) in segs:
                if c0 <= col0 < c0 + wd:
                    return ["W%d_%d" % (c0, k) for k in range(8)]
            raise AssertionError(col0)

        def fm_chunk(col0, T, bank, xt=None, xtname="XT"):
            def mm(e):
                for k in range(8):
                    r = e.matmul(pa[bank][:, 0:T], lhsT=Win[:, k, col0:col0 + 128], rhs=(XT if xt is None else xt)[:, k, 0:T], start=(k == 0), stop=(k == 7))
                return r
            S.op("pe", mm, reads=wnames(col0) + [xtname], writes=[pan[bank]])

        def tm_matmul(col0, tok0, ntok, xt=None, xtname="XT"):
            def mm(e):
                for k in range(8):
                    r = e.matmul(pst[0:ntok, 0:128], lhsT=(XT if xt is None else xt)[:, k, tok0:tok0 + ntok], rhs=Win[:, k, col0:col0 + 128],
                                 start=(k == 0), stop=(k == 7))
                return r
            S.op("pe", mm, reads=wnames(col0) + [xtname], writes=["pst"])

        def v_aug_copy(e, tile_ap, src_ap):
            e.copy(out=tile_ap[:, 0, 0:64], in_=src_ap[:, 0:64])
            return e.copy(out=tile_ap[:, 1, 64:128], in_=src_ap[:, 64:128])

        def attention_unit(nq, qcol0, kprev, vprev, kown, vown, nown, pbuf, full):
            N = 4 * nq
            Pp, Po = P01[pbuf]
            Pm = P2[pbuf]
            for kv in range(2):
                rows = slice(kv * 64, (kv + 1) * 64)
                mrows = slice(kv * 32, kv * 32 + NMETA)
                mrows1 = slice(kv * 32, kv * 32 + NMETA + 1)
                qrhs = QT[:, :, qcol0:qcol0 + nq] if full else QT[rows, :, qcol0:qcol0 + nq]
                mk = MK2[kv][:, :] if full else MK2[kv][rows, :]
                bp = Bprev[:, kv * 512:(kv + 1) * 512].rearrange("p (g q) -> p g q", g=4)[:, :, 0:nq]
                bo = Bown[:, kv * 512:(kv + 1) * 512].rearrange("p (g q) -> p g q", g=4)[:, :, 0:nq]

                def qk(e, kv=kv, qrhs=qrhs, bp=bp, bo=bo, mrows=mrows, mk=mk):
                    if kprev is not None:
                        e.matmul(pS[0][:, 0:N], lhsT=kprev(kv), rhs=qrhs, start=True, stop=False)
                        e.matmul(pS[0][:, 0:N], lhsT=identb[:, :], rhs=bp, start=False, stop=True)
                    e.matmul(pS[1][0:nown, 0:N], lhsT=kown(kv), rhs=qrhs, start=True, stop=False)
                    e.matmul(pS[1][0:nown, 0:N], lhsT=identb[:, 0:nown], rhs=bo, start=False, stop=True)
                    return e.matmul(pS[2][mrows, 0:N], lhsT=mk, rhs=qrhs, start=True, stop=True)
                S.op("pe", qk, reads=["QT", "KT", "KC", "MK", "Bprev", "Bown", "identb"], writes=["pS0", "pS1", "pS2"])
                if kprev is not None:
                    S.op("act", lambda e: e.activation(out=Pp[:, 0:N], in_=pS[0][:, 0:N], func=AF.Exp, scale=SCALE),
                         reads=["pS0"], writes=["P%d_0" % pbuf])
                S.op("act", lambda e: e.activation(out=Po[0:nown, 0:N], in_=pS[1][0:nown, 0:N], func=AF.Exp, scale=SCALE),
                     reads=["pS1"], writes=["P%d_1" % pbuf])
                S.op("act", lambda e, mrows=mrows: e.activation(out=Pm[mrows, 0:N], in_=pS[2][mrows, 0:N], func=AF.Exp, scale=SCALE),
                     reads=["pS2"], writes=["P%d_2" % pbuf])

                def pv(e, kv=kv, mrows1=mrows1):
                    first = True
                    if kprev is not None:
                        e.matmul(pO[:, 0:N], lhsT=vprev(kv), rhs=Pp[:, 0:N], start=True, stop=False)
                        first = False
                    e.matmul(pO[:, 0:N], lhsT=vown(kv), rhs=Po[0:nown, 0:N], start=first, stop=False)
                    return e.matmul(pO[:, 0:N], lhsT=VM[mrows1, :], rhs=Pm[mrows1, 0:N], start=False, stop=True)
                S.op("pe", pv, reads=["VA", "VC", "VM", "P%d_0" % pbuf, "P%d_1" % pbuf, "P%d_2" % pbuf], writes=["pO"])
                num = slice(0, 64) if kv == 0 else slice(64, 128)
                den = slice(64, 128) if kv == 0 else slice(0, 64)
                S.op("act", lambda e, num=num, den=den: e.activation(out=RD[num, 0:N], in_=pO[den, 0:N], func=AF.Ln), reads=["pO"], writes=["RD"])
                S.op("act", lambda e, num=num: e.activation(out=RD[num, 0:N], in_=RD[num, 0:N], func=AF.Exp, scale=-1.0), reads=["RD"], writes=["RD"])
                yield None
                S.op("dve", lambda e, num=num: e.tensor_tensor(
                    out=ATT[num, :, qcol0:qcol0 + nq], in0=pO[num, 0:N].rearrange("p (g q) -> p g q", g=4),
                    in1=RD[num, 0:N].rearrange("p (g q) -> p g q", g=4), op=ALU.mult),
                    reads=["pO", "RD"], writes=["ATT"])
                yield None

        def att_finish(T, par):
            A = AM[par]
            an = "AM%d" % par
            S.op("act", lambda e: e.activation(out=A[:, :, 0:T], in_=ATT[:, :, 0:T], func=AF.Square), reads=["ATT"], writes=[an])

            def stats(e):
                for j in range(4):
                    r = e.matmul(pS[0][:, 0:T], lhsT=onesm[:, :], rhs=A[:, j, 0:T], start=(j == 0), stop=(j == 3))
                return r
            S.op("pe", stats, reads=[an, "onesm"], writes=["pS0"])
            S.op("act", lambda e: e.activation(out=TBa[:, 0:T], in_=pS[0][:, 0:T], func=AF.Ln, bias=RMS_EPS, scale=1.0), reads=["pS0"], writes=["TBa"])
            S.op("act", lambda e: e.activation(out=RD[:, 0:T], in_=TBa[:, 0:T], func=AF.Exp, scale=-0.5), reads=["TBa"], writes=["RD"])
            for g in range(4):
                S.op("dve", lambda e, g=g: e.tensor_tensor(out=ATT[:, g, 0:T], in0=ATT[:, g, 0:T], in1=SGA[:, g, 0:T], op=ALU.mult),
                     reads=["ATT", "SGA"], writes=["ATT"])
                S.op("pool", lambda e, g=g: e.tensor_tensor(out=A[:, g, 0:T], in0=ATT[:, g, 0:T], in1=RD[:, 0:T], op=ALU.mult),
                     reads=["ATT", "RD"], writes=[an])

        xr_state = {"n": 0}

        def out_tile(tok0, ntok, par, xsrc_dram, ydst):
            A = AM[par]
            h = xr_state["n"]
            xr_state["n"] += 1
            xs = h % 2
            xn_ = "XR%d" % xs
            dma(XR[0:ntok, xs, :], xsrc_dram, [], [xn_])

            def mm(e):
                for half in range(2):
                    for g in range(4):
                        e.matmul(pz[half][0:ntok, :], lhsT=A[:, g, tok0:tok0 + ntok], rhs=WoA[:, g, half * 512:(half + 1) * 512],
                                 start=(g == 0), stop=False)
                    for j in range(4):
                        r = e.matmul(pz[half][0:ntok, :], lhsT=CM[:, j, tok0:tok0 + ntok], rhs=WoC[:, j, half * 512:(half + 1) * 512],
                                     start=False, stop=(j == 3))
                return r
            S.op("pe", mm, reads=["AM%d" % par, "SQ0", "SQ1", "SQ2", "SQ3", "WoA", "WoC"], writes=["pz0", "pz1"])
            for half in range(2):
                S.op("act", lambda e, half=half: e.activation(out=TBb[0:ntok, :].bitcast(BF16)[:, 0:512], in_=pz[half][0:ntok, :], func=AF.Square,
                                                              accum_out=sst[0:ntok, 4 + half:5 + half]),
                     reads=["pz%d" % half], writes=["TBb", "sst%d" % (4 + half)])
            S.op("pool", lambda e: e.tensor_tensor(out=sst[0:ntok, 6:7], in0=sst[0:ntok, 4:5], in1=sst[0:ntok, 5:6], op=ALU.add),
                 reads=["sst4", "sst5"], writes=["sst6"])
            rsqrt_act(sst[0:ntok, 7:8], sst[0:ntok, 6:7], float(D), RMS_EPS, ["sst6"], ["sst7"], sst[0:ntok, 8:9], "sst8")
            for half in range(2):
                S.op("act", lambda e, half=half: e.activation(out=YS[0:ntok, half * 512:(half + 1) * 512], in_=pz[half][0:ntok, :],
                                                              func=AF.Copy, scale=sst[0:ntok, 7:8]),
                     reads=["pz%d" % half, "sst7"], writes=["YS0"])
            S.op("pool", lambda e: e.tensor_tensor(out=YS[0:ntok, :], in0=YS[0:ntok, :], in1=Gpost[0:ntok, :], op=ALU.mult),
                 reads=["YS0", "Gpost"], writes=["YS0"])
            yield None
            yield None
            S.op("dve", lambda e: e.tensor_tensor(out=XR[0:ntok, xs, :], in0=YS[0:ntok, :], in1=XR[0:ntok, xs, :], op=ALU.add),
                 reads=["YS0", xn_], writes=[xn_])
            dma(ydst, XR[0:ntok, xs, :], [xn_], [uniq("yout")])
            yield None

        XTm = KT2[0][:, 0:128].rearrange("p (k t) -> p k t", k=8)
        dma(XS[0:NMETA, 0, :], meta[:, :], [], ["XS0"])
        rms_and_transpose(XS[0:NMETA, 0, :], ["XS0"], NMETA, 0, dst=XTm[:, :, 0:NMETA], dstname="KT")
        def meta_kv():
            fm_chunk(OK_, NMETA, 0, xt=XTm, xtname="KT")

            def mk_copy(e):
                e.tensor_copy(out=MK2[0][0:64, :], in_=pa[0][0:64, 0:NMETA])
                return e.tensor_copy(out=MK2[1][64:128, :], in_=pa[0][64:128, 0:NMETA])
            S.op("dve", mk_copy, reads=[pan[0]], writes=["MK"])
            tm_matmul(OV, 0, NMETA, xt=XTm, xtname="KT")

            def vm_copy(e):
                e.copy(out=VM[0:NMETA, 0:64], in_=pst[0:NMETA, 0:64])
                return e.copy(out=VM[32:32 + NMETA, 64:128], in_=pst[0:NMETA, 64:128])
            S.op("act", vm_copy, reads=["pst"], writes=["VM"])
            dma(VM[NMETA:NMETA + 1, :], vmrow[0:1, 0, :], ["vmrow", "VM"], ["VM"])
            dma(VM[32 + NMETA:32 + NMETA + 1, :], vmrow[0:1, 1, :], ["vmrow", "VM"], ["VM"])


        def meta_glu():
            for j in range(4):
                fm_chunk(OB + j * 128, NMETA, 0, xt=XTm, xtname="KT")
                fm_chunk(OA + j * 128, NMETA, 1, xt=XTm, xtname="KT")
                S.op("act", lambda e: e.activation(out=TBa[:, 0:NMETA], in_=pa[0][:, 0:NMETA], func=AF.Tanh, scale=0.5), reads=[pan[0]], writes=["TBa"])
                S.op("dve", lambda e, j=j: e.scalar_tensor_tensor(out=UM[:, j, HALO - NMETA:HALO], in0=TBa[:, 0:NMETA], scalar=1.0, in1=pa[1][:, 0:NMETA],
                                                              op0=ALU.add, op1=ALU.mult),
                     reads=["TBa", pan[1]], writes=["UM"])

        for i in range(NSB):
            dma(CST[:], cache_k[i, :, :], [], ["CST"])
            S.op("dve", lambda e: e.tensor_copy(out=CSB[:], in_=CST[:]), reads=["CST"], writes=["Xs"])
            S.op("pe", lambda e: e.transpose(ptr[:, 0, :], CSB[:, :], identb[:, :]), reads=["Xs", "identb"], writes=["ptr"])
            S.op("act", lambda e, i=i: e.copy(out=KC[:, i, :], in_=ptr[:, 0, :]), reads=["ptr"], writes=["KC"])
            dma(CST[:], cache_v[i, :, :], [], ["CST"])
            S.op("act", lambda e, i=i: v_aug_copy(e, VC[:, i, :, :], CST[:, :]), reads=["CST"], writes=["VC"])
            dma(nk_s[i, 0:64, :], cache_k[i, 64:128, :], [], [uniq("o")])
            dma(nv_s[i, 0:64, :], cache_v[i, 64:128, :], [], [uniq("o")])

        nst = SEQ // ST
        units = []
        for b in range(NPB):
            for s_ in range(nst):
                units.append(dict(kind="p", b=b, s=s_, first=(s_ == 0), last=(s_ == nst - 1), T=ST, ntiles=4))
        units.append(dict(kind="s", T=NSB * DEC, ntiles=2, first=True, last=True))
        for i, u in enumerate(units):
            u["par"] = i % 2
            u["i"] = i

        tiles = []
        for u in units:
            for t in range(u["ntiles"]):
                if u["kind"] == "p":
                    tiles.append(x_p[u["b"], u["s"] * ST + t * 128:u["s"] * ST + (t + 1) * 128, :])
                else:
                    tiles.append(x_s[t * 128:(t + 1) * 128, :])
        ring = {"issued": 0}

        def ensure_loaded(gidx, ahead=2):
            while ring["issued"] < min(len(tiles), gidx + 1 + ahead):
                g = ring["issued"]
                sl = g % NXS
                dma(XS[:, sl, :], tiles[g], [], ["XS%d" % sl] + (["XS1a", "XS1b"] if sl == 1 else []))
                ring["issued"] += 1

        gt = {"n": 0}

        def thread_A(u):
            T = u["T"]
            par = u["par"]
            i = u["i"]
            p = (u["kind"] == "p")
            if p and not u["first"]:
                for kv in range(2):
                    S.op("pool", lambda e, kv=kv: e.tensor_copy(out=KT2[kv][:, 0:128], in_=KT2[kv][:, ST:ST + 128]), reads=["KT"], writes=["KT"])
                S.op("pool", lambda e: e.tensor_copy(out=VA[:, 0, :, :], in_=VA[:, 4, :, :]), reads=["VA"], writes=["VA"])
            for t in range(u["ntiles"]):
                g = gt["n"]
                gt["n"] += 1
                ensure_loaded(g, ahead=1)
                rms_and_transpose(XS[:, g % NXS, :], ["XS%d" % (g % NXS)], 128, t * 128)
                yield None
            if i == 0:
                yield ("need", "W%d" % OQ)
            for g in range(4):
                fm_chunk(OQ + g * 128, T, g)
                S.op("act", lambda e, g=g: e.copy(out=QT[:, g, 0:T], in_=pa[g][:, 0:T]), reads=[pan[g]], writes=["QT"])
                yield None
            if i == 0:
                yield ("need", "W%d" % OK_)
                meta_kv()
            fm_chunk(OK_, T, 0)
            def k_copy(e):
                e.copy(out=KT2[0][0:64, 128:128 + T], in_=pa[0][0:64, 0:T])
                return e.copy(out=KT2[1][64:128, 128:128 + T], in_=pa[0][64:128, 0:T])
            S.op("act", k_copy, reads=[pan[0]], writes=["KT"])
            yield None
            if p:
                for t in range(4):
                    tm_matmul(OV, t * 128, 128)
                    S.op("act", lambda e, t=t: v_aug_copy(e, VA[:, 1 + t, :, :], pst[:, 0:128]), reads=["pst"], writes=["VA"])
                    if u["last"] and t == 3:
                        S.op("act", lambda e: e.copy(out=CST[:, :], in_=pst[:, 0:128]), reads=["pst"], writes=["CST"])
                        dma(nv_p[u["b"], :, :], CST[:, :], ["CST"], [uniq("o")])
                        tm_matmul(OK_, t * 128, 128)
                        S.op("act", lambda e: e.copy(out=CST[:, :], in_=pst[:, 0:128]), reads=["pst"], writes=["CST"])
                        dma(nk_p[u["b"], :, :], CST[:, :], ["CST"], [uniq("o")])
                    yield None
            else:
                for q in range(NSB):
                    tm_matmul(OV, q * DEC, DEC)
                    S.op("act", lambda e, q=q: v_aug_copy(e, VS[:, q, :, :], pst[0:DEC, 0:128]), reads=["pst"], writes=["VA"])
                    S.op("act", lambda e: e.copy(out=CST[0:DEC, :], in_=pst[0:DEC, 0:128]), reads=["pst"], writes=["CST"])
                    dma(nv_s[q, 64:128, :], CST[0:DEC, :], ["CST"], [uniq("o")])
                    tm_matmul(OK_, q * DEC, DEC)
                    S.op("act", lambda e: e.copy(out=CST[0:DEC, :], in_=pst[0:DEC, 0:128]), reads=["pst"], writes=["CST"])
                    dma(nk_s[q, 64:128, :], CST[0:DEC, :], ["CST"], [uniq("o")])
                    yield None
            if i == 0:
                yield ("need", "W%d" % OGA)
            for g in range(4):
                fm_chunk(OGA + g * 128, T, g)
                S.op("act", lambda e, g=g: e.activation(out=SGA[:, g, 0:T], in_=pa[g][:, 0:T], func=AF.Silu), reads=[pan[g]], writes=["SGA"])
                yield None
            if i >= 2:
                yield ("need", "gate_done%d" % (i - 2))
            if i == 0:
                yield ("need", "W%d" % OGB)
            for j in range(4):
                fm_chunk(OGB + j * 128, T, j)
                S.op("act", lambda e, j=j: e.activation(out=SGB[par][:, j, 0:T], in_=pa[j][:, 0:T], func=AF.Silu),
                     reads=[pan[j]], writes=["SGB%d" % par])
                yield None
            if p:
                for t in range(4):
                    has_prev = not (u["first"] and t == 0)
                    yield from attention_unit(
                        128, t * 128,
                        (lambda kv, t=t: KT2[kv][:, t * 128:(t + 1) * 128]) if has_prev else None,
                        (lambda kv, t=t: VA[:, t, kv, :]),
                        (lambda kv, t=t: KT2[kv][:, 128 + t * 128:128 + (t + 1) * 128]),
                        (lambda kv, t=t: VA[:, 1 + t, kv, :]),
                        128, t % 2, True)
            else:
                write_sink_rows(DEC)
                for q in range(NSB):
                    yield from attention_unit(
                        DEC, q * DEC,
                        (lambda kv, q=q: KC[kv * 64:(kv + 1) * 64, q, :]),
                        (lambda kv, q=q: VC[:, q, kv, :]),
                        (lambda kv, q=q: KT2[kv][kv * 64:(kv + 1) * 64, 128 + q * DEC:128 + (q + 1) * DEC]),
                        (lambda kv, q=q: VS[:, q, kv, :]),
                        DEC, q % 2, False)
            if i >= 3:
                yield ("need", "out_done%d" % (i - 3))
            att_finish(T, i % 3)
            yield None
            if i >= 1:
                yield ("need", "conv_done%d" % (i - 1))
            if i == 0:
                yield ("need", "W%d" % OB)
                yield ("need", "W%d" % OA)
                meta_glu()
            if p:
                src = UM if u["first"] else UH
                S.op("pool", lambda e: e.tensor_copy(out=U[:, :, 0:HALO], in_=src[:, :, :]), reads=["UM", "UH", "U"], writes=["U"])
            else:
                for q in range(NSB):
                    dma(OST[0:HALO, :], state_conv[q, :, :], [], ["RD"])

                    def trs(e):
                        for j in range(4):
                            r = e.transpose(pst[:, j * 32:j * 32 + HALO], OST[0:HALO, j * 128:(j + 1) * 128], identf[0:HALO, 0:HALO])
                        return r
                    S.op("pe", trs, reads=["RD", "identf"], writes=["pst"])
                    S.op("act", lambda e, q=q: e.copy(out=US[:, :, q, 0:HALO], in_=pst[:, 0:128].rearrange("p (j t) -> p j t", t=32)[:, :, 0:HALO]),
                         reads=["pst", "U"], writes=["U"])
                yield None
            for j in range(4):
                bb = 2 * (j % 2)
                fm_chunk(OB + j * 128, T, bb)
                fm_chunk(OA + j * 128, T, bb + 1)
                S.op("act", lambda e, bb=bb: e.activation(out=TBa[:, 0:T], in_=pa[bb][:, 0:T], func=AF.Tanh, scale=0.5), reads=[pan[bb]], writes=["TBa"])
                if p:
                    S.op("dve", lambda e, j=j, bb=bb: e.scalar_tensor_tensor(out=U[:, j, HALO:HALO + ST], in0=TBa[:, :], scalar=1.0, in1=pa[bb + 1][:, :],
                                                                  op0=ALU.add, op1=ALU.mult),
                         reads=["TBa", pan[bb + 1], "U"], writes=["U"])
                    if u["last"]:
                        S.op("dve", lambda e, j=j, bb=bb: e.scalar_tensor_tensor(out=UL[:, j, 0:HALO], in0=TBa[:, ST - HALO:ST], scalar=1.0,
                                                                      in1=pa[bb + 1][:, ST - HALO:ST], op0=ALU.add, op1=ALU.mult),
                             reads=["TBa", pan[bb + 1]], writes=["UL"])
                else:
                    S.op("dve", lambda e, j=j, bb=bb: e.scalar_tensor_tensor(
                        out=US[:, j, :, HALO:HALO + DEC], in0=TBa[:, 0:T].rearrange("p (q t) -> p q t", q=NSB), scalar=1.0,
                        in1=pa[bb + 1][:, 0:T].rearrange("p (q t) -> p q t", q=NSB), op0=ALU.add, op1=ALU.mult),
                        reads=["TBa", pan[bb + 1], "U"], writes=["U"])
                    S.op("dve", lambda e, j=j, bb=bb: e.scalar_tensor_tensor(
                        out=UL[:, j, :].rearrange("p (q t) -> p q t", q=NSB),
                        in0=TBa[:, 0:T].rearrange("p (q t) -> p q t", q=NSB)[:, :, DEC - HALO:DEC], scalar=1.0,
                        in1=pa[bb + 1][:, 0:T].rearrange("p (q t) -> p q t", q=NSB)[:, :, DEC - HALO:DEC], op0=ALU.add, op1=ALU.mult),
                        reads=["TBa", pan[bb + 1]], writes=["UL"])
                yield None
            if p and not u["last"]:
                S.op("pool", lambda e: e.tensor_copy(out=UH[:, :, :], in_=U[:, :, ST:ST + HALO]), reads=["U"], writes=["UH"])
            if p and u["last"]:
                def tru(e):
                    for j in range(4):
                        r = e.transpose(pst[0:HALO, j * 128:(j + 1) * 128], UL[:, j, 0:HALO], identf[:, :])
                    return r
                S.op("pe", tru, reads=["UL", "identf"], writes=["pst"])
                S.op("act", lambda e: e.copy(out=OST[0:HALO, :], in_=pst[0:HALO, :]), reads=["pst"], writes=["RD"])
                dma(nc_p[u["b"], :, :], OST[0:HALO, :], ["RD"], [uniq("o")])
            if not p:
                for q in range(NSB):
                    def tru(e, q=q):
                        for j in range(4):
                            r = e.transpose(pst[0:HALO, j * 128:(j + 1) * 128], UL[:, j, q * HALO:(q + 1) * HALO], identf[:, :])
                        return r
                    S.op("pe", tru, reads=["UL", "identf"], writes=["pst"])
                    S.op("act", lambda e: e.copy(out=OST[0:HALO, :], in_=pst[0:HALO, :]), reads=["pst"], writes=["RD"])
                    dma(nc_s[q, :, :], OST[0:HALO, :], ["RD"], [uniq("o")])
            yield None

        def conv_gen(u):
            T = u["T"]
            p = (u["kind"] == "p")
            H = H2[u["par"]]
            hn = lambda j: "H%d_%d" % (u["par"], j)
            if p:
                u_of = lambda j, tau: U[:, j, tau:tau + ST]
                h_of = lambda buf, j: buf[:, j, 0:ST]
                c_of = lambda j: pst[:, 0:ST]
            else:
                u_of = lambda j, tau: US[:, j, :, tau:tau + DEC]
                h_of = lambda buf, j: buf[:, j, 0:T].rearrange("p (q t) -> p q t", q=NSB)
                c_of = lambda j: pst[:, 0:T].rearrange("p (q t) -> p q t", q=NSB)
            for j in range(4):
                def cmm(e, j=j):
                    for tau in range(NPE):
                        r = e.matmul(pst[:, 0:T], lhsT=Dg[:, j * NPE + tau, :], rhs=u_of(j, tau), start=(tau == 0), stop=(tau == NPE - 1))
                    return r
                S.op("pe", cmm, reads=["U", "Dg"], writes=["pst"])
                S.op("dve", lambda e, j=j: e.scalar_tensor_tensor(
                    out=h_of(H, j), in0=u_of(j, NPE), scalar=PV[:, j, NPE:NPE + 1], in1=c_of(j), op0=ALU.mult, op1=ALU.add),
                    reads=["U", "PV", "pst"], writes=[hn(j)])
                for tau in range(NPE + 1, CW):
                    S.op("dve", lambda e, j=j, tau=tau: e.scalar_tensor_tensor(
                        out=h_of(H, j), in0=u_of(j, tau), scalar=PV[:, j, tau:tau + 1], in1=h_of(H, j), op0=ALU.mult, op1=ALU.add),
                        reads=["U", "PV", hn(j)], writes=[hn(j)])
                    if tau % 3 == 0:
                        yield None
                yield None
            yield ("flag", "conv_done%d" % u["i"])

        def back_gen(u):
            T = u["T"]
            par = u["par"]
            i = u["i"]
            hb = ["HB%d" % j for j in range(4)]
            sq = ["SQ%d" % j for j in range(4)]
            H = H2[par]
            hn = lambda j: "H%d_%d" % (par, j)
            for j in range(4):
                S.op("act", lambda e, j=j: e.copy(out=HB[:, j, 0:T], in_=H[:, j, 0:T]), reads=[hn(j)], writes=["HB%d" % j])
                S.op("act", lambda e, j=j: e.activation(out=SQ[:, j, 0:T], in_=H[:, j, 0:T], func=AF.Square), reads=[hn(j)], writes=["SQ%d" % j])
            yield None

            def stats(e):
                for j in range(4):
                    e.matmul(pz[0][:, 0:T], lhsT=onesm[:, :], rhs=HB[:, j, 0:T], start=(j == 0), stop=(j == 3))
                for j in range(4):
                    r = e.matmul(pz[1][:, 0:T], lhsT=onesm[:, :], rhs=SQ[:, j, 0:T], start=(j == 0), stop=(j == 3))
                return r
            S.op("pe", stats, reads=hb + sq + ["onesm"], writes=["pz0", "pz1"])
            S.op("act", lambda e: e.copy(out=R0[:, 0:T], in_=pz[0][:, 0:T]), reads=["pz0"], writes=["R0"])
            S.op("pool", lambda e: e.tensor_tensor(out=TBb[:, 0:T], in0=R0[:, 0:T], in1=R0[:, 0:T], op=ALU.mult), reads=["R0"], writes=["TBb"])
            S.op("act", lambda e: e.copy(out=R1[:, 0:T], in_=pz[1][:, 0:T]), reads=["pz1"], writes=["R1"])
            S.op("pool", lambda e: e.tensor_tensor(out=TBb[:, 0:T], in0=R1[:, 0:T], in1=TBb[:, 0:T], op=ALU.subtract), reads=["R1", "TBb"], writes=["TBb"])
            S.op("act", lambda e: e.activation(out=TBb[:, 0:T], in_=TBb[:, 0:T], func=AF.Ln, bias=LN_EPS, scale=1.0), reads=["TBb"], writes=["TBb"])
            S.op("act", lambda e: e.activation(out=R1[:, 0:T], in_=TBb[:, 0:T], func=AF.Exp, scale=-0.5), reads=["TBb"], writes=["R1"])
            yield None
            for j in range(4):
                S.op("dve", lambda e, j=j: e.tensor_tensor(out=H[:, j, 0:T], in0=H[:, j, 0:T], in1=R0[:, 0:T], op=ALU.subtract),
                     reads=[hn(j), "R0"], writes=[hn(j)])
                S.op("pool", lambda e, j=j: e.tensor_tensor(out=H[:, j, 0:T], in0=H[:, j, 0:T], in1=R1[:, 0:T], op=ALU.mult),
                     reads=[hn(j), "R1"], writes=[hn(j)])
                S.op("act", lambda e, j=j: e.activation(out=HB[:, j, 0:T], in_=H[:, j, 0:T], func=AF.Silu, bias=PV[:, j, 32:33], scale=PV[:, j, 31:32]),
                     reads=[hn(j), "PV"], writes=["HB%d" % j])
                yield None
            for jj in range(4):
                bank = jj % 2

                def mm(e, jj=jj, bank=bank):
                    for j in range(4):
                        r = e.matmul(pz[bank][:, 0:T], lhsT=Wpw[:, j, jj * 128:(jj + 1) * 128], rhs=HB[:, j, 0:T], start=(j == 0), stop=(j == 3))
                    return r
                S.op("pe", mm, reads=hb + ["Wpw"], writes=["pz%d" % bank])
                S.op("act", lambda e, jj=jj, bank=bank: e.activation(out=SQ[:, jj, 0:T], in_=pz[bank][:, 0:T], func=AF.Square),
                     reads=["pz%d" % bank], writes=["SQ%d" % jj])
                S.op("act", lambda e, jj=jj, bank=bank: e.copy(out=H[:, jj, 0:T], in_=pz[bank][:, 0:T]), reads=["pz%d" % bank], writes=[hn(jj)])
                S.op("pool", lambda e, jj=jj: e.tensor_tensor(out=H[:, jj, 0:T], in0=H[:, jj, 0:T], in1=SGB[par][:, jj, 0:T], op=ALU.mult),
                     reads=[hn(jj), "SGB%d" % par], writes=[hn(jj)])
                yield None
            yield ("flag", "gate_done%d" % i)

            def stats2(e):
                for j in range(4):
                    r = e.matmul(pz[0][:, 0:T], lhsT=onesm[:, :], rhs=SQ[:, j, 0:T], start=(j == 0), stop=(j == 3))
                return r
            S.op("pe", stats2, reads=sq + ["onesm"], writes=["pz0"])
            S.op("act", lambda e: e.activation(out=TBb[:, 0:T], in_=pz[0][:, 0:T], func=AF.Ln, bias=RMS_EPS, scale=1.0), reads=["pz0"], writes=["TBb"])
            S.op("act", lambda e: e.activation(out=R1[:, 0:T], in_=TBb[:, 0:T], func=AF.Exp, scale=-0.5), reads=["TBb"], writes=["R1"])
            for jj in range(4):
                S.op("pool", lambda e, jj=jj: e.tensor_tensor(out=CM[:, jj, 0:T], in0=H[:, jj, 0:T], in1=R1[:, 0:T], op=ALU.mult),
                     reads=[hn(jj), "R1"], writes=["SQ%d" % jj])
            yield None

        def out_gen(u):
            par = u["i"] % 3
            p = (u["kind"] == "p")
            for t in range(u["ntiles"]):
                if p:
                    r0 = u["s"] * ST + t * 128
                    yield from out_tile(t * 128, 128, par, x_p[u["b"], r0:r0 + 128, :], y_p[u["b"], r0:r0 + 128, :])
                else:
                    yield from out_tile(t * 128, 128, par, x_s[t * 128:(t + 1) * 128, :], y_s[t * 128:(t + 1) * 128, :])

        def interleave(g1, g2, n1=2):
            d1 = g1 is None
            d2 = g2 is None
            while not (d1 and d2):
                for _ in range(n1):
                    if not d1:
                        try:
                            yield next(g1)
                        except StopIteration:
                            d1 = True
                if not d2:
                    try:
                        yield next(g2)
                    except StopIteration:
                        d2 = True

        def A_all():
            for u in units:
                yield from thread_A(u)
                yield ("flag", "A_done%d" % u["i"])

        def chain(*gs):
            for g in gs:
                yield from g

        def one(x):
            yield x

        def B_all():
            prev = None
            for u in units:
                i = u["i"]
                yield ("need", "A_done%d" % i)
                if i == 0:
                    yield ("need", "W_done")
                yield from interleave(conv_gen(u), prev, n1=2)
                prev = chain(back_gen(u), out_gen(u), one(("flag", "out_done%d" % i)))
            yield from prev

        def run_threads(gens, weights):
            flags = set()
            st = [dict(gen=g, wait=None, done=False) for g in gens]
            order = []
            for t, w in zip(st, weights):
                order += [t] * w
            while not all(t["done"] for t in st):
                progressed = False
                for t in order:
                    if t["done"]:
                        continue
                    if t["wait"] is not None:
                        if t["wait"] not in flags:
                            continue
                        t["wait"] = None
                    try:
                        r = next(t["gen"])
                    except StopIteration:
                        t["done"] = True
                        progressed = True
                        continue
                    progressed = True
                    if isinstance(r, tuple):
                        if r[0] == "flag":
                            flags.add(r[1])
                        elif r[0] == "need" and r[1] not in flags:
                            t["wait"] = r[1]
                assert progressed, "schedule deadlock: " + str([t["wait"] for t in st])

        run_threads([thread_W(), B_all(), A_all()], [1, 1, 1])

        S.wait_all("sp", list(S.dma_count.items()))
        S.emit()
    return nc


_CACHE = {}


def _consts():
    h = np.arange(1, 9, dtype=np.float64)
    slopes = (2.0 ** (-h)).reshape(2, 4)
    j = np.arange(128)[:, None]
    i = np.arange(128)[None, :]
    NEG = -30000.0
    bprev = np.zeros((128, 2, 4, 128), np.float32)
    bown = np.zeros((128, 2, 4, 128), np.float32)
    dprev = (i + 128 - j).astype(np.float64)
    mprev = (i >= 64) & (j < 64)
    down = np.abs(i - j).astype(np.float64)
    mown = (i < 64) & (j >= 64)
    for kv in range(2):
        for g in range(4):
            bp = -slopes[kv, g] * dprev / SCALE
            bo = -slopes[kv, g] * down / SCALE
            bprev[:, kv, g, :] = np.where(mprev, NEG, bp)
            bown[:, kv, g, :] = np.where(mown, NEG, bo)
    return (np.eye(128, dtype=np.float32), bprev.reshape(128, 1024), bown.reshape(128, 1024))


def kernel(x_prompt, x_sample, cache_k, cache_v, state_conv, meta_tokens, g_pre, w_in,
           sinks, g_att, conv_w, ln_g, ln_b, w_pw, g_conv, w_out, g_post):
    f = lambda a: np.ascontiguousarray(np.asarray(a, dtype=np.float32))
    x_prompt, x_sample, cache_k, cache_v, state_conv = map(f, (x_prompt, x_sample, cache_k, cache_v, state_conv))
    if "nc" not in _CACHE:
        _CACHE["nc"] = build_program()
    nc = _CACHE["nc"]
    ident, bprev, bown = _consts()
    vecs = np.concatenate([f(conv_w)[0], f(ln_g), f(ln_b), f(g_conv)], axis=0)
    shared = {
        "meta_tokens": f(meta_tokens), "g_pre": f(g_pre).reshape(8, 128), "w_in": f(w_in)[0],
        "sinks": f(sinks), "g_att": f(g_att), "vecs": np.ascontiguousarray(vecs),
        "w_pw": f(w_pw)[0], "w_out": f(w_out)[0], "g_post": f(g_post),
        "c_ident": ident, "c_bprev": bprev, "c_bown": bown,
    }
    in_maps = []
    for c in range(NCORES):
        m = dict(shared)
        m["x_prompt"] = x_prompt[NPB * c:NPB * (c + 1)]
        m["x_sample"] = x_sample[NSB * c:NSB * (c + 1)].reshape(NSB * DEC, D)
        m["cache_k"] = cache_k[0, NSB * c:NSB * (c + 1)].reshape(NSB, 128, 128)
        m["cache_v"] = cache_v[0, NSB * c:NSB * (c + 1)].reshape(NSB, 128, 128)
        m["state_conv"] = state_conv[0, NSB * c:NSB * (c + 1)]
        in_maps.append(m)
    res = run_bass_kernel_spmd(nc, in_maps, core_ids=list(range(NCORES)))
    R = res.results
    cat = lambda k: np.concatenate([np.asarray(r[k], dtype=np.float32) for r in R], axis=0)
    y_p = cat("y_prompt")
    y_s = cat("y_sample").reshape(32, DEC, D)
    nk_p = cat("nk_p").reshape(1, 16, 128, 2, 64)
    nv_p = cat("nv_p").reshape(1, 16, 128, 2, 64)
    nc_p = cat("nc_p").reshape(1, 16, HALO, 512)
    nk_s = cat("nk_s").reshape(1, 32, 128, 2, 64)
    nv_s = cat("nv_s").reshape(1, 32, 128, 2, 64)
    nc_s = cat("nc_s").reshape(1, 32, HALO, 512)
    return (y_p, y_s, nk_p, nv_p, nc_p, nk_s, nv_s, nc_s)
```

```python
import contextlib
import numpy as np
import concourse.bass as bass
import concourse.mybir as mybir
from concourse.bass_utils import run_bass_kernel_spmd

F32 = mybir.dt.float32
BF16 = mybir.dt.bfloat16
AF = mybir.ActivationFunctionType
ALU = mybir.AluOpType

NCORES = 8
D = 1024
SEQ = 2048
NPB = 2
NSB = 4
DEC = 64
D_IN = 2816
CW = 31
HALO = CW - 1
NMETA = 16
RMS_EPS = 1e-6
LN_EPS = 1e-5
SCALE = 0.125
ST = 512
OQ, OK_, OV, OGA, OA, OB, OGB = 0, 512, 640, 768, 1280, 1792, 2304


class Sched:
    ENGS = ("pe", "act", "dve", "pool", "sp")
    EXCL = frozenset(["pz0", "pz1", "ptr", "pst", "pS0", "pS1", "pS2", "pO"])

    def __init__(self, nc):
        self.nc = nc
        self.streams = {e: [] for e in self.ENGS}
        self.count = {e: 0 for e in self.ENGS}
        self.waited = {e: {} for e in self.ENGS}
        self.dma_count = {}
        self.dma_rr = 0
        self.last_w = {}
        self.readers = {}
        self.sem_names = set(self.ENGS)

    def _deps(self, reads, writes):
        deps = set()
        for r in reads:
            t = self.last_w.get(r)
            if t is not None:
                deps.add(t)
            if r in self.EXCL:
                for t in self.readers.get(r, ()):
                    deps.add(t)
        for w in writes:
            t = self.last_w.get(w)
            if t is not None:
                deps.add(t)
            for t in self.readers.get(w, ()):
                deps.add(t)
        return deps

    def _commit(self, tok, reads, writes):
        for r in reads:
            self.readers.setdefault(r, []).append(tok)
        for w in writes:
            self.last_w[w] = tok
            self.readers[w] = []

    def _emit_waits(self, eng, deps):
        need = {}
        for (s, v) in deps:
            if s == "pe" and eng == "pe":
                continue
            if v > need.get(s, 0):
                need[s] = v
        for s, v in sorted(need.items()):
            if self.waited[eng].get(s, 0) >= v:
                continue
            self.waited[eng][s] = v
            self.streams[eng].append(("wait", s, v))

    def op(self, eng, fn, reads=(), writes=()):
        deps = self._deps(reads, writes)
        self._emit_waits(eng, deps)
        self.count[eng] += 1
        tok = (eng, self.count[eng])
        self.streams[eng].append(("op", fn, eng, 1))
        self._commit(tok, reads, writes)
        return tok

    NDMA = 24

    def dma(self, fn, sem, reads=(), writes=(), n=1, eng="sp"):
        assert n == 1
        sem = "d%d" % (self.dma_rr % self.NDMA)
        self.dma_rr += 1
        self.sem_names.add(sem)
        deps = self._deps(reads, writes)
        prev = self.dma_count.get(sem, 0)
        if prev:
            deps.add((sem, prev))
        self._emit_waits(eng, deps)
        self.dma_count[sem] = prev + 16
        tok = (sem, self.dma_count[sem])
        self.streams[eng].append(("dma", fn, sem, 1))
        self._commit(tok, reads, writes)
        return tok

    def wait_all(self, eng, toks):
        self._emit_waits(eng, toks)

    def emit(self):
        nc = self.nc
        with contextlib.ExitStack() as es:
            sems = {}
            for s in sorted(self.sem_names):
                sems[s] = es.enter_context(nc.semaphore("s_" + s))
            block = es.enter_context(nc.Block())

            def run(engname, e):
                for item in self.streams[engname]:
                    if item[0] == "wait":
                        e.wait_ge(sems[item[1]], item[2])
                    elif item[0] == "op":
                        ins = item[1](e)
                        ins.then_inc(sems[item[2]], 1)
                    else:
                        lst = item[1](e)
                        if not isinstance(lst, (list, tuple)):
                            lst = [lst]
                        assert len(lst) == item[3]
                        for ins in lst:
                            ins.then_inc(sems[item[2]], 16)

            @block.tensor
            def _(e):
                run("pe", e)

            @block.scalar
            def _(e):
                run("act", e)

            @block.vector
            def _(e):
                run("dve", e)

            @block.gpsimd
            def _(e):
                run("pool", e)

            @block.sync
            def _(e):
                run("sp", e)


def build_program():
    nc = bass.Bass("TRN2", target_bir_lowering=False)
    di = lambda name, shape: nc.dram_tensor(name, shape, F32, kind="ExternalInput").ap()
    do = lambda name, shape: nc.dram_tensor(name, shape, F32, kind="ExternalOutput").ap()
    x_p = di("x_prompt", [NPB, SEQ, D])
    x_s = di("x_sample", [NSB * DEC, D])
    cache_k = di("cache_k", [NSB, 128, 128])
    cache_v = di("cache_v", [NSB, 128, 128])
    state_conv = di("state_conv", [NSB, HALO, 512])
    meta = di("meta_tokens", [NMETA, D])
    g_pre = di("g_pre", [8, 128])
    w_in = di("w_in", [D, D_IN])
    sinks = di("sinks", [1, 8])
    g_att = di("g_att", [1, 512])
    vecs = di("vecs", [34, 512])
    w_pw = di("w_pw", [512, 512])
    w_out = di("w_out", [D, D])
    g_post = di("g_post", [1, D])
    c_ident = di("c_ident", [128, 128])
    c_bprev = di("c_bprev", [128, 1024])
    c_bown = di("c_bown", [128, 1024])

    y_p = do("y_prompt", [NPB, SEQ, D])
    y_s = do("y_sample", [NSB * DEC, D])
    nk_p = do("nk_p", [NPB, 128, 128])
    nv_p = do("nv_p", [NPB, 128, 128])
    nc_p = do("nc_p", [NPB, HALO, 512])
    nk_s = do("nk_s", [NSB, 128, 128])
    nv_s = do("nv_s", [NSB, 128, 128])
    nc_s = do("nc_s", [NSB, HALO, 512])

    S = Sched(nc)
    es = contextlib.ExitStack()
    with es:
        sb = lambda name, shape, dt: es.enter_context(nc.sbuf_tensor(name, shape, dt))
        ps = lambda name, shape, dt: es.enter_context(nc.psum_tensor(name, shape, dt))

        NXS = 3
        NPE = 8
        Win = sb("Win", [128, 8, D_IN], BF16)
        WoA = sb("WoA", [128, 4, D], BF16)
        WoC = sb("WoC", [128, 4, D], BF16)
        Wpw = sb("Wpw", [128, 4, 512], BF16)
        XS = sb("XS", [128, NXS, D], F32)
        XR = sb("XR", [128, 2, D], F32)
        Xs = sb("Xs", [128, D], BF16)
        XT = sb("XT", [128, 8, ST], BF16)
        QT = sb("QT", [128, 4, ST], BF16)
        KT2 = [sb("KT%d" % i, [128, 128 + ST], BF16) for i in range(2)]
        VA = sb("VA", [128, 5, 2, 128], BF16)
        SGA = sb("SGA", [128, 4, ST], BF16)
        SGB = [sb("SGB%d" % i, [128, 4, ST], BF16) for i in range(2)]
        U = sb("U", [128, 4, HALO + ST], BF16)
        UH = sb("UH", [128, 4, HALO], BF16)
        UL = sb("UL", [128, 4, NSB * HALO], F32)
        Dg = sb("Dg", [128, 4 * NPE, 128], BF16)
        TBa = sb("TBa", [128, ST], F32)
        TBb = sb("TBb", [128, ST], F32)
        H2 = [sb("H_%d" % i, [128, 4, ST], F32) for i in range(2)]
        HB = sb("HB", [128, 4, ST], BF16)
        R0 = sb("R0", [128, ST], F32)
        R1 = sb("R1", [128, ST], F32)
        CM = sb("CM", [128, 4, ST], BF16)
        SQ = CM
        P01 = [[sb("P%d_%d" % (i, j), [128, 512], BF16) for j in range(2)] for i in range(2)]
        P2 = [sb("P%d_2" % i, [64, 512], BF16) for i in range(2)]
        RD = sb("RD", [128, 512], F32)
        ATT = sb("ATT", [128, 4, ST], F32)
        AM = [sb("AM%d" % i, [128, 4, ST], BF16) for i in range(3)]
        YS = sb("YS0", [128, D], F32)
        Gpost = sb("Gpost", [128, D], F32)
        Bprev = sb("Bprev", [128, 1024], BF16)
        Bown = sb("Bown", [128, 1024], BF16)
        identb = sb("identb", [128, 128], BF16)
        identf = sb("identf", [128, 128], F32)
        onesm = sb("onesm", [128, 128], BF16)
        PV = sb("PVEC", [128, 4, 34], F32)
        GP = sb("GP", [128, 8], F32)
        GPh = sb("GPh", [128, 8], F32)
        GA = sb("GA", [128, 4], F32)
        sst = sb("sst", [128, 16], F32)
        skt = sb("skt", [1, 8], F32)
        ske = sb("ske", [1, 8], F32)
        vmrow = sb("vmrow", [1, 2, 128], BF16)
        MK2 = [sb("MK%d" % i, [128, NMETA], BF16) for i in range(2)]
        VM = sb("VM", [49, 128], BF16)
        UM = sb("UM", [128, 4, HALO], BF16)
        KC = sb("KC", [128, NSB, 128], BF16)
        VC = sb("VC", [128, NSB, 2, 128], BF16)
        CST = sb("CST", [128, 128], F32)
        VS = VA[0:64, 0:NSB, :, :]
        US = U[:].rearrange("p j t -> p (j t)")[:, 0:4 * NSB * (HALO + DEC)].rearrange("p (j i t) -> p j i t", j=4, i=NSB)
        OST = RD
        VST = RD[0:34, :]
        VST2 = TBb[0:8, 0:128]
        CSB = Xs[:, 0:128]

        pz = [ps("pz%d" % i, [128, 512], F32) for i in range(2)]
        ptr = ps("ptr", [128, 8, 128], BF16)
        pst = ps("pst", [128, 512], F32)
        pS = [ps("pS%d" % i, [128, 512], F32) for i in range(3)]
        pO = ps("pO", [128, 512], F32)

        cnt = {"n": 0}

        def uniq(p):
            cnt["n"] += 1
            return "%s%d" % (p, cnt["n"])

        def dma(out, in_, reads, writes, **kw):
            return S.dma(lambda e: [e.dma_start(out=out, in_=in_, **kw)], None, reads=reads, writes=writes)

        def rsqrt_act(out_ap, in_ap, n, eps, reads, writes, tmp_ap, tmpname):
            S.op("act", lambda e: e.activation(out=tmp_ap, in_=in_ap, func=AF.Ln, bias=eps, scale=1.0 / n),
                 reads=reads, writes=[tmpname])
            S.op("act", lambda e: e.activation(out=out_ap, in_=tmp_ap, func=AF.Exp, scale=-0.5),
                 reads=[tmpname], writes=writes)

        ALLXS = ["XS0", "XS1", "XS1a", "XS1b", "XS2"]

        dma(XS[:, 0, 0:128], c_ident[:, :], [], ["XS0"])
        S.op("dve", lambda e: e.tensor_copy(out=identb[:], in_=XS[:, 0, 0:128]), reads=["XS0"], writes=["identb"])
        S.op("act", lambda e: e.copy(out=identf[:], in_=XS[:, 0, 0:128]), reads=["XS0"], writes=["identf"])
        dma(XS[:, 1, :], c_bprev[:, :], [], ["XS1"])
        S.op("dve", lambda e: e.tensor_copy(out=Bprev[:], in_=XS[:, 1, :]), reads=["XS1"], writes=["Bprev"])
        dma(XS[:, 2, :], c_bown[:, :], [], ["XS2"])
        S.op("dve", lambda e: e.tensor_copy(out=Bown[:], in_=XS[:, 2, :]), reads=["XS2"], writes=["Bown"])
        S.op("pool", lambda e: e.memset(onesm[:], 1.0 / 512.0), writes=["onesm"])
        S.op("pool", lambda e: e.memset(VA[:], 1.0), writes=["VA"])
        S.op("pool", lambda e: e.memset(VC[:], 1.0), writes=["VC"])
        S.op("pool", lambda e: e.memset(VM[:], 1.0), writes=["VM"])
        S.op("pool", lambda e: e.memset(UM[:], 0.0), writes=["UM"])
        for kv in range(2):
            S.op("pool", lambda e, kv=kv: e.memset(KT2[kv][:], 0.0), writes=["KT"])
            S.op("pool", lambda e, kv=kv: e.memset(MK2[kv][:], 0.0), writes=["MK"])
        dma(Gpost[:], g_post[0:1, :].partition_broadcast(128), [], ["Gpost"])
        dma(VST, vecs[:, :], [], ["RD0", "RD1"])
        dma(VST2, g_pre[:, :], [], ["TBb"])
        for kv in range(2):
            dma(GA[kv * 64:(kv + 1) * 64, :],
                g_att[0, kv * 256:(kv + 1) * 256].rearrange("(g d) -> d g", d=64), [], ["GA"],
                allow_slow_non_contiguous=True)
        dma(skt[:], sinks[:, :], [], ["skt"])

        def tr_vec(e):
            for j in range(4):
                e.transpose(pst[:, j * 34:(j + 1) * 34], VST[:, j * 128:(j + 1) * 128], identf[0:34, 0:34])
            return e.transpose(pst[:, 136:144], VST2, identf[0:8, 0:8])
        S.op("pe", tr_vec, reads=["RD0", "RD1", "TBb", "identf"], writes=["pst"])
        S.op("dve", lambda e: e.tensor_copy(out=PV[:].rearrange("p j t -> p (j t)"), in_=pst[:, 0:136]),
             reads=["pst"], writes=["PV"])
        S.op("dve", lambda e: e.tensor_copy(out=GP[:], in_=pst[:, 136:144]), reads=["pst"], writes=["GP"])

        S.op("act", lambda e: e.activation(out=ske[:], in_=skt[:], func=AF.Exp), reads=["skt"], writes=["ske"])
        S.op("pool", lambda e: e.memset(vmrow[:], 0.0), writes=["vmrow"])
        S.op("pool", lambda e: e.memset(vmrow[0:1, 0, 64:128], 1.0), reads=["vmrow"], writes=["vmrow"])
        S.op("pool", lambda e: e.memset(vmrow[0:1, 1, 0:64], 1.0), reads=["vmrow"], writes=["vmrow"])

        def write_sink_rows(nq):
            row = TBa[0:1, :].bitcast(BF16)
            for kv in range(2):
                S.op("dve", lambda e, kv=kv: e.tensor_copy(
                    out=row[0:1, kv * 512:kv * 512 + 4 * nq].rearrange("p (g q) -> p g q", g=4),
                    in_=ske[0:1, kv * 4:(kv + 1) * 4].unsqueeze(2).to_broadcast([1, 4, nq])),
                    reads=["ske", "TBa"], writes=["TBa"])
            for pb in range(2):
                for kv in range(2):
                    dma(P2[pb][kv * 32 + NMETA:kv * 32 + NMETA + 1, 0:4 * nq], row[0:1, kv * 512:kv * 512 + 4 * nq],
                        ["TBa", "P%d_2" % pb], ["P%d_2" % pb])

        write_sink_rows(128)

        def perm_out(k, base):
            return Win[:, k, base:base + 512].rearrange("p (g kv d) -> p kv g d", g=4, kv=2)

        def perm_in(stg, base):
            return stg[:, base:base + 512].rearrange("p (kv g d) -> p kv g d", kv=2, g=4)

        HC = D_IN // 2
        S.op("dve", lambda e: e.tensor_scalar(out=GPh[:], in0=GP[:], scalar1=0.5, scalar2=None, op0=ALU.mult), reads=["GP"], writes=["GP"])
        stgbufs = [
            (H2[0][:].rearrange("p a b -> p (a b)"), ["H0_%d" % j for j in range(4)]),
            (H2[1][:].rearrange("p a b -> p (a b)"), ["H1_%d" % j for j in range(4)]),
            (XR[:].rearrange("p a b -> p (a b)"), ["XR0", "XR1"]),
        ]
        rot = {"n": 0}

        def next_stage():
            b = stgbufs[rot["n"] % len(stgbufs)]
            rot["n"] += 1
            return b

        segs = [(OQ, 512, "perm"), (OK_, 256, "plain"), (OGA, 512, "perm"), (OGB, 512, "plain"), (OB, 512, "plain"), (OA, 512, "half")]

        def thread_W():
            cvt = {"n": 0}
            for (c0, wd, mode) in segs:
                for kh in range(2):
                    flat, names = next_stage()
                    stg = flat[:, 0:4 * wd].rearrange("p (kk c) -> p kk c", kk=4)
                    dma(stg, w_in[kh * 512:(kh + 1) * 512, c0:c0 + wd].rearrange("(kk p) c -> p kk c", p=128), [], names)
                    for kk in range(4):
                        k = kh * 4 + kk
                        sc = GPh[:, k:k + 1] if mode == "half" else GP[:, k:k + 1]
                        if mode == "perm":
                            o_ap = Win[:, k, c0:c0 + 512].rearrange("p (g kv d) -> p kv g d", g=4, kv=2)
                            i_ap = stg[:, kk, :].rearrange("p (kv g d) -> p kv g d", kv=2, g=4)
                        else:
                            o_ap = Win[:, k, c0:c0 + wd]
                            i_ap = stg[:, kk, :]
                        cvt["n"] += 1
                        if cvt["n"] % 2 == 0:
                            S.op("act", lambda e, o_ap=o_ap, i_ap=i_ap, sc=sc: e.activation(out=o_ap, in_=i_ap, func=AF.Copy, scale=sc),
                                 reads=names + ["GP"], writes=["W%d_%d" % (c0, k)])
                        else:
                            S.op("dve", lambda e, o_ap=o_ap, i_ap=i_ap, sc=sc: e.tensor_scalar(out=o_ap, in0=i_ap, scalar1=sc, scalar2=None, op0=ALU.mult),
                                 reads=names + ["GP"], writes=["W%d_%d" % (c0, k)])
                    yield None
                yield ("flag", "W%d" % c0)
            for gp in range(2):
                flat, names = next_stage()
                stgA = flat[:, 0:2 * D].rearrange("p (a b) -> p a b", a=2)
                for kv in range(2):
                    dma(stgA[kv * 64:(kv + 1) * 64, :, :],
                        w_out[kv * 256 + gp * 128:kv * 256 + (gp + 1) * 128, :].rearrange("(g d) n -> d g n", d=64),
                        [], names if kv == 0 else ["XSw"])
                for gl in range(2):
                    g = gp * 2 + gl
                    S.op("act", lambda e, g=g, gl=gl, stgA=stgA: e.activation(out=WoA[:, g, :], in_=stgA[:, gl, :], func=AF.Copy, scale=GA[:, g:g + 1]),
                         reads=names + ["XSw", "GA"], writes=["WoA", "XSw"])
                yield None
            for hh in range(2):
                flat, names = next_stage()
                stgC = flat[:, 0:2 * D].rearrange("p (a b) -> p a b", a=2)
                dma(stgC, w_out[512 + hh * 256:512 + (hh + 1) * 256, :].rearrange("(j p) n -> p j n", p=128), [], names)
                for jl in range(2):
                    j = hh * 2 + jl
                    S.op("dve", lambda e, j=j, jl=jl, stgC=stgC: e.tensor_scalar(out=WoC[:, j, :], in0=stgC[:, jl, :], scalar1=PV[:, j, 33:34], scalar2=None, op0=ALU.mult),
                         reads=names + ["PV"], writes=["WoC"])
                yield None
            flat, names = next_stage()
            stgP = flat[:, 0:4 * 512].rearrange("p (j n) -> p j n", n=512)
            dma(stgP, w_pw.rearrange("(j p) n -> p j n", p=128), [], names)
            S.op("dve", lambda e: e.tensor_copy(out=Wpw[:], in_=stgP), reads=names, writes=["Wpw"])
            yield None
            for j in range(4):
                for tau in range(NPE):
                    eng = "act" if (j * NPE + tau) % 2 == 0 else "dve"
                    if eng == "act":
                        S.op("act", lambda e, j=j, tau=tau: e.activation(out=Dg[:, j * NPE + tau, :], in_=identf[:, :], func=AF.Copy, scale=PV[:, j, tau:tau + 1]),
                             reads=["identf", "PV"], writes=["Dg"])
                    else:
                        S.op("dve", lambda e, j=j, tau=tau: e.tensor_scalar(out=Dg[:, j * NPE + tau, :], in0=identf[:, :], scalar1=PV[:, j, tau:tau + 1], scalar2=None, op0=ALU.mult),
                             reads=["identf", "PV"], writes=["Dg"])
                yield None
            yield ("flag", "W_done")

        def rms_and_transpose(xsrc, xres, T, xt_col0, dst=None, dstname="XT"):
            S.op("act", lambda e: e.activation(out=TBa[0:T, :].bitcast(BF16)[:, 0:D], in_=xsrc, func=AF.Square, accum_out=sst[0:T, 0:1]),
                 reads=xres, writes=["TBa", "sst0"])
            rsqrt_act(sst[0:T, 1:2], sst[0:T, 0:1], float(D), RMS_EPS, ["sst0"], ["sst1"], sst[0:T, 2:3], "sst2")
            S.op("act", lambda e: e.activation(out=Xs[0:T, :], in_=xsrc, func=AF.Copy, scale=sst[0:T, 1:2]),
                 reads=xres + ["sst1"], writes=["Xs"])

            def tr(e):
                for k in range(8):
                    r = e.transpose(ptr[:, k, 0:T], Xs[0:T, k * 128:(k + 1) * 128], identb[0:T, 0:T])
                return r
            S.op("pe", tr, reads=["Xs", "identb"], writes=["ptr"])
            d_ap = XT[:, :, xt_col0:xt_col0 + T] if dst is None else dst
            S.op("dve", lambda e: e.tensor_copy(out=d_ap, in_=ptr[:, :, 0:T]), reads=["ptr"], writes=[dstname])

        pa = [pS[0], pS[1], pS[2], pO]
        pan = ["pS0", "pS1", "pS2", "pO"]

        def wnames(col0):
            for (c0, wd, _m) in segs:
                if c0 <= col0 < c0 + wd:
                    return ["W%d_%d" % (c0, k) for k in range(8)]
            raise AssertionError(col0)

        def fm_chunk(col0, T, bank, xt=None, xtname="XT"):
            def mm(e):
                for k in range(8):
                    r = e.matmul(pa[bank][:, 0:T], lhsT=Win[:, k, col0:col0 + 128], rhs=(XT if xt is None else xt)[:, k, 0:T], start=(k == 0), stop=(k == 7))
                return r
            S.op("pe", mm, reads=wnames(col0) + [xtname], writes=[pan[bank]])

        def tm_matmul(col0, tok0, ntok, xt=None, xtname="XT"):
            def mm(e):
                for k in range(8):
                    r = e.matmul(pst[0:ntok, 0:128], lhsT=(XT if xt is None else xt)[:, k, tok0:tok0 + ntok], rhs=Win[:, k, col0:col0 + 128],
                                 start=(k == 0), stop=(k == 7))
                return r
            S.op("pe", mm, reads=wnames(col0) + [xtname], writes=["pst"])

        def v_aug_copy(e, tile_ap, src_ap):
            e.copy(out=tile_ap[:, 0, 0:64], in_=src_ap[:, 0:64])
            return e.copy(out=tile_ap[:, 1, 64:128], in_=src_ap[:, 64:128])

        def attention_unit(nq, qcol0, kprev, vprev, kown, vown, nown, pbuf, full):
            N = 4 * nq
            for kv in range(2):
                pbuf = kv
                Pp, Po = P01[pbuf]
                Pm = P2[pbuf]
                Ob = [pO, pst][kv]
                on = ["pO", "pst"][kv]
                rows = slice(kv * 64, (kv + 1) * 64)
                mrows = slice(kv * 32, kv * 32 + NMETA)
                mrows1 = slice(kv * 32, kv * 32 + NMETA + 1)
                qrhs = QT[:, :, qcol0:qcol0 + nq] if full else QT[rows, :, qcol0:qcol0 + nq]
                mk = MK2[kv][:, :] if full else MK2[kv][rows, :]
                bp = Bprev[:, kv * 512:(kv + 1) * 512].rearrange("p (g q) -> p g q", g=4)[:, :, 0:nq]
                bo = Bown[:, kv * 512:(kv + 1) * 512].rearrange("p (g q) -> p g q", g=4)[:, :, 0:nq]

                def qk(e, kv=kv, qrhs=qrhs, bp=bp, bo=bo, mrows=mrows, mk=mk):
                    if kprev is not None:
                        e.matmul(pS[0][:, 0:N], lhsT=kprev(kv), rhs=qrhs, start=True, stop=False)
                        e.matmul(pS[0][:, 0:N], lhsT=identb[:, :], rhs=bp, start=False, stop=True)
                    e.matmul(pS[1][0:nown, 0:N], lhsT=kown(kv), rhs=qrhs, start=True, stop=False)
                    e.matmul(pS[1][0:nown, 0:N], lhsT=identb[:, 0:nown], rhs=bo, start=False, stop=True)
                    return e.matmul(pS[2][mrows, 0:N], lhsT=mk, rhs=qrhs, start=True, stop=True)
                S.op("pe", qk, reads=["QT", "KT", "KC", "MK", "Bprev", "Bown", "identb"], writes=["pS0", "pS1", "pS2"])
                if kprev is not None:
                    S.op("act", lambda e, Pp=Pp: e.activation(out=Pp[:, 0:N], in_=pS[0][:, 0:N], func=AF.Exp, scale=SCALE),
                         reads=["pS0"], writes=["P%d_0" % pbuf])
                S.op("act", lambda e, Po=Po: e.activation(out=Po[0:nown, 0:N], in_=pS[1][0:nown, 0:N], func=AF.Exp, scale=SCALE),
                     reads=["pS1"], writes=["P%d_1" % pbuf])
                S.op("act", lambda e, mrows=mrows, Pm=Pm: e.activation(out=Pm[mrows, 0:N], in_=pS[2][mrows, 0:N], func=AF.Exp, scale=SCALE),
                     reads=["pS2"], writes=["P%d_2" % pbuf])

                def pv(e, kv=kv, mrows1=mrows1, Pp=Pp, Po=Po, Pm=Pm, Ob=Ob):
                    first = True
                    if kprev is not None:
                        e.matmul(Ob[:, 0:N], lhsT=vprev(kv), rhs=Pp[:, 0:N], start=True, stop=False)
                        first = False
                    e.matmul(Ob[:, 0:N], lhsT=vown(kv), rhs=Po[0:nown, 0:N], start=first, stop=False)
                    return e.matmul(Ob[:, 0:N], lhsT=VM[mrows1, :], rhs=Pm[mrows1, 0:N], start=False, stop=True)
                S.op("pe", pv, reads=["VA", "VC", "VM", "P%d_0" % pbuf, "P%d_1" % pbuf, "P%d_2" % pbuf], writes=[on])
                num = slice(0, 64) if kv == 0 else slice(64, 128)
                den = slice(64, 128) if kv == 0 else slice(0, 64)
                S.op("act", lambda e, num=num, den=den, Ob=Ob: e.activation(out=RD[num, 0:N], in_=Ob[den, 0:N], func=AF.Ln), reads=[on], writes=["RD%d" % kv])
                S.op("act", lambda e, num=num: e.activation(out=RD[num, 0:N], in_=RD[num, 0:N], func=AF.Exp, scale=-1.0), reads=["RD%d" % kv], writes=["RD%d" % kv])
                yield None
                S.op("dve", lambda e, num=num, Ob=Ob: e.tensor_tensor(
                    out=ATT[num, :, qcol0:qcol0 + nq], in0=Ob[num, 0:N].rearrange("p (g q) -> p g q", g=4),
                    in1=RD[num, 0:N].rearrange("p (g q) -> p g q", g=4), op=ALU.mult),
                    reads=[on, "RD%d" % kv], writes=["ATT"])
                yield None

                yield None

        def att_finish(T, par):
            A = AM[par]
            an = "AM%d" % par
            S.op("act", lambda e: e.activation(out=A[:, :, 0:T], in_=ATT[:, :, 0:T], func=AF.Square), reads=["ATT"], writes=[an])

            def stats(e):
                for j in range(4):
                    r = e.matmul(pS[0][:, 0:T], lhsT=onesm[:, :], rhs=A[:, j, 0:T], start=(j == 0), stop=(j == 3))
                return r
            S.op("pe", stats, reads=[an, "onesm"], writes=["pS0"])
            S.op("act", lambda e: e.activation(out=TBa[:, 0:T], in_=pS[0][:, 0:T], func=AF.Ln, bias=RMS_EPS, scale=1.0), reads=["pS0"], writes=["TBa"])
            S.op("act", lambda e: e.activation(out=RD[:, 0:T], in_=TBa[:, 0:T], func=AF.Exp, scale=-0.5), reads=["TBa"], writes=["RD0", "RD1"])
            for g in range(4):
                S.op("dve", lambda e, g=g: e.tensor_tensor(out=ATT[:, g, 0:T], in0=ATT[:, g, 0:T], in1=SGA[:, g, 0:T], op=ALU.mult),
                     reads=["ATT", "SGA"], writes=["ATT"])
                S.op("pool", lambda e, g=g: e.tensor_tensor(out=A[:, g, 0:T], in0=ATT[:, g, 0:T], in1=RD[:, 0:T], op=ALU.mult),
                     reads=["ATT", "RD0", "RD1"], writes=[an])

        xr_state = {"n": 0}

        def out_tile(tok0, ntok, par, xsrc_dram, ydst):
            A = AM[par]
            h = xr_state["n"]
            xr_state["n"] += 1
            xs = h % 2
            xn_ = "XR%d" % xs
            dma(XR[0:ntok, xs, :], xsrc_dram, [], [xn_])

            def mm(e):
                for half in range(2):
                    for g in range(4):
                        e.matmul(pz[half][0:ntok, :], lhsT=A[:, g, tok0:tok0 + ntok], rhs=WoA[:, g, half * 512:(half + 1) * 512],
                                 start=(g == 0), stop=False)
                    for j in range(4):
                        r = e.matmul(pz[half][0:ntok, :], lhsT=CM[:, j, tok0:tok0 + ntok], rhs=WoC[:, j, half * 512:(half + 1) * 512],
                                     start=False, stop=(j == 3))
                return r
            S.op("pe", mm, reads=["AM%d" % par, "SQ0", "SQ1", "SQ2", "SQ3", "WoA", "WoC"], writes=["pz0", "pz1"])
            for half in range(2):
                S.op("act", lambda e, half=half: e.activation(out=TBb[0:ntok, :].bitcast(BF16)[:, 0:512], in_=pz[half][0:ntok, :], func=AF.Square,
                                                              accum_out=sst[0:ntok, 4 + half:5 + half]),
                     reads=["pz%d" % half], writes=["TBb", "sst%d" % (4 + half)])
            for half in range(2):
                S.op("dve", lambda e, half=half: e.tensor_tensor(out=YS[0:ntok, half * 512:(half + 1) * 512], in0=pz[half][0:ntok, :],
                                                                 in1=Gpost[0:ntok, half * 512:(half + 1) * 512], op=ALU.mult),
                     reads=["pz%d" % half, "Gpost"], writes=["YS0"])
            S.op("pool", lambda e: e.tensor_tensor(out=sst[0:ntok, 6:7], in0=sst[0:ntok, 4:5], in1=sst[0:ntok, 5:6], op=ALU.add),
                 reads=["sst4", "sst5"], writes=["sst6"])
            rsqrt_act(sst[0:ntok, 7:8], sst[0:ntok, 6:7], float(D), RMS_EPS, ["sst6"], ["sst7"], sst[0:ntok, 8:9], "sst8")
            yield None
            S.op("dve", lambda e: e.scalar_tensor_tensor(out=XR[0:ntok, xs, :], in0=YS[0:ntok, :], scalar=sst[0:ntok, 7:8], in1=XR[0:ntok, xs, :],
                                                         op0=ALU.mult, op1=ALU.add),
                 reads=["YS0", "sst7", xn_], writes=[xn_])
            dma(ydst, XR[0:ntok, xs, :], [xn_], [uniq("yout")])
            yield None

        XTm = KT2[0][:, 0:128].rearrange("p (k t) -> p k t", k=8)
        dma(XS[0:NMETA, 0, :], meta[:, :], [], ["XS0"])
        rms_and_transpose(XS[0:NMETA, 0, :], ["XS0"], NMETA, 0, dst=XTm[:, :, 0:NMETA], dstname="KT")
        def meta_kv():
            fm_chunk(OK_, NMETA, 0, xt=XTm, xtname="KT")

            def mk_copy(e):
                e.tensor_copy(out=MK2[0][0:64, :], in_=pa[0][0:64, 0:NMETA])
                return e.tensor_copy(out=MK2[1][64:128, :], in_=pa[0][64:128, 0:NMETA])
            S.op("dve", mk_copy, reads=[pan[0]], writes=["MK"])
            tm_matmul(OV, 0, NMETA, xt=XTm, xtname="KT")

            def vm_copy(e):
                e.copy(out=VM[0:NMETA, 0:64], in_=pst[0:NMETA, 0:64])
                return e.copy(out=VM[32:32 + NMETA, 64:128], in_=pst[0:NMETA, 64:128])
            S.op("act", vm_copy, reads=["pst"], writes=["VM"])
            dma(VM[NMETA:NMETA + 1, :], vmrow[0:1, 0, :], ["vmrow", "VM"], ["VM"])
            dma(VM[32 + NMETA:32 + NMETA + 1, :], vmrow[0:1, 1, :], ["vmrow", "VM"], ["VM"])


        def meta_glu():
            for j in range(4):
                fm_chunk(OB + j * 128, NMETA, 0, xt=XTm, xtname="KT")
                fm_chunk(OA + j * 128, NMETA, 1, xt=XTm, xtname="KT")
                S.op("act", lambda e: e.activation(out=TBa[:, 0:NMETA], in_=pa[0][:, 0:NMETA], func=AF.Tanh, scale=0.5), reads=[pan[0]], writes=["TBa"])
                S.op("dve", lambda e, j=j: e.scalar_tensor_tensor(out=UM[:, j, HALO - NMETA:HALO], in0=TBa[:, 0:NMETA], scalar=1.0, in1=pa[1][:, 0:NMETA],
                                                              op0=ALU.add, op1=ALU.mult),
                     reads=["TBa", pan[1]], writes=["UM"])

        for i in range(NSB):
            dma(CST[:], cache_k[i, :, :], [], ["CST"])
            S.op("dve", lambda e: e.tensor_copy(out=CSB[:], in_=CST[:]), reads=["CST"], writes=["Xs"])
            S.op("pe", lambda e: e.transpose(ptr[:, 0, :], CSB[:, :], identb[:, :]), reads=["Xs", "identb"], writes=["ptr"])
            S.op("act", lambda e, i=i: e.copy(out=KC[:, i, :], in_=ptr[:, 0, :]), reads=["ptr"], writes=["KC"])
            dma(CST[:], cache_v[i, :, :], [], ["CST"])
            S.op("act", lambda e, i=i: v_aug_copy(e, VC[:, i, :, :], CST[:, :]), reads=["CST"], writes=["VC"])
            dma(nk_s[i, 0:64, :], cache_k[i, 64:128, :], [], [uniq("o")])
            dma(nv_s[i, 0:64, :], cache_v[i, 64:128, :], [], [uniq("o")])

        nst = SEQ // ST
        units = []
        for b in range(NPB):
            for s_ in range(nst):
                units.append(dict(kind="p", b=b, s=s_, first=(s_ == 0), last=(s_ == nst - 1), T=ST, ntiles=4))
        units.append(dict(kind="s", T=NSB * DEC, ntiles=2, first=True, last=True))
        for i, u in enumerate(units):
            u["par"] = i % 2
            u["i"] = i

        tiles = []
        for u in units:
            for t in range(u["ntiles"]):
                if u["kind"] == "p":
                    tiles.append(x_p[u["b"], u["s"] * ST + t * 128:u["s"] * ST + (t + 1) * 128, :])
                else:
                    tiles.append(x_s[t * 128:(t + 1) * 128, :])
        ring = {"issued": 0}

        def ensure_loaded(gidx, ahead=2):
            while ring["issued"] < min(len(tiles), gidx + 1 + ahead):
                g = ring["issued"]
                sl = g % NXS
                dma(XS[:, sl, :], tiles[g], [], ["XS%d" % sl] + (["XS1a", "XS1b"] if sl == 1 else []))
                ring["issued"] += 1

        gt = {"n": 0}

        def thread_A(u):
            T = u["T"]
            par = u["par"]
            i = u["i"]
            p = (u["kind"] == "p")
            if p and not u["first"]:
                for kv in range(2):
                    S.op("pool", lambda e, kv=kv: e.tensor_copy(out=KT2[kv][:, 0:128], in_=KT2[kv][:, ST:ST + 128]), reads=["KT"], writes=["KT"])
                S.op("pool", lambda e: e.tensor_copy(out=VA[:, 0, :, :], in_=VA[:, 4, :, :]), reads=["VA"], writes=["VA"])
            for t in range(u["ntiles"]):
                g = gt["n"]
                gt["n"] += 1
                ensure_loaded(g, ahead=1)
                rms_and_transpose(XS[:, g % NXS, :], ["XS%d" % (g % NXS)], 128, t * 128)
                yield None
            if i == 0:
                yield ("need", "W%d" % OQ)
            for g in range(4):
                fm_chunk(OQ + g * 128, T, g)
                S.op("act", lambda e, g=g: e.copy(out=QT[:, g, 0:T], in_=pa[g][:, 0:T]), reads=[pan[g]], writes=["QT"])
                yield None
            if i == 0:
                yield ("need", "W%d" % OK_)
                meta_kv()
            fm_chunk(OK_, T, 0)
            def k_copy(e):
                e.copy(out=KT2[0][0:64, 128:128 + T], in_=pa[0][0:64, 0:T])
                return e.copy(out=KT2[1][64:128, 128:128 + T], in_=pa[0][64:128, 0:T])
            S.op("act", k_copy, reads=[pan[0]], writes=["KT"])
            yield None
            if p:
                for t in range(4):
                    tm_matmul(OV, t * 128, 128)
                    S.op("act", lambda e, t=t: v_aug_copy(e, VA[:, 1 + t, :, :], pst[:, 0:128]), reads=["pst"], writes=["VA"])
                    if u["last"] and t == 3:
                        S.op("act", lambda e: e.copy(out=CST[:, :], in_=pst[:, 0:128]), reads=["pst"], writes=["CST"])
                        dma(nv_p[u["b"], :, :], CST[:, :], ["CST"], [uniq("o")])
                        tm_matmul(OK_, t * 128, 128)
                        S.op("act", lambda e: e.copy(out=CST[:, :], in_=pst[:, 0:128]), reads=["pst"], writes=["CST"])
                        dma(nk_p[u["b"], :, :], CST[:, :], ["CST"], [uniq("o")])
                    yield None
            else:
                for q in range(NSB):
                    tm_matmul(OV, q * DEC, DEC)
                    S.op("act", lambda e, q=q: v_aug_copy(e, VS[:, q, :, :], pst[0:DEC, 0:128]), reads=["pst"], writes=["VA"])
                    S.op("act", lambda e: e.copy(out=CST[0:DEC, :], in_=pst[0:DEC, 0:128]), reads=["pst"], writes=["CST"])
                    dma(nv_s[q, 64:128, :], CST[0:DEC, :], ["CST"], [uniq("o")])
                    tm_matmul(OK_, q * DEC, DEC)
                    S.op("act", lambda e: e.copy(out=CST[0:DEC, :], in_=pst[0:DEC, 0:128]), reads=["pst"], writes=["CST"])
                    dma(nk_s[q, 64:128, :], CST[0:DEC, :], ["CST"], [uniq("o")])
                    yield None
            if i == 0:
                yield ("need", "W%d" % OGA)
            for g in range(4):
                fm_chunk(OGA + g * 128, T, g)
                S.op("act", lambda e, g=g: e.activation(out=SGA[:, g, 0:T], in_=pa[g][:, 0:T], func=AF.Silu), reads=[pan[g]], writes=["SGA"])
                yield None
            if i >= 2:
                yield ("need", "gate_done%d" % (i - 2))
            if i == 0:
                yield ("need", "W%d" % OGB)
            for j in range(4):
                fm_chunk(OGB + j * 128, T, j)
                S.op("act", lambda e, j=j: e.activation(out=SGB[par][:, j, 0:T], in_=pa[j][:, 0:T], func=AF.Silu),
                     reads=[pan[j]], writes=["SGB%d" % par])
                yield None
            if p:
                for t in range(4):
                    has_prev = not (u["first"] and t == 0)
                    yield from attention_unit(
                        128, t * 128,
                        (lambda kv, t=t: KT2[kv][:, t * 128:(t + 1) * 128]) if has_prev else None,
                        (lambda kv, t=t: VA[:, t, kv, :]),
                        (lambda kv, t=t: KT2[kv][:, 128 + t * 128:128 + (t + 1) * 128]),
                        (lambda kv, t=t: VA[:, 1 + t, kv, :]),
                        128, t % 2, True)
            else:
                write_sink_rows(DEC)
                for q in range(NSB):
                    yield from attention_unit(
                        DEC, q * DEC,
                        (lambda kv, q=q: KC[kv * 64:(kv + 1) * 64, q, :]),
                        (lambda kv, q=q: VC[:, q, kv, :]),
                        (lambda kv, q=q: KT2[kv][kv * 64:(kv + 1) * 64, 128 + q * DEC:128 + (q + 1) * DEC]),
                        (lambda kv, q=q: VS[:, q, kv, :]),
                        DEC, q % 2, False)
            if i >= 3:
                yield ("need", "out_done%d" % (i - 3))
            att_finish(T, i % 3)
            yield None
            if i >= 1:
                yield ("need", "conv_done%d" % (i - 1))
            if i == 0:
                yield ("need", "W%d" % OB)
                yield ("need", "W%d" % OA)
                meta_glu()
            if p:
                src = UM if u["first"] else UH
                S.op("pool", lambda e: e.tensor_copy(out=U[:, :, 0:HALO], in_=src[:, :, :]), reads=["UM", "UH", "U"], writes=["U"])
            else:
                for q in range(NSB):
                    dma(OST[0:HALO, :], state_conv[q, :, :], [], ["RD0", "RD1"])

                    def trs(e):
                        for j in range(4):
                            r = e.transpose(pst[:, j * 32:j * 32 + HALO], OST[0:HALO, j * 128:(j + 1) * 128], identf[0:HALO, 0:HALO])
                        return r
                    S.op("pe", trs, reads=["RD0", "RD1", "identf"], writes=["pst"])
                    S.op("act", lambda e, q=q: e.copy(out=US[:, :, q, 0:HALO], in_=pst[:, 0:128].rearrange("p (j t) -> p j t", t=32)[:, :, 0:HALO]),
                         reads=["pst", "U"], writes=["U"])
                yield None
            for j in range(4):
                bb = 2 * (j % 2)
                fm_chunk(OB + j * 128, T, bb)
                fm_chunk(OA + j * 128, T, bb + 1)
                S.op("act", lambda e, bb=bb: e.activation(out=TBa[:, 0:T], in_=pa[bb][:, 0:T], func=AF.Tanh, scale=0.5), reads=[pan[bb]], writes=["TBa"])
                if p:
                    S.op("dve", lambda e, j=j, bb=bb: e.scalar_tensor_tensor(out=U[:, j, HALO:HALO + ST], in0=TBa[:, :], scalar=1.0, in1=pa[bb + 1][:, :],
                                                                  op0=ALU.add, op1=ALU.mult),
                         reads=["TBa", pan[bb + 1], "U"], writes=["U"])
                    if u["last"]:
                        S.op("dve", lambda e, j=j, bb=bb: e.scalar_tensor_tensor(out=UL[:, j, 0:HALO], in0=TBa[:, ST - HALO:ST], scalar=1.0,
                                                                      in1=pa[bb + 1][:, ST - HALO:ST], op0=ALU.add, op1=ALU.mult),
                             reads=["TBa", pan[bb + 1]], writes=["UL"])
                else:
                    S.op("dve", lambda e, j=j, bb=bb: e.scalar_tensor_tensor(
                        out=US[:, j, :, HALO:HALO + DEC], in0=TBa[:, 0:T].rearrange("p (q t) -> p q t", q=NSB), scalar=1.0,
                        in1=pa[bb + 1][:, 0:T].rearrange("p (q t) -> p q t", q=NSB), op0=ALU.add, op1=ALU.mult),
                        reads=["TBa", pan[bb + 1], "U"], writes=["U"])
                    S.op("dve", lambda e, j=j, bb=bb: e.scalar_tensor_tensor(
                        out=UL[:, j, :].rearrange("p (q t) -> p q t", q=NSB),
                        in0=TBa[:, 0:T].rearrange("p (q t) -> p q t", q=NSB)[:, :, DEC - HALO:DEC], scalar=1.0,
                        in1=pa[bb + 1][:, 0:T].rearrange("p (q t) -> p q t", q=NSB)[:, :, DEC - HALO:DEC], op0=ALU.add, op1=ALU.mult),
                        reads=["TBa", pan[bb + 1]], writes=["UL"])
                yield None
            if p and not u["last"]:
                S.op("pool", lambda e: e.tensor_copy(out=UH[:, :, :], in_=U[:, :, ST:ST + HALO]), reads=["U"], writes=["UH"])
            if p and u["last"]:
                def tru(e):
                    for j in range(4):
                        r = e.transpose(pst[0:HALO, j * 128:(j + 1) * 128], UL[:, j, 0:HALO], identf[:, :])
                    return r
                S.op("pe", tru, reads=["UL", "identf"], writes=["pst"])
                S.op("act", lambda e: e.copy(out=OST[0:HALO, :], in_=pst[0:HALO, :]), reads=["pst"], writes=["RD0", "RD1"])
                dma(nc_p[u["b"], :, :], OST[0:HALO, :], ["RD0", "RD1"], [uniq("o")])
            if not p:
                for q in range(NSB):
                    def tru(e, q=q):
                        for j in range(4):
                            r = e.transpose(pst[0:HALO, j * 128:(j + 1) * 128], UL[:, j, q * HALO:(q + 1) * HALO], identf[:, :])
                        return r
                    S.op("pe", tru, reads=["UL", "identf"], writes=["pst"])
                    S.op("act", lambda e: e.copy(out=OST[0:HALO, :], in_=pst[0:HALO, :]), reads=["pst"], writes=["RD0", "RD1"])
                    dma(nc_s[q, :, :], OST[0:HALO, :], ["RD0", "RD1"], [uniq("o")])
            yield None

        def conv_gen(u):
            T = u["T"]
            p = (u["kind"] == "p")
            H = H2[u["par"]]
            hn = lambda j: "H%d_%d" % (u["par"], j)
            if p:
                u_of = lambda j, tau: U[:, j, tau:tau + ST]
                h_of = lambda buf, j: buf[:, j, 0:ST]
                c_of = lambda j: pz[j % 2][:, 0:ST]
            else:
                u_of = lambda j, tau: US[:, j, :, tau:tau + DEC]
                h_of = lambda buf, j: buf[:, j, 0:T].rearrange("p (q t) -> p q t", q=NSB)
                c_of = lambda j: pz[j % 2][:, 0:T].rearrange("p (q t) -> p q t", q=NSB)
            for j in range(4):
                def cmm(e, j=j):
                    for tau in range(NPE):
                        r = e.matmul(pz[j % 2][:, 0:T], lhsT=Dg[:, j * NPE + tau, :], rhs=u_of(j, tau), start=(tau == 0), stop=(tau == NPE - 1))
                    return r
                S.op("pe", cmm, reads=["U", "Dg"], writes=["pz%d" % (j % 2)])
                S.op("dve", lambda e, j=j: e.scalar_tensor_tensor(
                    out=h_of(H, j), in0=u_of(j, NPE), scalar=PV[:, j, NPE:NPE + 1], in1=c_of(j), op0=ALU.mult, op1=ALU.add),
                    reads=["U", "PV", "pz%d" % (j % 2)], writes=[hn(j)])
                for tau in range(NPE + 1, CW):
                    S.op("dve", lambda e, j=j, tau=tau: e.scalar_tensor_tensor(
                        out=h_of(H, j), in0=u_of(j, tau), scalar=PV[:, j, tau:tau + 1], in1=h_of(H, j), op0=ALU.mult, op1=ALU.add),
                        reads=["U", "PV", hn(j)], writes=[hn(j)])
                    if tau % 3 == 0:
                        yield None
                yield None
            yield ("flag", "conv_done%d" % u["i"])

        def back_gen(u):
            T = u["T"]
            par = u["par"]
            i = u["i"]
            hb = ["HB%d" % j for j in range(4)]
            sq = ["SQ%d" % j for j in range(4)]
            H = H2[par]
            hn = lambda j: "H%d_%d" % (par, j)
            for j in range(4):
                S.op("act", lambda e, j=j: e.copy(out=HB[:, j, 0:T], in_=H[:, j, 0:T]), reads=[hn(j)], writes=["HB%d" % j])
                S.op("act", lambda e, j=j: e.activation(out=SQ[:, j, 0:T], in_=H[:, j, 0:T], func=AF.Square), reads=[hn(j)], writes=["SQ%d" % j])
            yield None

            def stats(e):
                for j in range(4):
                    e.matmul(pz[0][:, 0:T], lhsT=onesm[:, :], rhs=HB[:, j, 0:T], start=(j == 0), stop=(j == 3))
                for j in range(4):
                    r = e.matmul(pz[1][:, 0:T], lhsT=onesm[:, :], rhs=SQ[:, j, 0:T], start=(j == 0), stop=(j == 3))
                return r
            S.op("pe", stats, reads=hb + sq + ["onesm"], writes=["pz0", "pz1"])
            S.op("act", lambda e: e.copy(out=R0[:, 0:T], in_=pz[0][:, 0:T]), reads=["pz0"], writes=["R0"])
            S.op("pool", lambda e: e.tensor_tensor(out=TBb[:, 0:T], in0=R0[:, 0:T], in1=R0[:, 0:T], op=ALU.mult), reads=["R0"], writes=["TBb"])
            S.op("dve", lambda e: e.tensor_tensor(out=TBb[:, 0:T], in0=pz[1][:, 0:T], in1=TBb[:, 0:T], op=ALU.subtract), reads=["pz1", "TBb"], writes=["TBb"])
            S.op("act", lambda e: e.activation(out=TBb[:, 0:T], in_=TBb[:, 0:T], func=AF.Ln, bias=LN_EPS, scale=1.0), reads=["TBb"], writes=["TBb"])
            S.op("act", lambda e: e.activation(out=R1[:, 0:T], in_=TBb[:, 0:T], func=AF.Exp, scale=-0.5), reads=["TBb"], writes=["R1"])
            yield None
            for j in range(4):
                S.op("dve", lambda e, j=j: e.tensor_tensor(out=H[:, j, 0:T], in0=H[:, j, 0:T], in1=R0[:, 0:T], op=ALU.subtract),
                     reads=[hn(j), "R0"], writes=[hn(j)])
                S.op("pool", lambda e, j=j: e.tensor_tensor(out=H[:, j, 0:T], in0=H[:, j, 0:T], in1=R1[:, 0:T], op=ALU.mult),
                     reads=[hn(j), "R1"], writes=[hn(j)])
                S.op("act", lambda e, j=j: e.activation(out=HB[:, j, 0:T], in_=H[:, j, 0:T], func=AF.Silu, bias=PV[:, j, 32:33], scale=PV[:, j, 31:32]),
                     reads=[hn(j), "PV"], writes=["HB%d" % j])
                yield None
            for jj in range(4):
                bank = jj % 2

                def mm(e, jj=jj, bank=bank):
                    for j in range(4):
                        r = e.matmul(pz[bank][:, 0:T], lhsT=Wpw[:, j, jj * 128:(jj + 1) * 128], rhs=HB[:, j, 0:T], start=(j == 0), stop=(j == 3))
                    return r
                S.op("pe", mm, reads=hb + ["Wpw"], writes=["pz%d" % bank])
                S.op("act", lambda e, jj=jj, bank=bank: e.activation(out=SQ[:, jj, 0:T], in_=pz[bank][:, 0:T], func=AF.Square),
                     reads=["pz%d" % bank], writes=["SQ%d" % jj])
                S.op("dve", lambda e, jj=jj, bank=bank: e.tensor_tensor(
                    out=H[:, jj, 0:T], in0=pz[bank][:, 0:T], in1=SGB[par][:, jj, 0:T], op=ALU.mult),
                    reads=["pz%d" % bank, "SGB%d" % par], writes=[hn(jj)])
                yield None
            yield ("flag", "gate_done%d" % i)

            def stats2(e):
                for j in range(4):
                    r = e.matmul(pz[0][:, 0:T], lhsT=onesm[:, :], rhs=SQ[:, j, 0:T], start=(j == 0), stop=(j == 3))
                return r
            S.op("pe", stats2, reads=sq + ["onesm"], writes=["pz0"])
            S.op("act", lambda e: e.activation(out=TBb[:, 0:T], in_=pz[0][:, 0:T], func=AF.Ln, bias=RMS_EPS, scale=1.0), reads=["pz0"], writes=["TBb"])
            S.op("act", lambda e: e.activation(out=R1[:, 0:T], in_=TBb[:, 0:T], func=AF.Exp, scale=-0.5), reads=["TBb"], writes=["R1"])
            for jj in range(4):
                S.op("pool", lambda e, jj=jj: e.tensor_tensor(out=CM[:, jj, 0:T], in0=H[:, jj, 0:T], in1=R1[:, 0:T], op=ALU.mult),
                     reads=[hn(jj), "R1"], writes=["SQ%d" % jj])
            yield None

        def out_gen(u):
            par = u["i"] % 3
            p = (u["kind"] == "p")
            for t in range(u["ntiles"]):
                if p:
                    r0 = u["s"] * ST + t * 128
                    yield from out_tile(t * 128, 128, par, x_p[u["b"], r0:r0 + 128, :], y_p[u["b"], r0:r0 + 128, :])
                else:
                    yield from out_tile(t * 128, 128, par, x_s[t * 128:(t + 1) * 128, :], y_s[t * 128:(t + 1) * 128, :])

        def interleave(g1, g2, n1=2):
            d1 = g1 is None
            d2 = g2 is None
            while not (d1 and d2):
                for _ in range(n1):
                    if not d1:
                        try:
                            yield next(g1)
                        except StopIteration:
                            d1 = True
                if not d2:
                    try:
                        yield next(g2)
                    except StopIteration:
                        d2 = True

        def A_all():
            for u in units:
                yield from thread_A(u)
                yield ("flag", "A_done%d" % u["i"])

        def chain(*gs):
            for g in gs:
                yield from g

        def one(x):
            yield x

        def B_all():
            prev = None
            for u in units:
                i = u["i"]
                yield ("need", "A_done%d" % i)
                if i == 0:
                    yield ("need", "W_done")
                yield from interleave(conv_gen(u), prev, n1=2)
                prev = chain(back_gen(u), out_gen(u), one(("flag", "out_done%d" % i)))
            yield from prev

        def run_threads(gens, weights):
            flags = set()
            st = [dict(gen=g, wait=None, done=False) for g in gens]
            order = []
            for t, w in zip(st, weights):
                order += [t] * w
            while not all(t["done"] for t in st):
                progressed = False
                for t in order:
                    if t["done"]:
                        continue
                    if t["wait"] is not None:
                        if t["wait"] not in flags:
                            continue
                        t["wait"] = None
                    try:
                        r = next(t["gen"])
                    except StopIteration:
                        t["done"] = True
                        progressed = True
                        continue
                    progressed = True
                    if isinstance(r, tuple):
                        if r[0] == "flag":
                            flags.add(r[1])
                        elif r[0] == "need" and r[1] not in flags:
                            t["wait"] = r[1]
                assert progressed, "schedule deadlock: " + str([t["wait"] for t in st])

        run_threads([thread_W(), B_all(), A_all()], [1, 1, 1])

        S.wait_all("sp", list(S.dma_count.items()))
        S.emit()
    return nc


_CACHE = {}


def _consts():
    h = np.arange(1, 9, dtype=np.float64)
    slopes = (2.0 ** (-h)).reshape(2, 4)
    j = np.arange(128)[:, None]
    i = np.arange(128)[None, :]
    NEG = -30000.0
    bprev = np.zeros((128, 2, 4, 128), np.float32)
    bown = np.zeros((128, 2, 4, 128), np.float32)
    dprev = (i + 128 - j).astype(np.float64)
    mprev = (i >= 64) & (j < 64)
    down = np.abs(i - j).astype(np.float64)
    mown = (i < 64) & (j >= 64)
    for kv in range(2):
        for g in range(4):
            bp = -slopes[kv, g] * dprev / SCALE
            bo = -slopes[kv, g] * down / SCALE
            bprev[:, kv, g, :] = np.where(mprev, NEG, bp)
            bown[:, kv, g, :] = np.where(mown, NEG, bo)
    return (np.eye(128, dtype=np.float32), bprev.reshape(128, 1024), bown.reshape(128, 1024))


def kernel(x_prompt, x_sample, cache_k, cache_v, state_conv, meta_tokens, g_pre, w_in,
           sinks, g_att, conv_w, ln_g, ln_b, w_pw, g_conv, w_out, g_post):
    f = lambda a: np.ascontiguousarray(np.asarray(a, dtype=np.float32))
    x_prompt, x_sample, cache_k, cache_v, state_conv = map(f, (x_prompt, x_sample, cache_k, cache_v, state_conv))
    if "nc" not in _CACHE:
        _CACHE["nc"] = build_program()
    nc = _CACHE["nc"]
    ident, bprev, bown = _consts()
    vecs = np.concatenate([f(conv_w)[0], f(ln_g), f(ln_b), f(g_conv)], axis=0)
    shared = {
        "meta_tokens": f(meta_tokens), "g_pre": f(g_pre).reshape(8, 128), "w_in": f(w_in)[0],
        "sinks": f(sinks), "g_att": f(g_att), "vecs": np.ascontiguousarray(vecs),
        "w_pw": f(w_pw)[0], "w_out": f(w_out)[0], "g_post": f(g_post),
        "c_ident": ident, "c_bprev": bprev, "c_bown": bown,
    }
    in_maps = []
    for c in range(NCORES):
        m = dict(shared)
        m["x_prompt"] = x_prompt[NPB * c:NPB * (c + 1)]
        m["x_sample"] = x_sample[NSB * c:NSB * (c + 1)].reshape(NSB * DEC, D)
        m["cache_k"] = cache_k[0, NSB * c:NSB * (c + 1)].reshape(NSB, 128, 128)
        m["cache_v"] = cache_v[0, NSB * c:NSB * (c + 1)].reshape(NSB, 128, 128)
        m["state_conv"] = state_conv[0, NSB * c:NSB * (c + 1)]
        in_maps.append(m)
    res = run_bass_kernel_spmd(nc, in_maps, core_ids=list(range(NCORES)))
    R = res.results
    cat = lambda k: np.concatenate([np.asarray(r[k], dtype=np.float32) for r in R], axis=0)
    y_p = cat("y_prompt")
    y_s = cat("y_sample").reshape(32, DEC, D)
    nk_p = cat("nk_p").reshape(1, 16, 128, 2, 64)
    nv_p = cat("nv_p").reshape(1, 16, 128, 2, 64)
    nc_p = cat("nc_p").reshape(1, 16, HALO, 512)
    nk_s = cat("nk_s").reshape(1, 32, 128, 2, 64)
    nv_s = cat("nv_s").reshape(1, 32, 128, 2, 64)
    nc_s = cat("nc_s").reshape(1, 32, HALO, 512)
    return (y_p, y_s, nk_p, nv_p, nc_p, nk_s, nv_s, nc_s)
```

```python
import contextlib
import numpy as np
import concourse.bass as bass
import concourse.mybir as mybir
from concourse.bass_utils import run_bass_kernel_spmd

F32 = mybir.dt.float32
BF16 = mybir.dt.bfloat16
AF = mybir.ActivationFunctionType
ALU = mybir.AluOpType

NCORES = 8
D = 1024
SEQ = 2048
NPB = 2
NSB = 4
DEC = 64
D_IN = 2816
CW = 31
HALO = CW - 1
NMETA = 16
RMS_EPS = 1e-6
LN_EPS = 1e-5
SCALE = 0.125
ST = 512
OQ, OK_, OV, OGA, OA, OB, OGB = 0, 512, 640, 768, 1280, 1792, 2304


class Sched:
    ENGS = ("pe", "act", "dve", "pool", "sp")
    EXCL = frozenset(["pz0", "pz1", "ptr", "pst", "pS0", "pS1", "pS2", "pO"])

    def __init__(self, nc):
        self.nc = nc
        self.streams = {e: [] for e in self.ENGS}
        self.count = {e: 0 for e in self.ENGS}
        self.waited = {e: {} for e in self.ENGS}
        self.dma_count = {}
        self.dma_rr = 0
        self.last_w = {}
        self.readers = {}
        self.sem_names = set(self.ENGS)

    def _deps(self, reads, writes):
        deps = set()
        for r in reads:
            t = self.last_w.get(r)
            if t is not None:
                deps.add(t)
            if r in self.EXCL:
                for t in self.readers.get(r, ()):
                    deps.add(t)
        for w in writes:
            t = self.last_w.get(w)
            if t is not None:
                deps.add(t)
            for t in self.readers.get(w, ()):
                deps.add(t)
        return deps

    def _commit(self, tok, reads, writes):
        for r in reads:
            self.readers.setdefault(r, []).append(tok)
        for w in writes:
            self.last_w[w] = tok
            self.readers[w] = []

    def _emit_waits(self, eng, deps):
        need = {}
        for (s, v) in deps:
            if s == "pe" and eng == "pe":
                continue
            if v > need.get(s, 0):
                need[s] = v
        for s, v in sorted(need.items()):
            if self.waited[eng].get(s, 0) >= v:
                continue
            self.waited[eng][s] = v
            self.streams[eng].append(("wait", s, v))

    def op(self, eng, fn, reads=(), writes=()):
        deps = self._deps(reads, writes)
        self._emit_waits(eng, deps)
        self.count[eng] += 1
        tok = (eng, self.count[eng])
        self.streams[eng].append(("op", fn, eng, 1))
        self._commit(tok, reads, writes)
        return tok

    NDMA = 24

    def dma(self, fn, sem, reads=(), writes=(), n=1, eng="sp"):
        assert n == 1
        sem = "d%d" % (self.dma_rr % self.NDMA)
        self.dma_rr += 1
        self.sem_names.add(sem)
        deps = self._deps(reads, writes)
        prev = self.dma_count.get(sem, 0)
        if prev:
            deps.add((sem, prev))
        self._emit_waits(eng, deps)
        self.dma_count[sem] = prev + 16
        tok = (sem, self.dma_count[sem])
        self.streams[eng].append(("dma", fn, sem, 1))
        self._commit(tok, reads, writes)
        return tok

    def wait_all(self, eng, toks):
        self._emit_waits(eng, toks)

    def emit(self):
        nc = self.nc
        with contextlib.ExitStack() as es:
            sems = {}
            for s in sorted(self.sem_names):
                sems[s] = es.enter_context(nc.semaphore("s_" + s))
            block = es.enter_context(nc.Block())

            def run(engname, e):
                for item in self.streams[engname]:
                    if item[0] == "wait":
                        e.wait_ge(sems[item[1]], item[2])
                    elif item[0] == "op":
                        ins = item[1](e)
                        ins.then_inc(sems[item[2]], 1)
                    else:
                        lst = item[1](e)
                        if not isinstance(lst, (list, tuple)):
                            lst = [lst]
                        assert len(lst) == item[3]
                        for ins in lst:
                            ins.then_inc(sems[item[2]], 16)

            @block.tensor
            def _(e):
                run("pe", e)

            @block.scalar
            def _(e):
                run("act", e)

            @block.vector
            def _(e):
                run("dve", e)

            @block.gpsimd
            def _(e):
                run("pool", e)

            @block.sync
            def _(e):
                run("sp", e)


def build_program():
    nc = bass.Bass("TRN2", target_bir_lowering=False)
    di = lambda name, shape: nc.dram_tensor(name, shape, F32, kind="ExternalInput").ap()
    do = lambda name, shape: nc.dram_tensor(name, shape, F32, kind="ExternalOutput").ap()
    x_p = di("x_prompt", [NPB, SEQ, D])
    x_s = di("x_sample", [NSB * DEC, D])
    cache_k = di("cache_k", [NSB, 128, 128])
    cache_v = di("cache_v", [NSB, 128, 128])
    state_conv = di("state_conv", [NSB, HALO, 512])
    meta = di("meta_tokens", [NMETA, D])
    g_pre = di("g_pre", [8, 128])
    w_in = di("w_in", [D, D_IN])
    sinks = di("sinks", [1, 8])
    g_att = di("g_att", [1, 512])
    vecs = di("vecs", [34, 512])
    w_pw = di("w_pw", [512, 512])
    w_out = di("w_out", [D, D])
    g_post = di("g_post", [1, D])
    c_ident = di("c_ident", [128, 128])
    c_bprev = di("c_bprev", [128, 1024])
    c_bown = di("c_bown", [128, 1024])

    y_p = do("y_prompt", [NPB, SEQ, D])
    y_s = do("y_sample", [NSB * DEC, D])
    nk_p = do("nk_p", [NPB, 128, 128])
    nv_p = do("nv_p", [NPB, 128, 128])
    nc_p = do("nc_p", [NPB, HALO, 512])
    nk_s = do("nk_s", [NSB, 128, 128])
    nv_s = do("nv_s", [NSB, 128, 128])
    nc_s = do("nc_s", [NSB, HALO, 512])

    S = Sched(nc)
    es = contextlib.ExitStack()
    with es:
        sb = lambda name, shape, dt: es.enter_context(nc.sbuf_tensor(name, shape, dt))
        ps = lambda name, shape, dt: es.enter_context(nc.psum_tensor(name, shape, dt))

        NXS = 3
        NPE = 8
        Win = sb("Win", [128, 8, D_IN], BF16)
        WoA = sb("WoA", [128, 4, D], BF16)
        WoC = sb("WoC", [128, 4, D], BF16)
        Wpw = sb("Wpw", [128, 4, 512], BF16)
        XS = sb("XS", [128, NXS, D], F32)
        XR = sb("XR", [128, 2, D], F32)
        Xs = sb("Xs", [128, D], BF16)
        XT = sb("XT", [128, 8, ST], BF16)
        QT = sb("QT", [128, 4, ST], BF16)
        KT2 = [sb("KT%d" % i, [128, 128 + ST], BF16) for i in range(2)]
        VA = sb("VA", [128, 5, 2, 128], BF16)
        SGA = sb("SGA", [128, 4, ST], BF16)
        SGB = [sb("SGB%d" % i, [128, 4, ST], BF16) for i in range(2)]
        U = sb("U", [128, 4, HALO + ST], BF16)
        UH = sb("UH", [128, 4, HALO], BF16)
        UL = sb("UL", [128, 4, NSB * HALO], F32)
        Dg = sb("Dg", [128, 4 * NPE, 128], BF16)
        TBa = sb("TBa", [128, ST], F32)
        TBb = sb("TBb", [128, ST], F32)
        H2 = [sb("H_%d" % i, [128, 4, ST], F32) for i in range(2)]
        HB = sb("HB", [128, 4, ST], BF16)
        R0 = sb("R0", [128, ST], F32)
        R1 = sb("R1", [128, ST], F32)
        CM = sb("CM", [128, 4, ST], BF16)
        SQ = CM
        P01 = [[sb("P%d_%d" % (i, j), [128, 512], BF16) for j in range(2)] for i in range(2)]
        P2 = [sb("P%d_2" % i, [64, 512], BF16) for i in range(2)]
        RD = sb("RD", [128, 512], F32)
        ATT = sb("ATT", [128, 4, ST], F32)
        AM = [sb("AM%d" % i, [128, 4, ST], BF16) for i in range(3)]
        YS = sb("YS0", [128, D], F32)
        Gpost = sb("Gpost", [128, D], F32)
        Bprev = sb("Bprev", [128, 1024], BF16)
        Bown = sb("Bown", [128, 1024], BF16)
        identb = sb("identb", [128, 128], BF16)
        identf = sb("identf", [128, 128], F32)
        onesm = sb("onesm", [128, 128], BF16)
        PV = sb("PVEC", [128, 4, 34], F32)
        GP = sb("GP", [128, 8], F32)
        GPh = sb("GPh", [128, 8], F32)
        GA = sb("GA", [128, 4], F32)
        sst = sb("sst", [128, 16], F32)
        skt = sb("skt", [1, 8], F32)
        ske = sb("ske", [1, 8], F32)
        vmrow = sb("vmrow", [1, 2, 128], BF16)
        MK2 = [sb("MK%d" % i, [128, NMETA], BF16) for i in range(2)]
        VM = sb("VM", [49, 128], BF16)
        UM = sb("UM", [128, 4, HALO], BF16)
        KC = sb("KC", [128, NSB, 128], BF16)
        VC = sb("VC", [128, NSB, 2, 128], BF16)
        CST = sb("CST", [128, 128], F32)
        VS = VA[0:64, 0:NSB, :, :]
        US = U[:].rearrange("p j t -> p (j t)")[:, 0:4 * NSB * (HALO + DEC)].rearrange("p (j i t) -> p j i t", j=4, i=NSB)
        OST = RD
        VST = RD[0:34, :]
        VST2 = TBb[0:8, 0:128]
        CSB = Xs[:, 0:128]

        pz = [ps("pz%d" % i, [128, 512], F32) for i in range(2)]
        ptr = ps("ptr", [128, 8, 128], BF16)
        pst = ps("pst", [128, 512], F32)
        pS = [ps("pS%d" % i, [128, 512], F32) for i in range(3)]
        pO = ps("pO", [128, 512], F32)

        cnt = {"n": 0}

        def uniq(p):
            cnt["n"] += 1
            return "%s%d" % (p, cnt["n"])

        def dma(out, in_, reads, writes, **kw):
            return S.dma(lambda e: [e.dma_start(out=out, in_=in_, **kw)], None, reads=reads, writes=writes)

        def rsqrt_act(out_ap, in_ap, n, eps, reads, writes, tmp_ap, tmpname):
            S.op("act", lambda e: e.activation(out=tmp_ap, in_=in_ap, func=AF.Ln, bias=eps, scale=1.0 / n),
                 reads=reads, writes=[tmpname])
            S.op("act", lambda e: e.activation(out=out_ap, in_=tmp_ap, func=AF.Exp, scale=-0.5),
                 reads=[tmpname], writes=writes)

        ALLXS = ["XS0", "XS1", "XS1a", "XS1b", "XS2"]

        dma(XS[:, 0, 0:128], c_ident[:, :], [], ["XS0"])
        S.op("dve", lambda e: e.tensor_copy(out=identb[:], in_=XS[:, 0, 0:128]), reads=["XS0"], writes=["identb"])
        S.op("act", lambda e: e.copy(out=identf[:], in_=XS[:, 0, 0:128]), reads=["XS0"], writes=["identf"])
        dma(XS[:, 1, :], c_bprev[:, :], [], ["XS1"])
        S.op("dve", lambda e: e.tensor_copy(out=Bprev[:], in_=XS[:, 1, :]), reads=["XS1"], writes=["Bprev"])
        dma(XS[:, 2, :], c_bown[:, :], [], ["XS2"])
        S.op("dve", lambda e: e.tensor_copy(out=Bown[:], in_=XS[:, 2, :]), reads=["XS2"], writes=["Bown"])
        S.op("pool", lambda e: e.memset(onesm[:], 1.0 / 512.0), writes=["onesm"])
        S.op("pool", lambda e: e.memset(VA[:], 1.0), writes=["VA"])
        S.op("pool", lambda e: e.memset(VC[:], 1.0), writes=["VC"])
        S.op("pool", lambda e: e.memset(VM[:], 1.0), writes=["VM"])
        S.op("pool", lambda e: e.memset(UM[:], 0.0), writes=["UM"])
        for kv in range(2):
            S.op("pool", lambda e, kv=kv: e.memset(KT2[kv][:], 0.0), writes=["KT"])
            S.op("pool", lambda e, kv=kv: e.memset(MK2[kv][:], 0.0), writes=["MK"])
        dma(Gpost[:], g_post[0:1, :].partition_broadcast(128), [], ["Gpost"])
        dma(VST, vecs[:, :], [], ["RD"])
        dma(VST2, g_pre[:, :], [], ["TBb"])
        for kv in range(2):
            dma(GA[kv * 64:(kv + 1) * 64, :],
                g_att[0, kv * 256:(kv + 1) * 256].rearrange("(g d) -> d g", d=64), [], ["GA"],
                allow_slow_non_contiguous=True)
        dma(skt[:], sinks[:, :], [], ["skt"])

        def tr_vec(e):
            for j in range(4):
                e.transpose(pst[:, j * 34:(j + 1) * 34], VST[:, j * 128:(j + 1) * 128], identf[0:34, 0:34])
            return e.transpose(pst[:, 136:144], VST2, identf[0:8, 0:8])
        S.op("pe", tr_vec, reads=["RD", "TBb", "identf"], writes=["pst"])
        S.op("dve", lambda e: e.tensor_copy(out=PV[:].rearrange("p j t -> p (j t)"), in_=pst[:, 0:136]),
             reads=["pst"], writes=["PV"])
        S.op("dve", lambda e: e.tensor_copy(out=GP[:], in_=pst[:, 136:144]), reads=["pst"], writes=["GP"])

        S.op("act", lambda e: e.activation(out=ske[:], in_=skt[:], func=AF.Exp), reads=["skt"], writes=["ske"])
        S.op("pool", lambda e: e.memset(vmrow[:], 0.0), writes=["vmrow"])
        S.op("pool", lambda e: e.memset(vmrow[0:1, 0, 64:128], 1.0), reads=["vmrow"], writes=["vmrow"])
        S.op("pool", lambda e: e.memset(vmrow[0:1, 1, 0:64], 1.0), reads=["vmrow"], writes=["vmrow"])

        def write_sink_rows(nq):
            row = TBa[0:1, :].bitcast(BF16)
            for kv in range(2):
                S.op("dve", lambda e, kv=kv: e.tensor_copy(
                    out=row[0:1, kv * 512:kv * 512 + 4 * nq].rearrange("p (g q) -> p g q", g=4),
                    in_=ske[0:1, kv * 4:(kv + 1) * 4].unsqueeze(2).to_broadcast([1, 4, nq])),
                    reads=["ske", "TBa"], writes=["TBa"])
            for pb in range(2):
                for kv in range(2):
                    dma(P2[pb][kv * 32 + NMETA:kv * 32 + NMETA + 1, 0:4 * nq], row[0:1, kv * 512:kv * 512 + 4 * nq],
                        ["TBa", "P%d_2" % pb], ["P%d_2" % pb])

        write_sink_rows(128)

        def perm_out(k, base):
            return Win[:, k, base:base + 512].rearrange("p (g kv d) -> p kv g d", g=4, kv=2)

        def perm_in(stg, base):
            return stg[:, base:base + 512].rearrange("p (kv g d) -> p kv g d", kv=2, g=4)

        HC = D_IN // 2
        S.op("dve", lambda e: e.tensor_scalar(out=GPh[:], in0=GP[:], scalar1=0.5, scalar2=None, op0=ALU.mult), reads=["GP"], writes=["GP"])
        stgbufs = [
            (H2[0][:].rearrange("p a b -> p (a b)"), ["H0_%d" % j for j in range(4)]),
            (H2[1][:].rearrange("p a b -> p (a b)"), ["H1_%d" % j for j in range(4)]),
            (XR[:].rearrange("p a b -> p (a b)"), ["XR0", "XR1"]),
        ]
        rot = {"n": 0}

        def next_stage():
            b = stgbufs[rot["n"] % len(stgbufs)]
            rot["n"] += 1
            return b

        segs = [(OQ, 512, "perm"), (OK_, 256, "plain"), (OGA, 512, "perm"), (OGB, 512, "plain"), (OB, 512, "plain"), (OA, 512, "half")]

        def thread_W():
            cvt = {"n": 0}
            for (c0, wd, mode) in segs:
                for kh in range(2):
                    flat, names = next_stage()
                    stg = flat[:, 0:4 * wd].rearrange("p (kk c) -> p kk c", kk=4)
                    dma(stg, w_in[kh * 512:(kh + 1) * 512, c0:c0 + wd].rearrange("(kk p) c -> p kk c", p=128), [], names)
                    for kk in range(4):
                        k = kh * 4 + kk
                        sc = GPh[:, k:k + 1] if mode == "half" else GP[:, k:k + 1]
                        if mode == "perm":
                            o_ap = Win[:, k, c0:c0 + 512].rearrange("p (g kv d) -> p kv g d", g=4, kv=2)
                            i_ap = stg[:, kk, :].rearrange("p (kv g d) -> p kv g d", kv=2, g=4)
                        else:
                            o_ap = Win[:, k, c0:c0 + wd]
                            i_ap = stg[:, kk, :]
                        cvt["n"] += 1
                        if cvt["n"] % 2 == 0:
                            S.op("act", lambda e, o_ap=o_ap, i_ap=i_ap, sc=sc: e.activation(out=o_ap, in_=i_ap, func=AF.Copy, scale=sc),
                                 reads=names + ["GP"], writes=["W%d_%d" % (c0, k)])
                        else:
                            S.op("dve", lambda e, o_ap=o_ap, i_ap=i_ap, sc=sc: e.tensor_scalar(out=o_ap, in0=i_ap, scalar1=sc, scalar2=None, op0=ALU.mult),
                                 reads=names + ["GP"], writes=["W%d_%d" % (c0, k)])
                    yield None
                yield ("flag", "W%d" % c0)
            for gp in range(2):
                flat, names = next_stage()
                stgA = flat[:, 0:2 * D].rearrange("p (a b) -> p a b", a=2)
                for kv in range(2):
                    dma(stgA[kv * 64:(kv + 1) * 64, :, :],
                        w_out[kv * 256 + gp * 128:kv * 256 + (gp + 1) * 128, :].rearrange("(g d) n -> d g n", d=64),
                        [], names if kv == 0 else ["XSw"])
                for gl in range(2):
                    g = gp * 2 + gl
                    S.op("act", lambda e, g=g, gl=gl, stgA=stgA: e.activation(out=WoA[:, g, :], in_=stgA[:, gl, :], func=AF.Copy, scale=GA[:, g:g + 1]),
                         reads=names + ["XSw", "GA"], writes=["WoA", "XSw"])
                yield None
            for hh in range(2):
                flat, names = next_stage()
                stgC = flat[:, 0:2 * D].rearrange("p (a b) -> p a b", a=2)
                dma(stgC, w_out[512 + hh * 256:512 + (hh + 1) * 256, :].rearrange("(j p) n -> p j n", p=128), [], names)
                for jl in range(2):
                    j = hh * 2 + jl
                    S.op("dve", lambda e, j=j, jl=jl, stgC=stgC: e.tensor_scalar(out=WoC[:, j, :], in0=stgC[:, jl, :], scalar1=PV[:, j, 33:34], scalar2=None, op0=ALU.mult),
                         reads=names + ["PV"], writes=["WoC"])
                yield None
            flat, names = next_stage()
            stgP = flat[:, 0:4 * 512].rearrange("p (j n) -> p j n", n=512)
            dma(stgP, w_pw.rearrange("(j p) n -> p j n", p=128), [], names)
            S.op("dve", lambda e: e.tensor_copy(out=Wpw[:], in_=stgP), reads=names, writes=["Wpw"])
            yield None
            for j in range(4):
                for tau in range(NPE):
                    eng = "act" if (j * NPE + tau) % 2 == 0 else "dve"
                    if eng == "act":
                        S.op("act", lambda e, j=j, tau=tau: e.activation(out=Dg[:, j * NPE + tau, :], in_=identf[:, :], func=AF.Copy, scale=PV[:, j, tau:tau + 1]),
                             reads=["identf", "PV"], writes=["Dg"])
                    else:
                        S.op("dve", lambda e, j=j, tau=tau: e.tensor_scalar(out=Dg[:, j * NPE + tau, :], in0=identf[:, :], scalar1=PV[:, j, tau:tau + 1], scalar2=None, op0=ALU.mult),
                             reads=["identf", "PV"], writes=["Dg"])
                yield None
            yield ("flag", "W_done")

        def rms_and_transpose(xsrc, xres, T, xt_col0, dst=None, dstname="XT"):
            S.op("act", lambda e: e.activation(out=TBa[0:T, :].bitcast(BF16)[:, 0:D], in_=xsrc, func=AF.Square, accum_out=sst[0:T, 0:1]),
                 reads=xres, writes=["TBa", "sst0"])
            rsqrt_act(sst[0:T, 1:2], sst[0:T, 0:1], float(D), RMS_EPS, ["sst0"], ["sst1"], sst[0:T, 2:3], "sst2")
            S.op("act", lambda e: e.activation(out=Xs[0:T, :], in_=xsrc, func=AF.Copy, scale=sst[0:T, 1:2]),
                 reads=xres + ["sst1"], writes=["Xs"])

            def tr(e):
                for k in range(8):
                    r = e.transpose(ptr[:, k, 0:T], Xs[0:T, k * 128:(k + 1) * 128], identb[0:T, 0:T])
                return r
            S.op("pe", tr, reads=["Xs", "identb"], writes=["ptr"])
            d_ap = XT[:, :, xt_col0:xt_col0 + T] if dst is None else dst
            S.op("dve", lambda e: e.tensor_copy(out=d_ap, in_=ptr[:, :, 0:T]), reads=["ptr"], writes=[dstname])

        pa = [pS[0], pS[1], pS[2], pO]
        pan = ["pS0", "pS1", "pS2", "pO"]

        def wnames(col0):
            for (c0, wd, _m) in segs:
                if c0 <= col0 < c0 + wd:
                    return ["W%d_%d" % (c0, k) for k in range(8)]
            raise AssertionError(col0)

        def fm_chunk(col0, T, bank, xt=None, xtname="XT"):
            def mm(e):
                for k in range(8):
                    r = e.matmul(pa[bank][:, 0:T], lhsT=Win[:, k, col0:col0 + 128], rhs=(XT if xt is None else xt)[:, k, 0:T], start=(k == 0), stop=(k == 7))
                return r
            S.op("pe", mm, reads=wnames(col0) + [xtname], writes=[pan[bank]])

        def tm_matmul(col0, tok0, ntok, xt=None, xtname="XT"):
            def mm(e):
                for k in range(8):
                    r = e.matmul(pst[0:ntok, 0:128], lhsT=(XT if xt is None else xt)[:, k, tok0:tok0 + ntok], rhs=Win[:, k, col0:col0 + 128],
                                 start=(k == 0), stop=(k == 7))
                return r
            S.op("pe", mm, reads=wnames(col0) + [xtname], writes=["pst"])

        def v_aug_copy(e, tile_ap, src_ap):
            e.copy(out=tile_ap[:, 0, 0:64], in_=src_ap[:, 0:64])
            return e.copy(out=tile_ap[:, 1, 64:128], in_=src_ap[:, 64:128])

        def attention_unit(nq, qcol0, kprev, vprev, kown, vown, nown, pbuf, full):
            N = 4 * nq
            Pp, Po = P01[pbuf]
            Pm = P2[pbuf]
            for kv in range(2):
                rows = slice(kv * 64, (kv + 1) * 64)
                mrows = slice(kv * 32, kv * 32 + NMETA)
                mrows1 = slice(kv * 32, kv * 32 + NMETA + 1)
                qrhs = QT[:, :, qcol0:qcol0 + nq] if full else QT[rows, :, qcol0:qcol0 + nq]
                mk = MK2[kv][:, :] if full else MK2[kv][rows, :]
                bp = Bprev[:, kv * 512:(kv + 1) * 512].rearrange("p (g q) -> p g q", g=4)[:, :, 0:nq]
                bo = Bown[:, kv * 512:(kv + 1) * 512].rearrange("p (g q) -> p g q", g=4)[:, :, 0:nq]

                def qk(e, kv=kv, qrhs=qrhs, bp=bp, bo=bo, mrows=mrows, mk=mk):
                    if kprev is not None:
                        e.matmul(pS[0][:, 0:N], lhsT=kprev(kv), rhs=qrhs, start=True, stop=False)
                        e.matmul(pS[0][:, 0:N], lhsT=identb[:, :], rhs=bp, start=False, stop=True)
                    e.matmul(pS[1][0:nown, 0:N], lhsT=kown(kv), rhs=qrhs, start=True, stop=False)
                    e.matmul(pS[1][0:nown, 0:N], lhsT=identb[:, 0:nown], rhs=bo, start=False, stop=True)
                    return e.matmul(pS[2][mrows, 0:N], lhsT=mk, rhs=qrhs, start=True, stop=True)
                S.op("pe", qk, reads=["QT", "KT", "KC", "MK", "Bprev", "Bown", "identb"], writes=["pS0", "pS1", "pS2"])
                if kprev is not None:
                    S.op("act", lambda e: e.activation(out=Pp[:, 0:N], in_=pS[0][:, 0:N], func=AF.Exp, scale=SCALE),
                         reads=["pS0"], writes=["P%d_0" % pbuf])
                S.op("act", lambda e: e.activation(out=Po[0:nown, 0:N], in_=pS[1][0:nown, 0:N], func=AF.Exp, scale=SCALE),
                     reads=["pS1"], writes=["P%d_1" % pbuf])
                S.op("act", lambda e, mrows=mrows: e.activation(out=Pm[mrows, 0:N], in_=pS[2][mrows, 0:N], func=AF.Exp, scale=SCALE),
                     reads=["pS2"], writes=["P%d_2" % pbuf])

                def pv(e, kv=kv, mrows1=mrows1):
                    first = True
                    if kprev is not None:
                        e.matmul(pO[:, 0:N], lhsT=vprev(kv), rhs=Pp[:, 0:N], start=True, stop=False)
                        first = False
                    e.matmul(pO[:, 0:N], lhsT=vown(kv), rhs=Po[0:nown, 0:N], start=first, stop=False)
                    return e.matmul(pO[:, 0:N], lhsT=VM[mrows1, :], rhs=Pm[mrows1, 0:N], start=False, stop=True)
                S.op("pe", pv, reads=["VA", "VC", "VM", "P%d_0" % pbuf, "P%d_1" % pbuf, "P%d_2" % pbuf], writes=["pO"])
                num = slice(0, 64) if kv == 0 else slice(64, 128)
                den = slice(64, 128) if kv == 0 else slice(0, 64)
                S.op("act", lambda e, num=num, den=den: e.activation(out=RD[num, 0:N], in_=pO[den, 0:N], func=AF.Ln), reads=["pO"], writes=["RD"])
                S.op("act", lambda e, num=num: e.activation(out=RD[num, 0:N], in_=RD[num, 0:N], func=AF.Exp, scale=-1.0), reads=["RD"], writes=["RD"])
                yield None
                S.op("dve", lambda e, num=num: e.tensor_tensor(
                    out=ATT[num, :, qcol0:qcol0 + nq], in0=pO[num, 0:N].rearrange("p (g q) -> p g q", g=4),
                    in1=RD[num, 0:N].rearrange("p (g q) -> p g q", g=4), op=ALU.mult),
                    reads=["pO", "RD"], writes=["ATT"])
                yield None

        def att_finish(T, par):
            A = AM[par]
            an = "AM%d" % par
            S.op("act", lambda e: e.activation(out=A[:, :, 0:T], in_=ATT[:, :, 0:T], func=AF.Square), reads=["ATT"], writes=[an])

            def stats(e):
                for j in range(4):
                    r = e.matmul(pS[0][:, 0:T], lhsT=onesm[:, :], rhs=A[:, j, 0:T], start=(j == 0), stop=(j == 3))
                return r
            S.op("pe", stats, reads=[an, "onesm"], writes=["pS0"])
            S.op("act", lambda e: e.activation(out=TBa[:, 0:T], in_=pS[0][:, 0:T], func=AF.Ln, bias=RMS_EPS, scale=1.0), reads=["pS0"], writes=["TBa"])
            S.op("act", lambda e: e.activation(out=RD[:, 0:T], in_=TBa[:, 0:T], func=AF.Exp, scale=-0.5), reads=["TBa"], writes=["RD"])
            for g in range(4):
                S.op("dve", lambda e, g=g: e.tensor_tensor(out=ATT[:, g, 0:T], in0=ATT[:, g, 0:T], in1=SGA[:, g, 0:T], op=ALU.mult),
                     reads=["ATT", "SGA"], writes=["ATT"])
                S.op("pool", lambda e, g=g: e.tensor_tensor(out=A[:, g, 0:T], in0=ATT[:, g, 0:T], in1=RD[:, 0:T], op=ALU.mult),
                     reads=["ATT", "RD"], writes=[an])

        xr_state = {"n": 0}

        def out_tile(tok0, ntok, par, xsrc_dram, ydst):
            A = AM[par]
            h = xr_state["n"]
            xr_state["n"] += 1
            xs = h % 2
            xn_ = "XR%d" % xs
            dma(XR[0:ntok, xs, :], xsrc_dram, [], [xn_])

            def mm(e):
                for half in range(2):
                    for g in range(4):
                        e.matmul(pz[half][0:ntok, :], lhsT=A[:, g, tok0:tok0 + ntok], rhs=WoA[:, g, half * 512:(half + 1) * 512],
                                 start=(g == 0), stop=False)
                    for j in range(4):
                        r = e.matmul(pz[half][0:ntok, :], lhsT=CM[:, j, tok0:tok0 + ntok], rhs=WoC[:, j, half * 512:(half + 1) * 512],
                                     start=False, stop=(j == 3))
                return r
            S.op("pe", mm, reads=["AM%d" % par, "SQ0", "SQ1", "SQ2", "SQ3", "WoA", "WoC"], writes=["pz0", "pz1"])
            for half in range(2):
                S.op("act", lambda e, half=half: e.activation(out=TBb[0:ntok, :].bitcast(BF16)[:, 0:512], in_=pz[half][0:ntok, :], func=AF.Square,
                                                              accum_out=sst[0:ntok, 4 + half:5 + half]),
                     reads=["pz%d" % half], writes=["TBb", "sst%d" % (4 + half)])
            for half in range(2):
                S.op("dve", lambda e, half=half: e.tensor_tensor(out=YS[0:ntok, half * 512:(half + 1) * 512], in0=pz[half][0:ntok, :],
                                                                 in1=Gpost[0:ntok, half * 512:(half + 1) * 512], op=ALU.mult),
                     reads=["pz%d" % half, "Gpost"], writes=["YS0"])
            S.op("pool", lambda e: e.tensor_tensor(out=sst[0:ntok, 6:7], in0=sst[0:ntok, 4:5], in1=sst[0:ntok, 5:6], op=ALU.add),
                 reads=["sst4", "sst5"], writes=["sst6"])
            rsqrt_act(sst[0:ntok, 7:8], sst[0:ntok, 6:7], float(D), RMS_EPS, ["sst6"], ["sst7"], sst[0:ntok, 8:9], "sst8")
            yield None
            S.op("dve", lambda e: e.scalar_tensor_tensor(out=XR[0:ntok, xs, :], in0=YS[0:ntok, :], scalar=sst[0:ntok, 7:8], in1=XR[0:ntok, xs, :],
                                                         op0=ALU.mult, op1=ALU.add),
                 reads=["YS0", "sst7", xn_], writes=[xn_])
            dma(ydst, XR[0:ntok, xs, :], [xn_], [uniq("yout")])
            yield None

        XTm = KT2[0][:, 0:128].rearrange("p (k t) -> p k t", k=8)
        dma(XS[0:NMETA, 0, :], meta[:, :], [], ["XS0"])
        rms_and_transpose(XS[0:NMETA, 0, :], ["XS0"], NMETA, 0, dst=XTm[:, :, 0:NMETA], dstname="KT")
        def meta_kv():
            fm_chunk(OK_, NMETA, 0, xt=XTm, xtname="KT")

            def mk_copy(e):
                e.tensor_copy(out=MK2[0][0:64, :], in_=pa[0][0:64, 0:NMETA])
                return e.tensor_copy(out=MK2[1][64:128, :], in_=pa[0][64:128, 0:NMETA])
            S.op("dve", mk_copy, reads=[pan[0]], writes=["MK"])
            tm_matmul(OV, 0, NMETA, xt=XTm, xtname="KT")

            def vm_copy(e):
                e.copy(out=VM[0:NMETA, 0:64], in_=pst[0:NMETA, 0:64])
                return e.copy(out=VM[32:32 + NMETA, 64:128], in_=pst[0:NMETA, 64:128])
            S.op("act", vm_copy, reads=["pst"], writes=["VM"])
            dma(VM[NMETA:NMETA + 1, :], vmrow[0:1, 0, :], ["vmrow", "VM"], ["VM"])
            dma(VM[32 + NMETA:32 + NMETA + 1, :], vmrow[0:1, 1, :], ["vmrow", "VM"], ["VM"])


        def meta_glu():
            for j in range(4):
                fm_chunk(OB + j * 128, NMETA, 0, xt=XTm, xtname="KT")
                fm_chunk(OA + j * 128, NMETA, 1, xt=XTm, xtname="KT")
                S.op("act", lambda e: e.activation(out=TBa[:, 0:NMETA], in_=pa[0][:, 0:NMETA], func=AF.Tanh, scale=0.5), reads=[pan[0]], writes=["TBa"])
                S.op("dve", lambda e, j=j: e.scalar_tensor_tensor(out=UM[:, j, HALO - NMETA:HALO], in0=TBa[:, 0:NMETA], scalar=1.0, in1=pa[1][:, 0:NMETA],
                                                              op0=ALU.add, op1=ALU.mult),
                     reads=["TBa", pan[1]], writes=["UM"])

        for i in range(NSB):
            dma(CST[:], cache_k[i, :, :], [], ["CST"])
            S.op("dve", lambda e: e.tensor_copy(out=CSB[:], in_=CST[:]), reads=["CST"], writes=["Xs"])
            S.op("pe", lambda e: e.transpose(ptr[:, 0, :], CSB[:, :], identb[:, :]), reads=["Xs", "identb"], writes=["ptr"])
            S.op("act", lambda e, i=i: e.copy(out=KC[:, i, :], in_=ptr[:, 0, :]), reads=["ptr"], writes=["KC"])
            dma(CST[:], cache_v[i, :, :], [], ["CST"])
            S.op("act", lambda e, i=i: v_aug_copy(e, VC[:, i, :, :], CST[:, :]), reads=["CST"], writes=["VC"])
            dma(nk_s[i, 0:64, :], cache_k[i, 64:128, :], [], [uniq("o")])
            dma(nv_s[i, 0:64, :], cache_v[i, 64:128, :], [], [uniq("o")])

        nst = SEQ // ST
        units = []
        for b in range(NPB):
            for s_ in range(nst):
                units.append(dict(kind="p", b=b, s=s_, first=(s_ == 0), last=(s_ == nst - 1), T=ST, ntiles=4))
        units.append(dict(kind="s", T=NSB * DEC, ntiles=2, first=True, last=True))
        for i, u in enumerate(units):
            u["par"] = i % 2
            u["i"] = i

        tiles = []
        for u in units:
            for t in range(u["ntiles"]):
                if u["kind"] == "p":
                    tiles.append(x_p[u["b"], u["s"] * ST + t * 128:u["s"] * ST + (t + 1) * 128, :])
                else:
                    tiles.append(x_s[t * 128:(t + 1) * 128, :])
        ring = {"issued": 0}

        def ensure_loaded(gidx, ahead=2):
            while ring["issued"] < min(len(tiles), gidx + 1 + ahead):
                g = ring["issued"]
                sl = g % NXS
                dma(XS[:, sl, :], tiles[g], [], ["XS%d" % sl] + (["XS1a", "XS1b"] if sl == 1 else []))
                ring["issued"] += 1

        gt = {"n": 0}

        def thread_A(u):
            T = u["T"]
            par = u["par"]
            i = u["i"]
            p = (u["kind"] == "p")
            if p and not u["first"]:
                for kv in range(2):
                    S.op("pool", lambda e, kv=kv: e.tensor_copy(out=KT2[kv][:, 0:128], in_=KT2[kv][:, ST:ST + 128]), reads=["KT"], writes=["KT"])
                S.op("pool", lambda e: e.tensor_copy(out=VA[:, 0, :, :], in_=VA[:, 4, :, :]), reads=["VA"], writes=["VA"])
            for t in range(u["ntiles"]):
                g = gt["n"]
                gt["n"] += 1
                ensure_loaded(g, ahead=1)
                rms_and_transpose(XS[:, g % NXS, :], ["XS%d" % (g % NXS)], 128, t * 128)
                yield None
            if i == 0:
                yield ("need", "W%d" % OQ)
            for g in range(4):
                fm_chunk(OQ + g * 128, T, g)
                S.op("act", lambda e, g=g: e.copy(out=QT[:, g, 0:T], in_=pa[g][:, 0:T]), reads=[pan[g]], writes=["QT"])
                yield None
            if i == 0:
                yield ("need", "W%d" % OK_)
                meta_kv()
            fm_chunk(OK_, T, 0)
            def k_copy(e):
                e.copy(out=KT2[0][0:64, 128:128 + T], in_=pa[0][0:64, 0:T])
                return e.copy(out=KT2[1][64:128, 128:128 + T], in_=pa[0][64:128, 0:T])
            S.op("act", k_copy, reads=[pan[0]], writes=["KT"])
            yield None
            if p:
                for t in range(4):
                    tm_matmul(OV, t * 128, 128)
                    S.op("act", lambda e, t=t: v_aug_copy(e, VA[:, 1 + t, :, :], pst[:, 0:128]), reads=["pst"], writes=["VA"])
                    if u["last"] and t == 3:
                        S.op("act", lambda e: e.copy(out=CST[:, :], in_=pst[:, 0:128]), reads=["pst"], writes=["CST"])
                        dma(nv_p[u["b"], :, :], CST[:, :], ["CST"], [uniq("o")])
                        tm_matmul(OK_, t * 128, 128)
                        S.op("act", lambda e: e.copy(out=CST[:, :], in_=pst[:, 0:128]), reads=["pst"], writes=["CST"])
                        dma(nk_p[u["b"], :, :], CST[:, :], ["CST"], [uniq("o")])
                    yield None
            else:
                for q in range(NSB):
                    tm_matmul(OV, q * DEC, DEC)
                    S.op("act", lambda e, q=q: v_aug_copy(e, VS[:, q, :, :], pst[0:DEC, 0:128]), reads=["pst"], writes=["VA"])
                    S.op("act", lambda e: e.copy(out=CST[0:DEC, :], in_=pst[0:DEC, 0:128]), reads=["pst"], writes=["CST"])
                    dma(nv_s[q, 64:128, :], CST[0:DEC, :], ["CST"], [uniq("o")])
                    tm_matmul(OK_, q * DEC, DEC)
                    S.op("act", lambda e: e.copy(out=CST[0:DEC, :], in_=pst[0:DEC, 0:128]), reads=["pst"], writes=["CST"])
                    dma(nk_s[q, 64:128, :], CST[0:DEC, :], ["CST"], [uniq("o")])
                    yield None
            if i == 0:
                yield ("need", "W%d" % OGA)
            for g in range(4):
                fm_chunk(OGA + g * 128, T, g)
                S.op("act", lambda e, g=g: e.activation(out=SGA[:, g, 0:T], in_=pa[g][:, 0:T], func=AF.Silu), reads=[pan[g]], writes=["SGA"])
                yield None
            if i >= 2:
                yield ("need", "gate_done%d" % (i - 2))
            if i == 0:
                yield ("need", "W%d" % OGB)
            for j in range(4):
                fm_chunk(OGB + j * 128, T, j)
                S.op("act", lambda e, j=j: e.activation(out=SGB[par][:, j, 0:T], in_=pa[j][:, 0:T], func=AF.Silu),
                     reads=[pan[j]], writes=["SGB%d" % par])
                yield None
            if p:
                for t in range(4):
                    has_prev = not (u["first"] and t == 0)
                    yield from attention_unit(
                        128, t * 128,
                        (lambda kv, t=t: KT2[kv][:, t * 128:(t + 1) * 128]) if has_prev else None,
                        (lambda kv, t=t: VA[:, t, kv, :]),
                        (lambda kv, t=t: KT2[kv][:, 128 + t * 128:128 + (t + 1) * 128]),
                        (lambda kv, t=t: VA[:, 1 + t, kv, :]),
                        128, t % 2, True)
            else:
                write_sink_rows(DEC)
                for q in range(NSB):
                    yield from attention_unit(
                        DEC, q * DEC,
                        (lambda kv, q=q: KC[kv * 64:(kv + 1) * 64, q, :]),
                        (lambda kv, q=q: VC[:, q, kv, :]),
                        (lambda kv, q=q: KT2[kv][kv * 64:(kv + 1) * 64, 128 + q * DEC:128 + (q + 1) * DEC]),
                        (lambda kv, q=q: VS[:, q, kv, :]),
                        DEC, q % 2, False)
            if i >= 3:
                yield ("need", "out_done%d" % (i - 3))
            att_finish(T, i % 3)
            yield None
            if i >= 1:
                yield ("need", "conv_done%d" % (i - 1))
            if i == 0:
                yield ("need", "W%d" % OB)
                yield ("need", "W%d" % OA)
                meta_glu()
            if p:
                src = UM if u["first"] else UH
                S.op("pool", lambda e: e.tensor_copy(out=U[:, :, 0:HALO], in_=src[:, :, :]), reads=["UM", "UH", "U"], writes=["U"])
            else:
                for q in range(NSB):
                    dma(OST[0:HALO, :], state_conv[q, :, :], [], ["RD"])

                    def trs(e):
                        for j in range(4):
                            r = e.transpose(pst[:, j * 32:j * 32 + HALO], OST[0:HALO, j * 128:(j + 1) * 128], identf[0:HALO, 0:HALO])
                        return r
                    S.op("pe", trs, reads=["RD", "identf"], writes=["pst"])
                    S.op("act", lambda e, q=q: e.copy(out=US[:, :, q, 0:HALO], in_=pst[:, 0:128].rearrange("p (j t) -> p j t", t=32)[:, :, 0:HALO]),
                         reads=["pst", "U"], writes=["U"])
                yield None
            for j in range(4):
                bb = 2 * (j % 2)
                fm_chunk(OB + j * 128, T, bb)
                fm_chunk(OA + j * 128, T, bb + 1)
                S.op("act", lambda e, bb=bb: e.activation(out=TBa[:, 0:T], in_=pa[bb][:, 0:T], func=AF.Tanh, scale=0.5), reads=[pan[bb]], writes=["TBa"])
                if p:
                    S.op("dve", lambda e, j=j, bb=bb: e.scalar_tensor_tensor(out=U[:, j, HALO:HALO + ST], in0=TBa[:, :], scalar=1.0, in1=pa[bb + 1][:, :],
                                                                  op0=ALU.add, op1=ALU.mult),
                         reads=["TBa", pan[bb + 1], "U"], writes=["U"])
                    if u["last"]:
                        S.op("dve", lambda e, j=j, bb=bb: e.scalar_tensor_tensor(out=UL[:, j, 0:HALO], in0=TBa[:, ST - HALO:ST], scalar=1.0,
                                                                      in1=pa[bb + 1][:, ST - HALO:ST], op0=ALU.add, op1=ALU.mult),
                             reads=["TBa", pan[bb + 1]], writes=["UL"])
                else:
                    S.op("dve", lambda e, j=j, bb=bb: e.scalar_tensor_tensor(
                        out=US[:, j, :, HALO:HALO + DEC], in0=TBa[:, 0:T].rearrange("p (q t) -> p q t", q=NSB), scalar=1.0,
                        in1=pa[bb + 1][:, 0:T].rearrange("p (q t) -> p q t", q=NSB), op0=ALU.add, op1=ALU.mult),
                        reads=["TBa", pan[bb + 1], "U"], writes=["U"])
                    S.op("dve", lambda e, j=j, bb=bb: e.scalar_tensor_tensor(
                        out=UL[:, j, :].rearrange("p (q t) -> p q t", q=NSB),
                        in0=TBa[:, 0:T].rearrange("p (q t) -> p q t", q=NSB)[:, :, DEC - HALO:DEC], scalar=1.0,
                        in1=pa[bb + 1][:, 0:T].rearrange("p (q t) -> p q t", q=NSB)[:, :, DEC - HALO:DEC], op0=ALU.add, op1=ALU.mult),
                        reads=["TBa", pan[bb + 1]], writes=["UL"])
                yield None
            if p and not u["last"]:
                S.op("pool", lambda e: e.tensor_copy(out=UH[:, :, :], in_=U[:, :, ST:ST + HALO]), reads=["U"], writes=["UH"])
            if p and u["last"]:
                def tru(e):
                    for j in range(4):
                        r = e.transpose(pst[0:HALO, j * 128:(j + 1) * 128], UL[:, j, 0:HALO], identf[:, :])
                    return r
                S.op("pe", tru, reads=["UL", "identf"], writes=["pst"])
                S.op("act", lambda e: e.copy(out=OST[0:HALO, :], in_=pst[0:HALO, :]), reads=["pst"], writes=["RD"])
                dma(nc_p[u["b"], :, :], OST[0:HALO, :], ["RD"], [uniq("o")])
            if not p:
                for q in range(NSB):
                    def tru(e, q=q):
                        for j in range(4):
                            r = e.transpose(pst[0:HALO, j * 128:(j + 1) * 128], UL[:, j, q * HALO:(q + 1) * HALO], identf[:, :])
                        return r
                    S.op("pe", tru, reads=["UL", "identf"], writes=["pst"])
                    S.op("act", lambda e: e.copy(out=OST[0:HALO, :], in_=pst[0:HALO, :]), reads=["pst"], writes=["RD"])
                    dma(nc_s[q, :, :], OST[0:HALO, :], ["RD"], [uniq("o")])
            yield None

        def conv_gen(u):
            T = u["T"]
            p = (u["kind"] == "p")
            H = H2[u["par"]]
            hn = lambda j: "H%d_%d" % (u["par"], j)
            if p:
                u_of = lambda j, tau: U[:, j, tau:tau + ST]
                h_of = lambda buf, j: buf[:, j, 0:ST]
                c_of = lambda j: pz[j % 2][:, 0:ST]
            else:
                u_of = lambda j, tau: US[:, j, :, tau:tau + DEC]
                h_of = lambda buf, j: buf[:, j, 0:T].rearrange("p (q t) -> p q t", q=NSB)
                c_of = lambda j: pz[j % 2][:, 0:T].rearrange("p (q t) -> p q t", q=NSB)
            for j in range(4):
                def cmm(e, j=j):
                    for tau in range(NPE):
                        r = e.matmul(pz[j % 2][:, 0:T], lhsT=Dg[:, j * NPE + tau, :], rhs=u_of(j, tau), start=(tau == 0), stop=(tau == NPE - 1))
                    return r
                S.op("pe", cmm, reads=["U", "Dg"], writes=["pz%d" % (j % 2)])
                S.op("dve", lambda e, j=j: e.scalar_tensor_tensor(
                    out=h_of(H, j), in0=u_of(j, NPE), scalar=PV[:, j, NPE:NPE + 1], in1=c_of(j), op0=ALU.mult, op1=ALU.add),
                    reads=["U", "PV", "pz%d" % (j % 2)], writes=[hn(j)])
                for tau in range(NPE + 1, CW):
                    S.op("dve", lambda e, j=j, tau=tau: e.scalar_tensor_tensor(
                        out=h_of(H, j), in0=u_of(j, tau), scalar=PV[:, j, tau:tau + 1], in1=h_of(H, j), op0=ALU.mult, op1=ALU.add),
                        reads=["U", "PV", hn(j)], writes=[hn(j)])
                    if tau % 3 == 0:
                        yield None
                yield None
            yield ("flag", "conv_done%d" % u["i"])

        def back_gen(u):
            T = u["T"]
            par = u["par"]
            i = u["i"]
            hb = ["HB%d" % j for j in range(4)]
            sq = ["SQ%d" % j for j in range(4)]
            H = H2[par]
            hn = lambda j: "H%d_%d" % (par, j)
            for j in range(4):
                S.op("act", lambda e, j=j: e.copy(out=HB[:, j, 0:T], in_=H[:, j, 0:T]), reads=[hn(j)], writes=["HB%d" % j])
                S.op("act", lambda e, j=j: e.activation(out=SQ[:, j, 0:T], in_=H[:, j, 0:T], func=AF.Square), reads=[hn(j)], writes=["SQ%d" % j])
            yield None

            def stats(e):
                for j in range(4):
                    e.matmul(pz[0][:, 0:T], lhsT=onesm[:, :], rhs=HB[:, j, 0:T], start=(j == 0), stop=(j == 3))
                for j in range(4):
                    r = e.matmul(pz[1][:, 0:T], lhsT=onesm[:, :], rhs=SQ[:, j, 0:T], start=(j == 0), stop=(j == 3))
                return r
            S.op("pe", stats, reads=hb + sq + ["onesm"], writes=["pz0", "pz1"])
            S.op("act", lambda e: e.copy(out=R0[:, 0:T], in_=pz[0][:, 0:T]), reads=["pz0"], writes=["R0"])
            S.op("pool", lambda e: e.tensor_tensor(out=TBb[:, 0:T], in0=R0[:, 0:T], in1=R0[:, 0:T], op=ALU.mult), reads=["R0"], writes=["TBb"])
            S.op("act", lambda e: e.copy(out=R1[:, 0:T], in_=pz[1][:, 0:T]), reads=["pz1"], writes=["R1"])
            S.op("pool", lambda e: e.tensor_tensor(out=TBb[:, 0:T], in0=R1[:, 0:T], in1=TBb[:, 0:T], op=ALU.subtract), reads=["R1", "TBb"], writes=["TBb"])
            S.op("act", lambda e: e.activation(out=TBb[:, 0:T], in_=TBb[:, 0:T], func=AF.Ln, bias=LN_EPS, scale=1.0), reads=["TBb"], writes=["TBb"])
            S.op("act", lambda e: e.activation(out=R1[:, 0:T], in_=TBb[:, 0:T], func=AF.Exp, scale=-0.5), reads=["TBb"], writes=["R1"])
            yield None
            for j in range(4):
                S.op("dve", lambda e, j=j: e.tensor_tensor(out=H[:, j, 0:T], in0=H[:, j, 0:T], in1=R0[:, 0:T], op=ALU.subtract),
                     reads=[hn(j), "R0"], writes=[hn(j)])
                S.op("pool", lambda e, j=j: e.tensor_tensor(out=H[:, j, 0:T], in0=H[:, j, 0:T], in1=R1[:, 0:T], op=ALU.mult),
                     reads=[hn(j), "R1"], writes=[hn(j)])
                S.op("act", lambda e, j=j: e.activation(out=HB[:, j, 0:T], in_=H[:, j, 0:T], func=AF.Silu, bias=PV[:, j, 32:33], scale=PV[:, j, 31:32]),
                     reads=[hn(j), "PV"], writes=["HB%d" % j])
                yield None
            for jj in range(4):
                bank = jj % 2

                def mm(e, jj=jj, bank=bank):
                    for j in range(4):
                        r = e.matmul(pz[bank][:, 0:T], lhsT=Wpw[:, j, jj * 128:(jj + 1) * 128], rhs=HB[:, j, 0:T], start=(j == 0), stop=(j == 3))
                    return r
                S.op("pe", mm, reads=hb + ["Wpw"], writes=["pz%d" % bank])
                S.op("act", lambda e, jj=jj, bank=bank: e.activation(out=SQ[:, jj, 0:T], in_=pz[bank][:, 0:T], func=AF.Square),
                     reads=["pz%d" % bank], writes=["SQ%d" % jj])
                S.op("act", lambda e, jj=jj, bank=bank: e.copy(out=H[:, jj, 0:T], in_=pz[bank][:, 0:T]), reads=["pz%d" % bank], writes=[hn(jj)])
                S.op("pool", lambda e, jj=jj: e.tensor_tensor(out=H[:, jj, 0:T], in0=H[:, jj, 0:T], in1=SGB[par][:, jj, 0:T], op=ALU.mult),
                     reads=[hn(jj), "SGB%d" % par], writes=[hn(jj)])
                yield None
            yield ("flag", "gate_done%d" % i)

            def stats2(e):
                for j in range(4):
                    r = e.matmul(pz[0][:, 0:T], lhsT=onesm[:, :], rhs=SQ[:, j, 0:T], start=(j == 0), stop=(j == 3))
                return r
            S.op("pe", stats2, reads=sq + ["onesm"], writes=["pz0"])
            S.op("act", lambda e: e.activation(out=TBb[:, 0:T], in_=pz[0][:, 0:T], func=AF.Ln, bias=RMS_EPS, scale=1.0), reads=["pz0"], writes=["TBb"])
            S.op("act", lambda e: e.activation(out=R1[:, 0:T], in_=TBb[:, 0:T], func=AF.Exp, scale=-0.5), reads=["TBb"], writes=["R1"])
            for jj in range(4):
                S.op("pool", lambda e, jj=jj: e.tensor_tensor(out=CM[:, jj, 0:T], in0=H[:, jj, 0:T], in1=R1[:, 0:T], op=ALU.mult),
                     reads=[hn(jj), "R1"], writes=["SQ%d" % jj])
            yield None

        def out_gen(u):
            par = u["i"] % 3
            p = (u["kind"] == "p")
            for t in range(u["ntiles"]):
                if p:
                    r0 = u["s"] * ST + t * 128
                    yield from out_tile(t * 128, 128, par, x_p[u["b"], r0:r0 + 128, :], y_p[u["b"], r0:r0 + 128, :])
                else:
                    yield from out_tile(t * 128, 128, par, x_s[t * 128:(t + 1) * 128, :], y_s[t * 128:(t + 1) * 128, :])

        def interleave(g1, g2, n1=2):
            d1 = g1 is None
            d2 = g2 is None
            while not (d1 and d2):
                for _ in range(n1):
                    if not d1:
                        try:
                            yield next(g1)
                        except StopIteration:
                            d1 = True
                if not d2:
                    try:
                        yield next(g2)
                    except StopIteration:
                        d2 = True

        def A_all():
            for u in units:
                yield from thread_A(u)
                yield ("flag", "A_done%d" % u["i"])

        def chain(*gs):
            for g in gs:
                yield from g

        def one(x):
            yield x

        def B_all():
            prev = None
            for u in units:
                i = u["i"]
                yield ("need", "A_done%d" % i)
                if i == 0:
                    yield ("need", "W_done")
                yield from interleave(conv_gen(u), prev, n1=2)
                prev = chain(back_gen(u), out_gen(u), one(("flag", "out_done%d" % i)))
            yield from prev

        def run_threads(gens, weights):
            flags = set()
            st = [dict(gen=g, wait=None, done=False) for g in gens]
            order = []
            for t, w in zip(st, weights):
                order += [t] * w
            while not all(t["done"] for t in st):
                progressed = False
                for t in order:
                    if t["done"]:
                        continue
                    if t["wait"] is not None:
                        if t["wait"] not in flags:
                            continue
                        t["wait"] = None
                    try:
                        r = next(t["gen"])
                    except StopIteration:
                        t["done"] = True
                        progressed = True
                        continue
                    progressed = True
                    if isinstance(r, tuple):
                        if r[0] == "flag":
                            flags.add(r[1])
                        elif r[0] == "need" and r[1] not in flags:
                            t["wait"] = r[1]
                assert progressed, "schedule deadlock: " + str([t["wait"] for t in st])

        run_threads([thread_W(), B_all(), A_all()], [1, 1, 1])

        S.wait_all("sp", list(S.dma_count.items()))
        S.emit()
    return nc


_CACHE = {}


def _consts():
    h = np.arange(1, 9, dtype=np.float64)
    slopes = (2.0 ** (-h)).reshape(2, 4)
    j = np.arange(128)[:, None]
    i = np.arange(128)[None, :]
    NEG = -30000.0
    bprev = np.zeros((128, 2, 4, 128), np.float32)
    bown = np.zeros((128, 2, 4, 128), np.float32)
    dprev = (i + 128 - j).astype(np.float64)
    mprev = (i >= 64) & (j < 64)
    down = np.abs(i - j).astype(np.float64)
    mown = (i < 64) & (j >= 64)
    for kv in range(2):
        for g in range(4):
            bp = -slopes[kv, g] * dprev / SCALE
            bo = -slopes[kv, g] * down / SCALE
            bprev[:, kv, g, :] = np.where(mprev, NEG, bp)
            bown[:, kv, g, :] = np.where(mown, NEG, bo)
    return (np.eye(128, dtype=np.float32), bprev.reshape(128, 1024), bown.reshape(128, 1024))


def kernel(x_prompt, x_sample, cache_k, cache_v, state_conv, meta_tokens, g_pre, w_in,
           sinks, g_att, conv_w, ln_g, ln_b, w_pw, g_conv, w_out, g_post):
    f = lambda a: np.ascontiguousarray(np.asarray(a, dtype=np.float32))
    x_prompt, x_sample, cache_k, cache_v, state_conv = map(f, (x_prompt, x_sample, cache_k, cache_v, state_conv))
    if "nc" not in _CACHE:
        _CACHE["nc"] = build_program()
    nc = _CACHE["nc"]
    ident, bprev, bown = _consts()
    vecs = np.concatenate([f(conv_w)[0], f(ln_g), f(ln_b), f(g_conv)], axis=0)
    shared = {
        "meta_tokens": f(meta_tokens), "g_pre": f(g_pre).reshape(8, 128), "w_in": f(w_in)[0],
        "sinks": f(sinks), "g_att": f(g_att), "vecs": np.ascontiguousarray(vecs),
        "w_pw": f(w_pw)[0], "w_out": f(w_out)[0], "g_post": f(g_post),
        "c_ident": ident, "c_bprev": bprev, "c_bown": bown,
    }
    in_maps = []
    for c in range(NCORES):
        m = dict(shared)
        m["x_prompt"] = x_prompt[NPB * c:NPB * (c + 1)]
        m["x_sample"] = x_sample[NSB * c:NSB * (c + 1)].reshape(NSB * DEC, D)
        m["cache_k"] = cache_k[0, NSB * c:NSB * (c + 1)].reshape(NSB, 128, 128)
        m["cache_v"] = cache_v[0, NSB * c:NSB * (c + 1)].reshape(NSB, 128, 128)
        m["state_conv"] = state_conv[0, NSB * c:NSB * (c + 1)]
        in_maps.append(m)
    res = run_bass_kernel_spmd(nc, in_maps, core_ids=list(range(NCORES)))
    R = res.results
    cat = lambda k: np.concatenate([np.asarray(r[k], dtype=np.float32) for r in R], axis=0)
    y_p = cat("y_prompt")
    y_s = cat("y_sample").reshape(32, DEC, D)
    nk_p = cat("nk_p").reshape(1, 16, 128, 2, 64)
    nv_p = cat("nv_p").reshape(1, 16, 128, 2, 64)
    nc_p = cat("nc_p").reshape(1, 16, HALO, 512)
    nk_s = cat("nk_s").reshape(1, 32, 128, 2, 64)
    nv_s = cat("nv_s").reshape(1, 32, 128, 2, 64)
    nc_s = cat("nc_s").reshape(1, 32, HALO, 512)
    return (y_p, y_s, nk_p, nv_p, nc_p, nk_s, nv_s, nc_s)
```
